# Optimizing a Trainium2 kernel written in Bass

```python
import math
import jax, jax.numpy as jnp
from jax import lax
import numpy as np

D_MODEL = 1024
BATCH = 16
SEQ = 2048
DEPTH = 4

PLE_DIM = 256
D_FF = 2816
RMS_EPS = 1e-6
N_BRANCH = 3
MIX_W = 512

HG_HEADS = 8
HG_DK = 64
HG_DV = 64
HG_CHUNK = 64
NSA_HEADS = 8
NSA_KV_HEADS = 2
NSA_HD = 64
CMP_LEN = 32
CMP_STRIDE = 16
CMP_HIDDEN = 128
SLC_BLOCK = 64
SLC_TOPN = 8
SLC_QBLOCK = 64
WIN = 256
WIN_QBLOCK = 128
RW_HEADS = 8
RW_HD = 64
RW_DECAY_LORA = 64
RW_A_LORA = 64
RW_GATE_LORA = 128
RW_GN_EPS = 64e-5
REL_BUCKETS = 32
REL_MAX_DIST = 128

HG_W = HG_HEADS * HG_DK
NSA_W = NSA_HEADS * NSA_HD
NSA_KV_W = NSA_KV_HEADS * NSA_HD
RW_W = RW_HEADS * RW_HD
RW_SIZES = (RW_W, RW_W, RW_W, RW_DECAY_LORA, RW_A_LORA, RW_GATE_LORA)
RW_COLS = 3 * RW_W + RW_DECAY_LORA + RW_A_LORA + RW_GATE_LORA
IN_SIZES = (HG_W, HG_W, HG_W, HG_W, NSA_W, NSA_KV_W, NSA_KV_W, NSA_KV_W, NSA_KV_W, NSA_KV_W, NSA_KV_W, 3 * NSA_HEADS, RW_COLS, N_BRANCH * D_MODEL)
IN_COLS = 4 * HG_W + NSA_W + 6 * NSA_KV_W + 3 * NSA_HEADS + RW_COLS + N_BRANCH * D_MODEL

kernel_name = "hybrid_hgrn2_nsa_rwkv7_macaron"


def _split(t, sizes):
    return jnp.split(t, np.cumsum(sizes)[:-1].tolist(), axis=-1)


def rmsnorm(x, g, eps=RMS_EPS):
    xf = x.astype(jnp.float32)
    y = xf * lax.rsqrt(jnp.mean(xf * xf, axis=-1, keepdims=True) + eps)
    return (y * g.astype(jnp.float32)).astype(x.dtype)


def swiglu(x, w_gu, w_d):
    gate, up = jnp.split(x @ w_gu, 2, axis=-1)
    return (jax.nn.silu(gate) * up) @ w_d


def t5_bucket(n):
    n = jnp.maximum(n, 0)
    max_exact = REL_BUCKETS // 2
    nf = jnp.maximum(n, 1).astype(jnp.float32)
    large = max_exact + (jnp.log(nf / max_exact) / math.log(REL_MAX_DIST / max_exact) * (REL_BUCKETS - max_exact)).astype(jnp.int32)
    large = jnp.minimum(large, REL_BUCKETS - 1)
    return jnp.where(n < max_exact, n, large)


def masked_softmax(logits, mask):
    logits = jnp.where(mask, logits.astype(jnp.float32), -jnp.inf)
    m = jnp.max(logits, axis=-1, keepdims=True)
    m = jnp.where(jnp.isfinite(m), m, 0.0)
    e = jnp.exp(logits - m)
    return e / jnp.maximum(jnp.sum(e, axis=-1, keepdims=True), 1e-30)


def hgrn2(q_raw, f_raw, i_raw, g_raw, lb, norm_g):
    B, S, _ = q_raw.shape
    H, C = HG_HEADS, HG_CHUNK
    N = S // C
    f32 = jnp.float32
    z = f_raw.astype(f32)
    lb = jnp.maximum(lb, 0.0)
    log_f = jnp.logaddexp(jax.nn.log_sigmoid(z), jnp.log(lb) + jax.nn.log_sigmoid(-z))
    k = (1.0 - lb) * jax.nn.sigmoid(-z)
    q = jax.nn.silu(q_raw.astype(f32))
    v = i_raw.astype(f32)

    def chunks(t, d):
        return t.reshape(B, N, C, H, d).transpose(1, 0, 3, 2, 4)

    causal = jnp.tril(jnp.ones((C, C), dtype=bool))[:, :, None]

    def step(state, inp):
        qc, kc, vc, lfc = inp
        b = jnp.cumsum(lfc, axis=2)
        decay = jnp.exp(jnp.where(causal, b[:, :, :, None, :] - b[:, :, None, :, :], -jnp.inf))
        scores = jnp.sum(qc[:, :, :, None, :] * kc[:, :, None, :, :] * decay, axis=-1)
        o = scores @ vc + jnp.einsum("bhtd,bhde->bhte", qc * jnp.exp(b), state)
        b_end = b[:, :, -1:, :]
        state = jnp.exp(b_end[:, :, 0, :, None]) * state + jnp.einsum("bhsd,bhse->bhde", kc * jnp.exp(b_end - b), vc)
        return state, o

    state0 = jnp.zeros((B, H, HG_DK, HG_DV), f32)
    _, o = lax.scan(step, state0, (chunks(q, HG_DK), chunks(k, HG_DK), chunks(v, HG_DV), chunks(log_f, HG_DK)))
    o = o.transpose(1, 0, 3, 2, 4).reshape(B, S, H, HG_DV)
    o = o * lax.rsqrt(jnp.mean(o * o, axis=-1, keepdims=True) + RMS_EPS) * norm_g.astype(f32).reshape(H, HG_DV)
    o = o.reshape(B, S, H * HG_DV) * jax.nn.silu(g_raw.astype(f32))
    return o.astype(q_raw.dtype)


def nsa(q_raw, k_cmp, v_cmp, k_slc, v_slc, k_win, v_win, gate_raw, pe, w1, w2, rel_bias):
    B, S, _ = q_raw.shape
    G, HPG, Dh = NSA_KV_HEADS, NSA_HEADS // NSA_KV_HEADS, NSA_HD
    q = q_raw.reshape(B, S, G, HPG, Dh) * (Dh ** -0.5)
    kv = lambda t: t.reshape(B, S, G, Dh)
    pos = jnp.arange(S)

    n_cmp = (S - CMP_LEN) // CMP_STRIDE + 1
    blk_idx = np.arange(n_cmp)[:, None] * CMP_STRIDE + np.arange(CMP_LEN)[None, :]

    def compress(t, pe_, w1_, w2_):
        blocks = t[:, blk_idx] + pe_[None, None, :, None, :]
        hid = jax.nn.silu(jnp.einsum("bnlgd,ldh->bngh", blocks, w1_.reshape(CMP_LEN, Dh, CMP_HIDDEN)))
        return hid @ w2_

    kc = compress(kv(k_cmp), pe[0], w1[0], w2[0])
    vc = compress(kv(v_cmp), pe[1], w1[1], w2[1])
    dist_c = pos[:, None] - jnp.asarray(blk_idx[:, -1])[None, :]
    bias_c = rel_bias[t5_bucket(dist_c)].reshape(S, n_cmp, G, HPG).transpose(2, 3, 0, 1)
    p_cmp = masked_softmax(jnp.einsum("bsghd,bngd->bghsn", q, kc) + bias_c, dist_c >= 0)
    o_cmp = jnp.einsum("bghsn,bngd->bsghd", p_cmp, vc.astype(jnp.float32))

    n_slc = S // SLC_BLOCK
    s_lo = np.arange(n_slc) * SLC_BLOCK
    cover = ((blk_idx[:, :1] <= (s_lo + SLC_BLOCK - 1)[None, :]) & (blk_idx[:, -1:] >= s_lo[None, :])).astype(np.float32)
    imp = jnp.einsum("bghsn,nm->bgsm", p_cmp, jnp.asarray(cover))
    cur = (pos // SLC_BLOCK)[:, None]
    blk = jnp.arange(n_slc)[None, :]
    forced = (blk == 0) | (blk == cur) | (blk == cur - 1)
    score = jnp.where(forced, jnp.inf, jnp.where(blk <= cur, imp, -jnp.inf))
    n_sel = min(SLC_TOPN, n_slc)
    _, sel = lax.top_k(score, n_sel)

    ks = kv(k_slc).reshape(B, n_slc, SLC_BLOCK, G, Dh).transpose(0, 3, 1, 2, 4)
    vs = kv(v_slc).reshape(B, n_slc, SLC_BLOCK, G, Dh).transpose(0, 3, 1, 2, 4)
    n_qb = S // SLC_QBLOCK
    q_b = q.reshape(B, n_qb, SLC_QBLOCK, G, HPG, Dh).transpose(1, 0, 2, 3, 4, 5)
    sel_b = sel.reshape(B, G, n_qb, SLC_QBLOCK, n_sel).transpose(2, 0, 1, 3, 4)
    qpos_b = pos.reshape(n_qb, SLC_QBLOCK)
    bi = jnp.arange(B)[:, None, None, None]
    gi = jnp.arange(G)[None, :, None, None]
    table_g = rel_bias.reshape(REL_BUCKETS, G, HPG).transpose(1, 0, 2)
    n_keys = n_sel * SLC_BLOCK

    def slc_block(args):
        qb, sb, qp = args
        kg = ks[bi, gi, sb].reshape(B, G, SLC_QBLOCK, n_keys, Dh)
        vg = vs[bi, gi, sb].reshape(B, G, SLC_QBLOCK, n_keys, Dh)
        kpos = (sb[..., None] * SLC_BLOCK + jnp.arange(SLC_BLOCK)).reshape(B, G, SLC_QBLOCK, n_keys)
        dist = qp[None, None, :, None] - kpos
        bias = table_g[gi, t5_bucket(dist)].transpose(0, 1, 4, 2, 3)
        logits = jnp.einsum("bqghd,bgqkd->bghqk", qb, kg) + bias
        pr = masked_softmax(logits, (dist >= 0)[:, :, None])
        return jnp.einsum("bghqk,bgqkd->bqghd", pr, vg.astype(jnp.float32))

    o_slc = lax.map(slc_block, (q_b, sel_b, qpos_b)).transpose(1, 0, 2, 3, 4, 5).reshape(B, S, G, HPG, Dh)

    nwb = S // WIN_QBLOCK
    nprev = WIN // WIN_QBLOCK
    kw_len = (nprev + 1) * WIN_QBLOCK

    def band(t):
        tb = jnp.pad(t.reshape(B, nwb, WIN_QBLOCK, G, Dh), ((0, 0), (nprev, 0), (0, 0), (0, 0), (0, 0)))
        return jnp.concatenate([tb[:, j:j + nwb] for j in range(nprev + 1)], axis=2)

    kwb, vwb = band(kv(k_win)), band(kv(v_win))
    ka = np.arange(kw_len)
    rel = nprev * WIN_QBLOCK + np.arange(WIN_QBLOCK)[:, None] - ka[None, :]
    kpos_w = (np.arange(nwb)[:, None] - nprev) * WIN_QBLOCK + ka[None, :]
    mask_w = ((rel >= 0) & (rel < WIN))[None] & (kpos_w >= 0)[:, None, :]
    bias_w = rel_bias[t5_bucket(jnp.asarray(rel))].reshape(WIN_QBLOCK, kw_len, G, HPG).transpose(2, 3, 0, 1)
    qw = q.reshape(B, nwb, WIN_QBLOCK, G, HPG, Dh)
    pw = masked_softmax(jnp.einsum("bnqghd,bnkgd->bnghqk", qw, kwb) + bias_w, jnp.asarray(mask_w)[None, :, None, None])
    o_win = jnp.einsum("bnghqk,bnkgd->bnqghd", pw, vwb.astype(jnp.float32)).reshape(B, S, G, HPG, Dh)

    g = jax.nn.sigmoid(gate_raw.astype(jnp.float32)).reshape(B, S, G, HPG, 3)
    o = g[..., 0:1] * o_cmp + g[..., 1:2] * o_slc + g[..., 2:3] * o_win
    return o.reshape(B, S, NSA_W).astype(q_raw.dtype)


def rwkv7(proj, mu, w0, wB, a0, aB, gB, k_k, k_a, r_k, ln_w, ln_b):
    B, S, _ = proj.shape
    H, N = RW_HEADS, RW_HD
    f32 = jnp.float32
    prev = jnp.pad(proj, ((0, 0), (1, 0), (0, 0)))[:, :-1]
    xm = proj + (prev - proj) * mu
    r, k, v, wl, al, gl = _split(xm, RW_SIZES)
    w = -jax.nn.softplus(-(w0 + jnp.tanh(wl) @ wB)) - 0.5
    decay = jnp.exp(-jnp.exp(w.astype(f32)))
    a = jax.nn.sigmoid(a0 + al @ aB)
    g = jax.nn.sigmoid(gl) @ gB
    hd = lambda t: t.astype(f32).reshape(B, S, H, N)
    kk = hd(k * k_k)
    kk = kk / jnp.maximum(jnp.sqrt(jnp.sum(kk * kk, axis=-1, keepdims=True)), 1e-12)
    k = k * (1.0 + (a - 1.0) * k_a)
    r_h, k_h, v_h, a_h, w_h = hd(r), hd(k), hd(v), hd(a), hd(decay)
    tm = lambda t: t.transpose(1, 0, 2, 3)

    def step(st, inp):
        r_t, w_t, k_t, v_t, kk_t, a_t = inp
        sa = jnp.einsum("bhvk,bhk->bhv", st, -kk_t)
        st = st * w_t[:, :, None, :] + sa[..., None] * (kk_t * a_t)[:, :, None, :] + v_t[..., None] * k_t[:, :, None, :]
        return st, jnp.einsum("bhvk,bhk->bhv", st, r_t)

    st0 = jnp.zeros((B, H, N, N), f32)
    _, y = lax.scan(step, st0, (tm(r_h), tm(w_h), tm(k_h), tm(v_h), tm(kk), tm(a_h)))
    y = y.transpose(1, 0, 2, 3)
    mean = jnp.mean(y, axis=-1, keepdims=True)
    var = jnp.mean(jnp.square(y - mean), axis=-1, keepdims=True)
    y = (y - mean) * lax.rsqrt(var + RW_GN_EPS) * ln_w.astype(f32).reshape(H, N) + ln_b.astype(f32).reshape(H, N)
    y = y + jnp.sum(r_h * k_h * r_k.astype(f32), axis=-1, keepdims=True) * v_h
    return (y.reshape(B, S, H * N) * g.astype(f32)).astype(proj.dtype)


def setup_inputs(seed: int = 0) -> dict:
    key = jax.random.key(seed)
    keys = iter(jax.random.split(key, 40))
    f32 = jnp.float32
    L, D = DEPTH, D_MODEL

    def nrm(shape, scale):
        return jax.random.normal(next(keys), shape, f32) * scale

    def gain(shape):
        return 1.0 + 0.02 * jax.random.normal(next(keys), shape, f32)

    return {
        "x": nrm((BATCH, SEQ, D), 1.0),
        "p": nrm((L, BATCH, SEQ, PLE_DIM), 1.0),
        "ffn1_norm": gain((L, D)),
        "ffn1_wgu": nrm((L, D, 2 * D_FF), D ** -0.5),
        "ffn1_wd": nrm((L, D_FF, D), D_FF ** -0.5),
        "mix_norm": gain((L, D)),
        "w_in": nrm((L, D, IN_COLS), D ** -0.5),
        "hg_lb": nrm((L, HG_W), 0.1),
        "hg_norm": gain((L, HG_W)),
        "cmp_pe": nrm((L, 2, CMP_LEN, NSA_HD), 0.1),
        "cmp_w1": nrm((L, 2, CMP_LEN * NSA_HD, CMP_HIDDEN), (CMP_LEN * NSA_HD) ** -0.5),
        "cmp_w2": nrm((L, 2, CMP_HIDDEN, NSA_HD), CMP_HIDDEN ** -0.5),
        "rel_bias": nrm((REL_BUCKETS, NSA_HEADS), 0.5),
        "rw_mu": jax.random.uniform(next(keys), (L, RW_COLS), f32),
        "rw_w0": nrm((L, RW_W), 0.5),
        "rw_wB": nrm((L, RW_DECAY_LORA, RW_W), RW_DECAY_LORA ** -0.5),
        "rw_a0": nrm((L, RW_W), 0.5),
        "rw_aB": nrm((L, RW_A_LORA, RW_W), RW_A_LORA ** -0.5),
        "rw_gB": nrm((L, RW_GATE_LORA, RW_W), RW_GATE_LORA ** -0.5),
        "rw_kk": 0.85 + nrm((L, RW_W), 0.1),
        "rw_ka": 1.0 + nrm((L, RW_W), 0.1),
        "rw_rk": nrm((L, RW_HEADS, RW_HD), 0.1),
        "rw_ln_w": gain((L, RW_W)),
        "rw_ln_b": nrm((L, RW_W), 0.02),
        "w_branch": nrm((L, N_BRANCH, MIX_W, D), MIX_W ** -0.5),
        "w_out": nrm((L, D, D), D ** -0.5),
        "ffn2_norm": gain((L, D)),
        "ffn2_wgu": nrm((L, D, 2 * D_FF), D ** -0.5),
        "ffn2_wd": nrm((L, D_FF, D), D_FF ** -0.5),
        "ple_norm": gain((L, D)),
        "ple_gate_w": nrm((L, D, D), D ** -0.5),
        "ple_w": nrm((L, PLE_DIM, D), PLE_DIM ** -0.5),
        "final_norm": gain((D,)),
    }


def reference(x, p, ffn1_norm, ffn1_wgu, ffn1_wd, mix_norm, w_in, hg_lb, hg_norm,
              cmp_pe, cmp_w1, cmp_w2, rel_bias, rw_mu, rw_w0, rw_wB, rw_a0, rw_aB, rw_gB,
              rw_kk, rw_ka, rw_rk, rw_ln_w, rw_ln_b, w_branch, w_out,
              ffn2_norm, ffn2_wgu, ffn2_wd, ple_norm, ple_gate_w, ple_w, final_norm):
    B, S, D = x.shape
    lb_w = jax.nn.softmax(hg_lb.astype(jnp.float32), axis=0)
    lower_bounds = jnp.cumsum(lb_w, axis=0) - lb_w[0]
    h = x
    for i in range(DEPTH):
        h = h + 0.5 * swiglu(rmsnorm(h, ffn1_norm[i]), ffn1_wgu[i], ffn1_wd[i])
        u = rmsnorm(h, mix_norm[i])
        (hq, hf, hi, hg, nq, kc, vc, ksl, vsl, kw, vw, ngate, rwp, mgate) = _split(u @ w_in[i], IN_SIZES)
        o_hg = hgrn2(hq, hf, hi, hg, lower_bounds[i], hg_norm[i])
        o_ns = nsa(nq, kc, vc, ksl, vsl, kw, vw, ngate, cmp_pe[i], cmp_w1[i], cmp_w2[i], rel_bias)
        o_rw = rwkv7(rwp, rw_mu[i], rw_w0[i], rw_wB[i], rw_a0[i], rw_aB[i], rw_gB[i],
                     rw_kk[i], rw_ka[i], rw_rk[i], rw_ln_w[i], rw_ln_b[i])
        gates = jax.nn.sigmoid(mgate).reshape(B, S, N_BRANCH, D)
        merged = (gates[:, :, 0] * (o_hg @ w_branch[i, 0])
                  + gates[:, :, 1] * (o_ns @ w_branch[i, 1])
                  + gates[:, :, 2] * (o_rw @ w_branch[i, 2]))
        h = h + merged @ w_out[i]
        h = h + 0.5 * swiglu(rmsnorm(h, ffn2_norm[i]), ffn2_wgu[i], ffn2_wd[i])
        ple_gate = jax.nn.sigmoid(rmsnorm(h, ple_norm[i]) @ ple_gate_w[i])
        h = h + ple_gate * (p[i] @ ple_w[i])
    return rmsnorm(h, final_norm)
```

```python
import numpy as np
from contextlib import ExitStack
import concourse.bass as bass
import concourse.mybir as mybir
from concourse.bass_utils import run_bass_kernel_spmd

F32 = mybir.dt.float32
BF16 = mybir.dt.bfloat16
I32 = mybir.dt.int32
U8 = mybir.dt.uint8
AF = mybir.ActivationFunctionType
ALU = mybir.AluOpType
AX = mybir.AxisListType

D = 1024
KC = 8
DFF = 2816
NJ = 22
PLE = 256
INCOLS = 8216
SEM_LIMIT = 30000


class Buf:
    def __init__(self, name, h, kind):
        self.name = name
        self.h = h
        self.kind = kind
        self.w = None
        self.w_eng = None
        self.r = {}
        self.dsem = None
        self.dcnt = 0

    def __getitem__(self, idx):
        return V(self, self.h[idx])


class V:
    def __init__(self, buf, ap):
        self.buf = buf
        self.ap = ap

    def __getitem__(self, idx):
        return V(self.buf, self.ap[idx])

    def rr(self, pat, **kw):
        return V(self.buf, self.ap.rearrange(pat, **kw))


class KB:
    def __init__(self):
        self.nc = bass.Bass("TRN2", target_bir_lowering=False)
        nc = self.nc
        self.es = ExitStack()
        self.E = {"pe": nc.tensor, "dve": nc.vector, "act": nc.scalar, "pool": nc.gpsimd, "sp": nc.sync}
        self.sems = []
        self.cur = {}
        self.cnt = {}
        self.seen = {e: {} for e in self.E}
        for e in self.E:
            self._newsem(e)
        self.nins = 0
        self.free_dsems = []
        self.semcnt = {}
        self.scopes = []

    def get_dsem(self, name):
        if self.free_dsems:
            return self.free_dsems.pop()
        s_ = self._alloc_sem(name)
        self.semcnt[s_] = 0
        return s_

    def _alloc_sem(self, name):
        s = self.es.enter_context(self.nc.semaphore(name))
        self.sems.append(s)
        return len(self.sems) - 1

    def _newsem(self, e):
        self.cur[e] = self._alloc_sem("s_%s_%d" % (e, len(self.sems)))
        self.cnt[e] = 0

    def sb(self, name, shape, dt, es=None):
        self.uid = getattr(self, "uid", 0) + 1
        name = "%s_%d" % (name, self.uid)
        h = (es or self.es).enter_context(self.nc.sbuf_tensor(name, list(shape), dt))
        b = Buf(name, h, "sb")
        if self.scopes:
            self.scopes[-1].append(b)
        return b

    def scope_begin(self):
        self.scopes.append([])

    def scope_end(self):
        bufs = self.scopes.pop()
        self.barrier(bufs)
        for b in bufs:
            if b.dsem is not None:
                self.free_dsems.append(b.dsem)
                b.dsem = None

    def ps(self, name, shape, dt):
        h = self.es.enter_context(self.nc.psum_tensor(name, list(shape), dt))
        return Buf(name, h, "ps")

    def dram(self, name, shape, dt, kind):
        h = self.nc.dram_tensor(name, list(shape), dt, kind=kind)
        b = Buf(name, h.ap(), "dram")
        return b

    def _wait(self, e, ev):
        if ev is None:
            return
        s, v = ev
        if self.seen[e].get(s, 0) >= v:
            return
        self.E[e].wait_ge(self.sems[s], v)
        self.seen[e][s] = v

    def _deps(self, e, reads, writes):
        for v in reads:
            b = v.buf
            if b.w is not None:
                self._wait(e, b.w)
        for v in writes:
            b = v.buf
            if b.w is not None and not (e == "pe" and b.w_eng == "pe"):
                if not (b.w_eng == e and e != "dma"):
                    self._wait(e, b.w)
            for s, val in b.r.items():
                if s == self.cur.get(e, -1):
                    continue
                self._wait(e, (s, val))

    def emit(self, e, fn, reads, writes):
        reads = [v for v in reads if isinstance(v, V)]
        self._deps(e, reads, writes)
        ins = fn()
        if self.cnt[e] >= SEM_LIMIT:
            self._newsem(e)
        self.cnt[e] += 1
        ins.then_inc(self.sems[self.cur[e]], 1)
        ev = (self.cur[e], self.cnt[e])
        for v in reads:
            b = v.buf
            if b.r.get(ev[0], 0) < ev[1]:
                b.r[ev[0]] = ev[1]
        for v in writes:
            b = v.buf
            b.w = ev
            b.w_eng = e
            b.r = {}
        self.nins += 1
        return ins

    def dma(self, q, out, in_, track=None, **kw):
        tb = track or (out.buf if out.buf.kind != "dramin" else in_.buf)
        if tb.dsem is None:
            tb.dsem = self.get_dsem("d_" + tb.name)
        reads = [in_] if in_.buf.kind != "dramin" else []
        writes = [out]
        for v in reads:
            if v.buf.w is not None:
                self._wait(q, v.buf.w)
        for v in writes:
            if v.buf.w is not None:
                self._wait(q, v.buf.w)
            for s, val in v.buf.r.items():
                self._wait(q, (s, val))
        ins = self.E[q].dma_start(out=out.ap, in_=in_.ap, **kw)
        ins.then_inc(self.sems[tb.dsem], 16)
        self.semcnt[tb.dsem] += 16
        ev = (tb.dsem, self.semcnt[tb.dsem])
        for v in reads:
            v.buf.r[ev[0]] = ev[1]
        out.buf.w = ev
        out.buf.w_eng = "dma"
        out.buf.r = {}
        self.nins += 1

    def barrier(self, bufs=()):
        evs = [(self.cur[e], self.cnt[e]) for e in self.E if self.cnt[e] > 0]
        for b in bufs:
            if b.w is not None:
                evs.append(b.w)
            evs.extend(b.r.items())
        for e in self.E:
            for ev in evs:
                if ev[0] == self.cur[e]:
                    continue
                self._wait(e, ev)

    def mm(self, out, lhsT, rhs, start=True, stop=True, sgc=False):
        return self.emit("pe", lambda: self.nc.tensor.matmul(out.ap, lhsT.ap, rhs.ap, start=start, stop=stop,
                                                             skip_group_check=sgc), [lhsT, rhs], [out])

    def tr(self, out, in_, ident):
        return self.emit("pe", lambda: self.nc.tensor.transpose(out.ap, in_.ap, ident.ap), [in_, ident], [out])

    def act(self, out, in_, func, bias=None, scale=None, accum=None):
        kw = {}
        rd = [in_]
        if bias is not None:
            kw["bias"] = bias.ap if isinstance(bias, V) else bias
            rd.append(bias)
        if scale is not None:
            kw["scale"] = scale.ap if isinstance(scale, V) else scale
            rd.append(scale)
        wr = [out]
        if accum is not None:
            kw["accum_out"] = accum.ap
            wr.append(accum)
        return self.emit("act", lambda: self.nc.scalar.activation(out=out.ap, in_=in_.ap, func=func, **kw), rd, wr)

    def ts(self, out, in0, s1, op0, s2=None, op1=None, eng="dve"):
        a1 = s1.ap if isinstance(s1, V) else s1
        a2 = s2.ap if isinstance(s2, V) else s2
        kw = {}
        if op1 is not None:
            kw["op1"] = op1
        E = self.E[eng]
        return self.emit(eng, lambda: E.tensor_scalar(out=out.ap, in0=in0.ap, scalar1=a1, scalar2=a2, op0=op0, **kw),
                         [in0, s1, s2], [out])

    def tt(self, out, in0, in1, op, eng="dve"):
        E = self.E[eng]
        return self.emit(eng, lambda: E.tensor_tensor(out=out.ap, in0=in0.ap, in1=in1.ap, op=op), [in0, in1], [out])

    def stt(self, out, in0, scalar, in1, op0, op1):
        a = scalar.ap if isinstance(scalar, V) else scalar
        return self.emit("dve", lambda: self.nc.vector.scalar_tensor_tensor(
            out=out.ap, in0=in0.ap, scalar=a, in1=in1.ap, op0=op0, op1=op1), [in0, scalar, in1], [out])

    def cp(self, out, in_, eng="dve"):
        if eng == "act":
            return self.emit("act", lambda: self.nc.scalar.copy(out=out.ap, in_=in_.ap), [in_], [out])
        E = self.E[eng]
        return self.emit(eng, lambda: E.tensor_copy(out=out.ap, in_=in_.ap), [in_], [out])

    def scan(self, out, d0, d1, init, op0, op1):
        a = init.ap if isinstance(init, V) else init
        return self.emit("dve", lambda: self.nc.vector.tensor_tensor_scan(
            out=out.ap, data0=d0.ap, data1=d1.ap, initial=a, op0=op0, op1=op1), [d0, d1, init], [out])

    def memset(self, out, val, eng="dve"):
        E = self.E[eng]
        return self.emit(eng, lambda: E.memset(out.ap, val), [], [out])

    def recip(self, out, in_):
        return self.emit("dve", lambda: self.nc.vector.reciprocal(out=out.ap, in_=in_.ap), [in_], [out])

    def max8(self, out, in_):
        return self.emit("dve", lambda: self.nc.vector.max(out=out.ap, in_=in_.ap), [in_], [out])

    def red(self, out, in_, op, axis=AX.X):
        return self.emit("dve", lambda: self.nc.vector.tensor_reduce(out=out.ap, in_=in_.ap, axis=axis, op=op),
                         [in_], [out])

    def cpred(self, out, mask, data):
        return self.emit("dve", lambda: self.nc.vector.copy_predicated(out=out.ap, mask=mask.ap, data=data.ap),
                         [mask, data, out], [out])


def blockify(W, colsets):
    K = W.shape[0]
    kc = K // 128
    out = []
    for cols in colsets:
        blk = W[:, cols]
        blk = blk.reshape(kc, 128, len(cols)).transpose(1, 0, 2)
        out.append(np.ascontiguousarray(blk).reshape(-1))
    return out


class Packer:
    def __init__(self):
        self.parts = []
        self.off = 0
        self.idx = {}

    def add(self, name, W, colsets):
        K = W.shape[0]
        blks = blockify(W, colsets)
        lst = []
        for b, cols in zip(blks, colsets):
            lst.append((self.off, K // 128, len(cols)))
            self.parts.append(b)
            self.off += b.size
        self.idx[name] = lst

    def finish(self, mult=16384):
        pad = (-self.off) % mult
        if pad:
            self.parts.append(np.zeros(pad, np.float32))
            self.off += pad
        return np.concatenate(self.parts)


def ar(a, n):
    return np.arange(a, a + n)


HG_OFF = 0
NSA_Q_OFF = 2048
NSA_KV_OFF = 2560
NSA_GATE_OFF = 3328
RW_OFF = 3352
MG_OFF = 5144


def pack_layer(inp, l):
    pk = Packer()
    for nm in ("ffn1", "ffn2"):
        wgu = inp[nm + "_wgu"][l]
        pk.add(nm + "_gu", wgu, [np.concatenate([ar(j * 128, 128), ar(DFF + j * 128, 128)]) for j in range(NJ)])
        wd = inp[nm + "_wd"][l]
        pk.add(nm + "_d", wd, [ar(m * 128, 128) for m in range(8)])
    w_in = inp["w_in"][l]
    pk.add("mg", w_in, [ar(MG_OFF + b * D + m * 128, 128) for b in range(3) for m in range(8)])
    wb = inp["w_branch"][l]
    for b in range(3):
        pk.add("br%d" % b, wb[b], [ar(m * 128, 128) for m in range(8)])
    pk.add("wout", inp["w_out"][l], [ar(m * 128, 128) for m in range(8)])
    pk.add("pgw", inp["ple_gate_w"][l], [ar(m * 128, 128) for m in range(8)])
    pk.add("plw", inp["ple_w"][l], [ar(m * 128, 128) for m in range(8)])
    pk.add("hg", w_in, [np.concatenate([ar(HG_OFF + t * 512 + hp * 128, 128) for t in range(4)]) for hp in range(4)])
    pk.add("nq", w_in, [ar(NSA_Q_OFF + g * 256, 256) for g in range(2)])
    pk.add("nkv", w_in, [np.concatenate([ar(NSA_KV_OFF + t * 128 + g * 64, 64) for t in range(6)]) for g in range(2)])
    pk.add("ngate", w_in, [ar(NSA_GATE_OFF, 24)])
    for kv in range(2):
        w1 = inp["cmp_w1"][l][kv]
        w1r = np.zeros((128, 4096), np.float32)
        w1r[0:64] = w1.reshape(32, 64, 128).transpose(1, 0, 2).reshape(64, 4096)
        pk.add("cw1_%d" % kv, w1r, [ar(0, 4096)])
        pk.add("cw1c_%d" % kv, w1, [ar(0, 128)])
        pk.add("cw2_%d" % kv, inp["cmp_w2"][l][kv], [ar(0, 64)])
    RW = RW_OFF
    pk.add("rwl", w_in, [ar(RW + 1536, 256)])
    pk.add("rw", w_in, [np.concatenate([ar(RW + t * 512 + hp * 128, 128) for t in range(3)]) for hp in range(4)])
    AB = np.concatenate([inp["rw_wB"][l], inp["rw_aB"][l]], axis=0)
    pk.add("rwAB", AB, [ar(hp * 128, 128) for hp in range(4)])
    pk.add("rwgB", inp["rw_gB"][l], [ar(hp * 128, 128) for hp in range(4)])
    flat = pk.finish()
    return flat, pk.idx


def pack_params(inp, l):
    cols = []
    idx = {}

    def add(name, vec):
        v = np.asarray(vec, np.float32).reshape(-1, 128).T
        idx[name] = (sum(c.shape[1] for c in cols), v.shape[1])
        cols.append(v)

    add("ffn1_norm", inp["ffn1_norm"][l])
    add("mix_norm", inp["mix_norm"][l])
    add("ffn2_norm", inp["ffn2_norm"][l])
    add("ple_norm", inp["ple_norm"][l])
    add("final_norm", inp["final_norm"])
    add("hg_norm", inp["hg_norm"][l])
    for j in range(4):
        add("hg_lb%d" % j, inp["hg_lb"][j])
    add("cmp_pe0", inp["cmp_pe"][l][0].reshape(-1))
    add("cmp_pe1", inp["cmp_pe"][l][1].reshape(-1))
    add("rw_mu", inp["rw_mu"][l])
    for nm in ("rw_w0", "rw_a0", "rw_kk", "rw_ka", "rw_ln_w", "rw_ln_b"):
        add(nm, inp[nm][l])
    add("rw_rk", inp["rw_rk"][l].reshape(-1))
    return np.ascontiguousarray(np.concatenate(cols, axis=1)), idx


class Prog:
    def __init__(self, S, NL, NSEQ, widx, pidx, wsize, npar, flags):
        self.S, self.NL, self.NSEQ = S, NL, NSEQ
        self.widx, self.pidx = widx, pidx
        self.flags = flags
        k = self.k = KB()
        nc = k.nc
        self.NT = S // 512
        self.xT = k.dram("xT", [NSEQ, D, S], F32, "ExternalInput")
        self.xT.kind = "dramin"
        self.pT = k.dram("pT", [NL, NSEQ, PLE, S], F32, "ExternalInput")
        self.pT.kind = "dramin"
        self.wl = []
        self.ws = []
        for l in range(NL):
            b = k.dram("wl%d" % l, [wsize // 1024, 1024], F32, "ExternalInput")
            b.kind = "dramin"
            self.wl.append(b)
            self.ws.append(k.dram("ws%d" % l, [wsize // 1024, 1024], BF16, "Internal"))
        self.par_d = k.dram("par", [NL, 128, npar], F32, "ExternalInput")
        self.par_d.kind = "dramin"
        self.NCST = 896 + S + 768
        self.cst_d = k.dram("cst", [128, self.NCST], F32, "ExternalInput")
        self.cst_d.kind = "dramin"
        self.outT = k.dram("outT", [NSEQ, D, S], F32, "ExternalOutput")
        NTq = S // 128
        self.dtl_d = k.dram("dtl", [2, 4, 128, 512], F32, "ExternalInput")
        self.bcb_d = k.dram("bcb", [2, NTq, 128, 512], F32, "ExternalInput")
        self.selc_d = k.dram("selc", [NTq, 128, 64], F32, "ExternalInput")
        self.exc_d = k.dram("exc", [32, S], F32, "ExternalInput")
        self.cov_d = k.dram("cov", [128, 32], F32, "ExternalInput")
        for b_ in (self.dtl_d, self.bcb_d, self.selc_d, self.exc_d, self.cov_d):
            b_.kind = "dramin"
        self.hT = k.sb("hT", [128, KC, S], F32)
        self.uT = k.sb("uT", [128, KC, S], BF16)
        self.par = k.sb("par_sb", [128, NL, npar], F32)
        self.cst = k.sb("cst_sb", [128, 896], F32)
        self.cst2 = k.sb("cst2_sb", [128, 768], F32)
        self.ones_bf = k.sb("ones_bf", [128, 128], BF16)
        self.blk_bf = k.sb("blk_bf", [128, 128], BF16)
        self.ident_bf = k.sb("ident_bf", [128, 128], BF16)
        self.cmask = k.sb("cmask", [128, S], BF16)
        self.oml = k.sb("oml", [128, 4, NL], F32)
        self.psb = [k.ps("ps%d" % i, [128, 512], F32) for i in range(8)]
        self.psi = 0
        self.wbufs = {}
        self.build()

    def ps(self):
        if getattr(self, "ps_fixed", None) is not None:
            return self.psb[self.ps_fixed]
        if getattr(self, "ps_stream", None) is not None:
            st = self.ps_stream
            st[1] += 1
            return self.psb[st[0] + st[1] % 3]
        p = self.psb[self.psi % getattr(self, "psn", 6)]
        self.psi += 1
        return p

    def P(self, l, name):
        o, n = self.pidx[name]
        return self.par[:, l, o:o + n]

    def wbuf(self, tag, shape, nbuf=3):
        if tag not in self.wbufs:
            self.wbufs[tag] = [[self.k.sb("w_%s_%d" % (tag, i), shape, BF16, es=self.pes) for i in range(nbuf)], 0]
        lst = self.wbufs[tag]
        b = lst[0][lst[1] % len(lst[0])]
        lst[1] += 1
        return b

    def wload(self, l, name, j, tag, nbuf=3):
        off, kc, nb = self.widx[l][name][j]
        b = self.wbuf(tag, [128, kc, nb], nbuf)
        flat = self.ws[l].h.rearrange("a b -> (a b)")
        src = flat[off:off + 128 * kc * nb].rearrange("(p k n) -> p k n", p=128, k=kc)
        self.k.dma("sp", b[:, :, :], V(self.ws[l], src))
        return b

    def cast_weights(self, l):
        k = self.k
        R = self.wl[l].h.shape[0]
        step = 2048
        for r0 in range(0, R, step):
            r1 = min(R, r0 + step)
            k.dma("pool", V(self.ws[l], self.ws[l].h[r0:r1, :]), V(self.wl[l], self.wl[l].h[r0:r1, :]))

    def rmsnorm(self, g, t0, n, out, out_t0, eps=1e-6):
        k = self.k
        for s0 in range(0, n, 512):
            ts_ = slice(t0 + s0, t0 + s0 + 512)
            pp = self.ps()
            for kc in range(KC):
                sq = self.wbuf_f("sq", [128, 512], BF16)
                k.act(sq[:, :], self.hT[:, kc, ts_], AF.Square)
                k.mm(pp[:, :], self.ones_bf[:, :], sq[:, :], start=(kc == 0), stop=(kc == KC - 1))
            rs = self.wbuf_f("rstd", [128, 512], F32, 2)
            k.act(rs[:, :], pp[:, :], AF.Sqrt, bias=self.cst[:, 130:131], scale=1.0 / D)
            k.recip(rs[:, :], rs[:, :])
            os_ = slice(out_t0 + s0, out_t0 + s0 + 512)
            for kc in range(KC):
                k.stt(out[:, kc, os_], self.hT[:, kc, ts_], g[:, kc:kc + 1], rs[:, :], ALU.mult, ALU.mult)

    def wbuf_f(self, tag, shape, dt, nbuf=3):
        key = "f_" + tag
        if key not in self.wbufs:
            self.wbufs[key] = [[self.k.sb("t_%s_%d" % (tag, i), shape, dt, es=self.pes) for i in range(nbuf)], 0]
        lst = self.wbufs[key]
        b = lst[0][lst[1] % len(lst[0])]
        lst[1] += 1
        return b

    def rmsnorm_gen(self, g, t0, n, out, out_t0):
        k = self.k
        for s0 in range(0, n, 512):
            ts_ = slice(t0 + s0, t0 + s0 + 512)
            pp = self.psb[7]
            for kc in range(KC):
                sq = self.wbuf_f("sq", [128, 512], BF16)
                k.act(sq[:, :], self.hT[:, kc, ts_], AF.Square)
                k.mm(pp[:, :], self.ones_bf[:, :], sq[:, :], start=(kc == 0), stop=(kc == KC - 1))
                yield
            rs = self.wbuf_f("rstd", [128, 512], F32, 2)
            k.act(rs[:, :], pp[:, :], AF.Sqrt, bias=self.cst[:, 130:131], scale=1.0 / D)
            k.recip(rs[:, :], rs[:, :])
            os_ = slice(out_t0 + s0, out_t0 + s0 + 512)
            for kc in range(KC):
                k.stt(out[:, kc, os_], self.hT[:, kc, ts_], g[:, kc:kc + 1], rs[:, :], ALU.mult, ALU.mult)
                yield

    def ffn_all(self, l, nm, gname, TT):
        k = self.k
        S = self.S
        g = self.P(l, gname)
        tiles = list(range(0, S, TT))
        nh = len(tiles)
        full = self.uT.h
        uh = [Buf("uT_h%d" % i, full[:, :, t0:t0 + TT], "sb") for i, t0 in enumerate(tiles)]
        actT = k.sb("actT", [128, NJ, TT], BF16, es=self.pes)
        for _ in self.rmsnorm_gen(g, tiles[0], TT, uh[0], 0):
            pass
        for ti, t0 in enumerate(tiles):
            u = uh[ti]
            gen = self.rmsnorm_gen(g, tiles[ti + 1], TT, uh[ti + 1], 0) if ti + 1 < nh else None
            for j in range(NJ):
                w = self.wload(l, nm + "_gu", j, "gu")
                for s0 in range(0, TT, 512):
                    pg, pu = self.ps(), self.ps()
                    for kc in range(KC):
                        k.mm(pg[:, :], w[:, kc, 0:128], u[:, kc, s0:s0 + 512], start=(kc == 0), stop=(kc == KC - 1))
                    for kc in range(KC):
                        k.mm(pu[:, :], w[:, kc, 128:256], u[:, kc, s0:s0 + 512], start=(kc == 0), stop=(kc == KC - 1))
                    sg = self.wbuf_f("sg", [128, 512], F32)
                    k.act(sg[:, :], pg[:, :], AF.Silu)
                    k.tt(actT[:, j, s0:s0 + 512], sg[:, :], pu[:, :], ALU.mult)
                    if gen is not None:
                        for _ in range(2):
                            try:
                                next(gen)
                            except StopIteration:
                                gen = None
                                break
            if gen is not None:
                for _ in gen:
                    pass
            for m in range(8):
                w = self.wload(l, nm + "_d", m, "wd", 2)
                for s0 in range(0, TT, 512):
                    ts_ = slice(t0 + s0, t0 + s0 + 512)
                    po = self.ps()
                    for j in range(NJ):
                        k.mm(po[:, :], w[:, j, :], actT[:, j, s0:s0 + 512], start=(j == 0), stop=(j == NJ - 1))
                    k.stt(self.hT[:, m, ts_], po[:, :], 0.5, self.hT[:, m, ts_], ALU.mult, ALU.add)

    def ple(self, l, q, t0, n):
        k = self.k
        u = self.uT
        self.rmsnorm(self.P(l, "ple_norm"), t0, n, u, t0)
        pf = k.sb("pf", [128, 2, n], F32, es=self.pes)
        pb = k.sb("pb", [128, 2, n], BF16, es=self.pes)
        src = self.pT.h[l, q].rearrange("(c p) t -> p c t", p=128)[:, :, t0:t0 + n]
        k.dma("sp", pf[:, :, :], V(self.pT, src))
        k.cp(pb[:, :, :], pf[:, :, :], eng="pool")
        for m in range(8):
            wg = self.wload(l, "pgw", m, "w8", 3)
            wp = self.wload(l, "plw", m, "w2", 2)
            for s0 in range(0, n, 512):
                ts_ = slice(t0 + s0, t0 + s0 + 512)
                pg, pp = self.ps(), self.ps()
                for kc in range(KC):
                    k.mm(pg[:, :], wg[:, kc, :], u[:, kc, ts_], start=(kc == 0), stop=(kc == KC - 1))
                for c in range(2):
                    k.mm(pp[:, :], wp[:, c, :], pb[:, c, s0:s0 + 512], start=(c == 0), stop=(c == 1))
                sg = self.wbuf_f("sg", [128, 512], F32)
                k.act(sg[:, :], pg[:, :], AF.Sigmoid)
                k.tt(sg[:, :], sg[:, :], pp[:, :], ALU.mult)
                k.tt(self.hT[:, m, ts_], self.hT[:, m, ts_], sg[:, :], ALU.add)

    def merge_branch(self, l, b, oT, merged, first):
        k = self.k
        S = self.S
        for m in range(8):
            wb = self.wload(l, "br%d" % b, m, "w4", 2)
            wg = self.wload(l, "mg", b * 8 + m, "w8", 3)
            for t0 in range(0, S, 512):
                ts_ = slice(t0, t0 + 512)
                pa, pg = self.ps(), self.ps()
                for c in range(4):
                    k.mm(pa[:, :], wb[:, c, :], oT[:, c, ts_], start=(c == 0), stop=(c == 3))
                for kc in range(KC):
                    k.mm(pg[:, :], wg[:, kc, :], self.uT[:, kc, ts_], start=(kc == 0), stop=(kc == KC - 1))
                sg = self.wbuf_f("sg", [128, 512], F32)
                k.act(sg[:, :], pg[:, :], AF.Sigmoid)
                if first:
                    k.tt(merged[:, m, ts_], sg[:, :], pa[:, :], ALU.mult)
                else:
                    k.tt(sg[:, :], sg[:, :], pa[:, :], ALU.mult)
                    k.tt(merged[:, m, ts_], merged[:, m, ts_], sg[:, :], ALU.add)

    def out_proj(self, l, merged):
        k = self.k
        for m in range(8):
            w = self.wload(l, "wout", m, "w8", 3)
            for t0 in range(0, self.S, 512):
                ts_ = slice(t0, t0 + 512)
                po = self.ps()
                for kc in range(KC):
                    k.mm(po[:, :], w[:, kc, :], merged[:, kc, ts_], start=(kc == 0), stop=(kc == KC - 1))
                k.tt(self.hT[:, m, ts_], self.hT[:, m, ts_], po[:, :], ALU.add)

    def phase_begin(self):
        self.pes = ExitStack()
        self.wbufs = {}
        self.k.scope_begin()

    def phase_end(self):
        self.k.scope_end()
        self.pes.close()
        self.wbufs = {}

    def mixers(self, l, q):
        k = self.k
        S = self.S
        self.phase_begin()
        self.rmsnorm(self.P(l, "mix_norm"), 0, S, self.uT, 0)
        self.phase_end()
        self.phase_begin()
        oT = k.sb("oT", [128, 4, S], BF16, es=self.pes)
        merged = None
        first = True
        for b in (2, 0, 1):
            if not self.flags.get("mix%d" % b, False):
                continue
            mes = ExitStack()
            save = self.pes, self.wbufs
            self.pes, self.wbufs = mes, {}
            k.scope_begin()
            [self.hgrn2, self.nsa, self.rwkv][b](l, q, oT)
            k.scope_end()
            mes.close()
            self.pes, self.wbufs = save
            if merged is None:
                merged = k.sb("merged", [128, KC, S], BF16, es=self.pes)
            mes = ExitStack()
            self.pes, self.wbufs = mes, {}
            k.scope_begin()
            self.merge_branch(l, b, oT, merged, first)
            k.scope_end()
            mes.close()
            self.pes, self.wbufs = save
            first = False
        if not first:
            self.out_proj(l, merged)
        self.phase_end()


    def prep_lb(self):
        k = self.k
        NL = self.NL
        e = k.sb("lb_e", [128, 4, 4], F32)
        ssum = k.sb("lb_s", [128, 4], F32)
        for j in range(4):
            k.act(e[:, :, j], self.P(0, "hg_lb%d" % j), AF.Exp)
        k.tt(ssum[:, :], e[:, :, 0], e[:, :, 1], ALU.add)
        k.tt(ssum[:, :], ssum[:, :], e[:, :, 2], ALU.add)
        k.tt(ssum[:, :], ssum[:, :], e[:, :, 3], ALU.add)
        k.recip(ssum[:, :], ssum[:, :])
        acc = k.sb("lb_acc", [128, 4], F32)
        k.memset(acc[:, :], 1.0)
        for l in range(NL):
            if l > 0:
                tmp = k.sb("lb_t%d" % l, [128, 4], F32)
                k.tt(tmp[:, :], e[:, :, l], ssum[:, :], ALU.mult)
                k.tt(acc[:, :], acc[:, :], tmp[:, :], ALU.subtract)
            k.cp(self.oml[:, :, l], acc[:, :])

    def psbf(self):
        p = self.ps()
        return V(p, p.h[:, :].bitcast(BF16))

    def hgrn2(self, l, q, oT):
        k = self.k
        S = self.S
        BW = 256
        NCH = BW // 64
        cmk = self.cst[:, 384:384 + BW]
        one = self.cst[:, 128:129]
        gn = self.P(l, "hg_norm")

        def stream(hp, sid):
            sfx = "_%d" % sid
            T2 = lambda nm, dt=F32, w=BW: k.sb(nm + sfx, [128, w], dt, es=self.pes)
            qT, kT, bT, sgT = T2("hq"), T2("hk"), T2("hb"), T2("hsg")
            dT, eT, q1, k1, q2 = T2("hd"), T2("he"), T2("hq1"), T2("hk1"), T2("hq2")
            vb, k2b, vtok, k2tok, scb, o2 = (T2("hvb", BF16), T2("hk2b", BF16), T2("hvtok", BF16), T2("hk2tok", BF16),
                                             T2("hscb", BF16), T2("ho2", BF16))
            dec = T2("hdec", F32, NCH)
            rs, tmp = T2("hrs"), T2("htmp")
            state = T2("hstate", F32, 64)
            pst = [3 * sid, 0]
            psO = self.psb[6 + sid]

            def PS():
                self.ps_stream = pst
                p = self.ps()
                self.ps_stream = None
                return p

            def PSBF():
                p = PS()
                return V(p, p.h[:, :].bitcast(BF16))

            def sigm(out, in_, sign):
                k.act(out, in_, AF.Exp, scale=-1.0 * sign)
                k.act(out, out, AF.Ln, bias=one)
                k.act(out, out, AF.Exp, scale=-1.0)

            w = self.wload(l, "hg", hp, "whg" + sfx, 1)
            k.memset(state[:, :], 0.0)
            units = [(slice(h * 64, h * 64 + 64), slice(c * 64, c * 64 + 64)) for h in range(2) for c in range(NCH)]

            def proj(i, ts_):
                pp = PS()
                for kc in range(KC):
                    k.mm(pp[:, 0:BW], w[:, kc, i * 128:(i + 1) * 128], self.uT[:, kc, ts_], start=(kc == 0), stop=(kc == KC - 1))
                return pp

            for t0 in range(0, S, BW):
                ts_ = slice(t0, t0 + BW)
                pq = proj(0, ts_)
                yield
                sigm(qT[:, :], pq[:, 0:BW], 1.0)
                k.tt(qT[:, :], qT[:, :], pq[:, 0:BW], ALU.mult)
                pg = proj(3, ts_)
                yield
                sigm(sgT[:, :], pg[:, 0:BW], 1.0)
                k.tt(sgT[:, :], sgT[:, :], pg[:, 0:BW], ALU.mult)
                pf = proj(1, ts_)
                yield
                sigm(kT[:, :], pf[:, 0:BW], -1.0)
                k.ts(kT[:, :], kT[:, :], self.oml[:, hp, l:l + 1], ALU.mult)
                k.act(dT[:, :], kT[:, :], AF.Ln, bias=one, scale=-1.0)
                pi = proj(2, ts_)
                yield
                k.cp(vb[:, :], pi[:, 0:BW], eng="act")
                k.scan(bT[:, :], self.cmask[:, ts_], dT[:, :], 0.0, ALU.mult, ALU.add)
                b3 = bT[:, :].rr("p (c t) -> p c t", t=64)
                bmid = V(bT, b3.ap[:, :, 31:32].broadcast_to([128, NCH, 64]))
                bend = V(bT, b3.ap[:, :, 63:64].broadcast_to([128, NCH, 64]))
                d3 = dT[:, :].rr("p (c t) -> p c t", t=64)
                k.tt(d3, b3, bmid, ALU.subtract)
                k.act(eT[:, :], dT[:, :], AF.Exp)
                k.tt(q1[:, :], qT[:, :], eT[:, :], ALU.mult, eng="pool")
                yield
                k.act(eT[:, :], dT[:, :], AF.Exp, scale=-1.0)
                k.tt(k1[:, :], kT[:, :], eT[:, :], ALU.mult)
                k.act(eT[:, :], bT[:, :], AF.Exp)
                k.tt(q2[:, :], qT[:, :], eT[:, :], ALU.mult, eng="pool")
                yield
                k.tt(d3, bend, b3, ALU.subtract)
                k.act(eT[:, :], dT[:, :], AF.Exp)
                k.tt(k2b[:, :], kT[:, :], eT[:, :], ALU.mult)
                k.act(dec[:, :], b3[:, :, 63], AF.Exp)
                yield
                pv = PSBF()
                for hs, cs in units:
                    k.tr(pv[hs, cs], vb[hs, cs], self.ident_bf[hs, hs])
                pk2 = PSBF()
                for hs, cs in units:
                    k.tr(pk2[hs, cs], k2b[hs, cs], self.ident_bf[hs, hs])
                yield
                k.cp(vtok[:, :], pv[:, 0:BW], eng="act")
                k.cp(k2tok[:, :], pk2[:, 0:BW])
                psS = PS()
                for hs, cs in units:
                    k.mm(psS[hs, cs], k1[hs, cs], q1[hs, cs])
                yield
                k.tt(scb[:, :], psS[:, 0:BW], cmk, ALU.mult)
                for c in range(NCH):
                    cs = slice(c * 64, c * 64 + 64)
                    psH = PS()
                    for h in range(2):
                        hs = slice(h * 64, h * 64 + 64)
                        k.mm(psO[hs, cs], vtok[hs, cs], scb[hs, cs], start=True, stop=False)
                        k.mm(psO[hs, cs], state[hs, :], q2[hs, cs], start=False, stop=True)
                        k.mm(psH[hs, 0:64], k2tok[hs, cs], vtok[hs, cs])
                    yield
                    k.stt(state[:, :], state[:, :], dec[:, c:c + 1], psH[:, 0:64], ALU.mult, ALU.add)
                k.cp(tmp[:, :], psO[:, 0:BW], eng="act")
                k.tt(o2[:, :], tmp[:, :], tmp[:, :], ALU.mult, eng="pool")
                psN = PS()
                k.mm(psN[:, 0:BW], self.blk_bf[:, :], o2[:, :])
                yield
                k.act(rs[:, :], psN[:, 0:BW], AF.Ln, bias=self.cst[:, 130:131], scale=1.0 / 64)
                k.act(rs[:, :], rs[:, :], AF.Exp, scale=-0.5)
                k.stt(tmp[:, :], tmp[:, :], gn[:, hp:hp + 1], rs[:, :], ALU.mult, ALU.mult)
                k.tt(oT[:, hp, ts_], tmp[:, :], sgT[:, :], ALU.mult)
                yield

        for pair in ((0, 1), (2, 3)):
            pes_ = ExitStack()
            k.scope_begin()
            save = self.pes, self.wbufs
            self.pes, self.wbufs = pes_, {}
            alive = [stream(pair[0], 0), stream(pair[1], 1)]
            if self.flags.get("hg1"):
                for g_ in alive:
                    for _ in g_:
                        pass
                alive = []
            while alive:
                for g_ in list(alive):
                    try:
                        next(g_)
                    except StopIteration:
                        alive.remove(g_)
            k.scope_end()
            pes_.close()
            self.pes, self.wbufs = save

    def nsa(self, l, q, oT):
        k = self.k
        S = self.S
        NT = S // 128
        NC = S // 16 - 1
        self.psn = 4
        es = self.pes
        sbt = lambda nm, shape, dt=F32: k.sb(nm, shape, dt, es=es)
        Ex = sbt("nEx", [128, S], BF16)
        k.memset(Ex[:, :], 0.0)
        k.dma("pool", Ex[0:32, :], V(self.exc_d, self.exc_d.h[:, :]))
        KsT, KwT = sbt("nKsT", [128, S], BF16), sbt("nKwT", [128, S], BF16)
        for t_ in (KsT, KwT):
            k.memset(t_[64:128, :], 0.0)
            k.memset(t_[64:65, :], 1.0)
        Vs, Vw = sbt("nVs", [128, NT, 65], BF16), sbt("nVw", [128, NT, 65], BF16)
        kcT = sbt("nkcT", [128, 128], BF16)
        k.memset(kcT[:, :], 0.0)
        Vc = sbt("nVc", [128, 97], BF16)
        pe_bf = sbt("npe", [128, 2, 16], BF16)
        cb = sbt("ncb", [128, 2])
        k.cp(pe_bf[:, 0, :], self.P(l, "cmp_pe0"))
        k.cp(pe_bf[:, 1, :], self.P(l, "cmp_pe1"))
        for g in range(2):
            ces = ExitStack()
            k.scope_begin()
            save = self.pes, self.wbufs
            self.pes, self.wbufs = ces, {}
            wkv = self.wload(l, "nkv", g, "wnkv", 1)
            xc = [k.sb("nxc%d" % i, [128, S], BF16, es=ces) for i in range(2)]
            for t_ in xc:
                k.memset(t_[64:128, :], 0.0)
            hid = k.sb("nhid", [128, 128], BF16, es=ces)
            k.memset(Vs[:, :, 64:65], 1.0)
            k.memset(Vw[:, :, 64:65], 1.0)
            for t0 in range(0, S, 512):
                ts_ = slice(t0, t0 + 512)
                for i, dst in ((0, xc[0]), (1, xc[1]), (2, KsT), (4, KwT)):
                    pp = self.ps()
                    for kc in range(KC):
                        k.mm(pp[0:64, :], wkv[:, kc, i * 64:(i + 1) * 64], self.uT[:, kc, ts_], start=(kc == 0), stop=(kc == KC - 1))
                    k.cp(dst[0:64, ts_], pp[0:64, :], eng=("act" if i % 4 == 0 else "dve"))
            for qt in range(NT):
                qs = slice(qt * 128, qt * 128 + 128)
                for i, dst in ((3, Vs), (5, Vw)):
                    pp = self.ps()
                    for kc in range(KC):
                        k.mm(pp[:, 0:64], self.uT[:, kc, qs], wkv[:, kc, i * 64:(i + 1) * 64], start=(kc == 0), stop=(kc == KC - 1))
                    k.cp(dst[:, qt, 0:64], pp[:, 0:64], eng=("act" if i == 3 else "dve"))
            for kv in range(2):
                w1 = self.wload(l, "cw1_%d" % kv, 0, "wcw1", 1)
                w1c = self.wload(l, "cw1c_%d" % kv, 0, "wcw1c", 1)
                w2 = self.wload(l, "cw2_%d" % kv, 0, "wcw2", 1)
                pc_ = self.ps()
                for kc in range(16):
                    k.mm(pc_[:, 0:1], w1c[:, kc, :], pe_bf[:, kv, kc:kc + 1], start=(kc == 0), stop=(kc == 15))
                k.cp(cb[:, kv:kv + 1], pc_[:, 0:1])
                ph = self.ps()
                for l_ in range(32):
                    k.mm(ph[:, 0:NC], w1[:, 0, l_ * 128:(l_ + 1) * 128], xc[kv][:, l_:l_ + 16 * (NC - 1) + 1:16],
                         start=(l_ == 0), stop=(l_ == 31))
                k.act(hid[:, 0:NC], ph[:, 0:NC], AF.Silu, bias=cb[:, kv:kv + 1])
                po = self.ps()
                if kv == 0:
                    k.mm(po[0:64, 0:NC], w2[:, 0, :], hid[:, 0:NC])
                    k.cp(kcT[0:64, 0:NC], po[0:64, 0:NC])
                else:
                    k.mm(po[0:NC, 0:64], hid[:, 0:NC], w2[:, 0, :])
                    k.cp(Vc[0:NC, 0:64], po[0:NC, 0:64])
                    k.memset(Vc[:, 64:65], 1.0)
                    k.dma("pool", Vc[:, 65:97], V(self.cov_d, self.cov_d.h[:, :]))
            k.scope_end()
            ces.close()
            self.pes, self.wbufs = save
            aes = ExitStack()
            k.scope_begin()
            save = self.pes, self.wbufs
            self.pes, self.wbufs = aes, {}
            sba = lambda nm, shape, dt=F32: k.sb(nm, shape, dt, es=aes)
            Dt = sba("nDt", [128, 4, 512])
            BCb = [sba("nBC%d" % i, [128, 512]) for i in range(2)]
            PTb = [sba("nPT%d" % i, [128, 512], BF16) for i in range(3)]
            QTb = [sba("nQT%d" % i, [128, 512], BF16) for i in range(2)]
            Ocb = [sba("nOc%d" % i, [128, 4, 97]) for i in range(2)]
            Osw = [sba("nOsw%d" % i, [128, 4, 2, 65]) for i in range(2)]
            seltb = [sba("nselc%d" % i, [128, 64]) for i in range(2)]
            Gb = [sba("nG%d" % i, [128, 12]) for i in range(2)]
            rzb = [sba("nrz%d" % i, [128, 12]) for i in range(2)]
            selT4b = [sba("nselT%d" % i, [128, 512], BF16) for i in range(2)]
            for t_ in selT4b:
                k.memset(t_[:, :], 0.0)
            imp, sc, m8 = sba("nimp", [128, 32]), sba("nsc", [128, 32]), sba("nm8", [128, 8])
            selb = sba("nselb", [128, 32], BF16)
            otok = sba("notok", [128, 256], BF16)
            coef = sba("ncoef", [128, 12])
            tmpo = sba("ntmpo", [128, 64])
            wgate = self.wload(l, "ngate", 0, "wng", 1)
            wq = self.wload(l, "nq", g, "wnq", 1)
            k.dma("sp", Dt[:, :, :], V(self.dtl_d, self.dtl_d.h[g].rearrange("j p n -> p j n")))
            for par in range(2):
                k.memset(QTb[par][64:128, :], 0.0)
                k.dma("pool", QTb[par][64:65, :], V(self.dtl_d, self.dtl_d.h[g, 3, 64:65, :]))
            for j in range(3):
                k.tt(Dt[:, j, :], Dt[:, j, :], Dt[:, 3, :], ALU.subtract)
            Awb = [self.psb[4], self.psb[6]]
            Asb = [self.psb[5], self.psb[7]]

            def prologue(qt):
                par = qt % 2
                qs = slice(qt * 128, qt * 128 + 128)
                BC, selt, QT, Oc, G, rz, selT4 = BCb[par], seltb[par], QTb[par], Ocb[par], Gb[par], rzb[par], selT4b[par]
                k.dma("sp", BC[:, :], V(self.bcb_d, self.bcb_d.h[g, qt]))
                k.dma("sp", selt[:, :], V(self.selc_d, self.selc_d.h[qt]))
                yield
                pq = self.ps()
                for hh in range(4):
                    for kc in range(KC):
                        k.mm(pq[0:64, hh * 128:(hh + 1) * 128], wq[:, kc, hh * 64:(hh + 1) * 64], self.uT[:, kc, qs],
                             start=(kc == 0), stop=(kc == KC - 1))
                    yield
                k.act(QT[0:64, :], pq[0:64, :], AF.Copy, scale=0.125)
                pg_ = self.ps()
                for kc in range(KC):
                    k.mm(pg_[:, 0:12], self.uT[:, kc, qs], wgate[:, kc, g * 12:(g + 1) * 12], start=(kc == 0), stop=(kc == KC - 1))
                k.act(G[:, :], pg_[:, 0:12], AF.Exp, scale=-1.0)
                k.ts(G[:, :], G[:, :], 1.0, ALU.add)
                k.recip(G[:, :], G[:, :])
                yield
                pl = self.ps()
                k.mm(pl[0:NC, :], kcT[:, 0:NC], QT[:, :])
                PT = PTb[2]
                k.tt(BC[0:NC, :], pl[0:NC, :], BC[0:NC, :], ALU.add)
                yield
                k.act(PT[0:NC, :], BC[0:NC, :], AF.Exp)
                yield
                po = self.ps()
                for hh in range(4):
                    k.mm(po[:, hh * 97:(hh + 1) * 97], PT[0:NC, hh * 128:(hh + 1) * 128], Vc[0:NC, :])
                yield
                k.cp(Oc[:, :, :], po[:, 0:388].rr("p (h n) -> p h n", n=97))
                k.ts(rz[:, 0:4], Oc[:, :, 64], 1e-30, ALU.max)
                k.recip(rz[:, 0:4], rz[:, 0:4])
                yield
                k.ts(imp[:, :], Oc[:, 0, 65:97], rz[:, 0:1], ALU.mult)
                for hh in range(1, 4):
                    k.stt(imp[:, :], Oc[:, hh, 65:97], rz[:, hh:hh + 1], imp[:, :], ALU.mult, ALU.add)
                yield
                k.tt(sc[:, :], imp[:, :], selt[:, 0:32], ALU.mult)
                k.tt(sc[:, :], sc[:, :], selt[:, 32:64], ALU.add)
                k.max8(m8[:, :], sc[:, :])
                yield
                k.ts(sc[:, :], sc[:, :], m8[:, 7:8], ALU.is_ge)
                k.ts(selb[:, :], sc[:, :], 30000.0, ALU.mult, -30000.0, ALU.add)
                yield
                pst = self.psbf()
                for hh in range(4):
                    k.tr(pst[0:32, hh * 128:(hh + 1) * 128], selb[:, :], self.ident_bf[:, :])
                k.cp(selT4[0:32, :], pst[0:32, 0:512])
                yield

            def epilogue(qt):
                par = qt % 2
                qs = slice(qt * 128, qt * 128 + 128)
                Oc, G, rz, OO = Ocb[par], Gb[par], rzb[par], Osw[par]
                k.cp(OO[:, :, 0, :], Awb[par][:, 0:260].rr("p (h n) -> p h n", n=65), eng="act")
                k.cp(OO[:, :, 1, :], Asb[par][:, 0:260].rr("p (h n) -> p h n", n=65), eng="dve")
                yield
                k.ts(rz[:, 4:8], OO[:, :, 1, 64], 1e-30, ALU.max)
                k.ts(rz[:, 8:12], OO[:, :, 0, 64], 1e-30, ALU.max)
                k.recip(rz[:, 4:12], rz[:, 4:12])
                G3 = G[:, :].rr("p (h j) -> p h j", j=3)
                for j in range(3):
                    k.tt(coef[:, j * 4:(j + 1) * 4], G3[:, :, j], rz[:, j * 4:(j + 1) * 4], ALU.mult)
                yield
                for hh in range(4):
                    k.ts(tmpo[:, 0:64], Oc[:, hh, 0:64], coef[:, hh:hh + 1], ALU.mult)
                    k.stt(tmpo[:, 0:64], OO[:, hh, 1, 0:64], coef[:, 4 + hh:5 + hh], tmpo[:, 0:64], ALU.mult, ALU.add)
                    k.stt(otok[:, hh * 64:(hh + 1) * 64], OO[:, hh, 0, 0:64], coef[:, 8 + hh:9 + hh], tmpo[:, 0:64], ALU.mult, ALU.add)
                    yield
                for j in range(2):
                    ptr = V(self.psb[2], self.psb[2].h[:, :].bitcast(BF16))
                    k.tr(ptr[:, 0:128], otok[:, j * 128:(j + 1) * 128], self.ident_bf[:, :])
                    k.cp(oT[:, g * 2 + j, qs], ptr[:, 0:128], eng=("act" if j else "dve"))
                yield

            def drain(gen):
                if gen is not None:
                    for _ in gen:
                        pass

            def step(gen):
                if gen is not None:
                    try:
                        next(gen)
                    except StopIteration:
                        return None
                return gen

            self.ps_fixed = 3
            drain(prologue(0))
            epi = None
            for qt in range(NT):
                par = qt % 2
                QT, selT4 = QTb[par], selT4b[par]
                pro = prologue(qt + 1) if qt + 1 < NT else None
                if self.flags.get("nopipe") or self.flags.get("noepi"):
                    drain(epi)
                    epi = None
                if self.flags.get("nopipe") or self.flags.get("nopro"):
                    drain(pro)
                    pro = None
                its = [("w", kt) for kt in (qt - 2, qt - 1, qt) if kt >= 0] + [("s", kt) for kt in range(qt + 1)]
                nw = len([1 for x in its if x[0] == "w"])
                pls = {}

                def logits(i):
                    kind, kt = its[i]
                    ks = slice(kt * 128, kt * 128 + 128)
                    pl = self.psb[i % 2]
                    if kind == "w":
                        k.mm(pl[:, :], KwT[:, ks], QT[:, :])
                    else:
                        k.mm(pl[:, :], KsT[:, ks], QT[:, :], start=True, stop=False)
                        k.mm(pl[:, :], Ex[:, ks], selT4[:, :], start=False, stop=True)
                    pls[i] = pl

                logits(0)
                for i, (kind, kt) in enumerate(its):
                    if i + 1 < len(its):
                        logits(i + 1)
                    pl = pls.pop(i)
                    dlt = qt - kt
                    PT = PTb[i % 2]
                    if dlt <= 1 or (kind == "w" and dlt == 2):
                        k.tt(pl[:, :], pl[:, :], Dt[:, dlt, :], ALU.add)
                        k.act(PT[:, :], pl[:, :], AF.Exp)
                    else:
                        k.act(PT[:, :], pl[:, :], AF.Exp)
                    for hh in range(4):
                        if kind == "w":
                            k.mm(Awb[par][:, hh * 65:(hh + 1) * 65], PT[:, hh * 128:(hh + 1) * 128], Vw[:, kt, :],
                                 start=(i == 0 and hh == 0), stop=(i == nw - 1), sgc=True)
                        else:
                            k.mm(Asb[par][:, hh * 65:(hh + 1) * 65], PT[:, hh * 128:(hh + 1) * 128], Vs[:, kt, :],
                                 start=(kt == 0 and hh == 0), stop=(kt == qt), sgc=True)
                    epi = step(epi)
                    pro = step(pro)
                drain(epi)
                drain(pro)
                epi = epilogue(qt)
            drain(epi)
            self.ps_fixed = None
            k.scope_end()
            aes.close()
            self.pes, self.wbufs = save
        self.psn = 6


    def rwkv(self, l, q, oT):
        k = self.k
        S = self.S
        BW = 256
        NCH = BW // 64
        T = lambda nm, dt=F32, w=BW: k.sb(nm, [128, w], dt, es=self.pes)
        mu = self.P(l, "rw_mu")
        LA = k.sb("rLA", [128, S], BF16, es=self.pes)
        LG = k.sb("rLG", [128, S], BF16, es=self.pes)
        msu, msl, idb = self.cst2[:, 0:256], self.cst2[:, 256:512], self.cst2[:, 512:768]
        cmk = self.cst[:, 384:384 + BW]
        blkf = self.cst[:, 256:384]
        negw = k.sb("rnegw", [128, 8], F32, es=self.pes)
        k.ts(negw[:, 0:4], self.P(l, "rw_w0"), -1.0, ALU.mult)
        k.ts(negw[:, 4:8], self.P(l, "rw_a0"), -1.0, ALU.mult)

        def mkshift(raw, tmp):
            def shift(ps_, out, mucol, carry, first):
                if first:
                    k.memset(raw[:, 0:1], 0.0)
                else:
                    k.cp(raw[:, 0:1], carry[:, 0:1])
                k.cp(raw[:, 1:BW + 1], ps_, eng="act")
                k.tt(tmp[:, :], raw[:, 0:BW], raw[:, 1:BW + 1], ALU.subtract, eng="pool")
                k.stt(out, tmp[:, :], mucol, raw[:, 1:BW + 1], ALU.mult, ALU.add)
                k.cp(carry[:, 0:1], raw[:, BW:BW + 1])
            return shift

        aes = ExitStack()
        k.scope_begin()
        save = self.pes, self.wbufs
        self.pes, self.wbufs = aes, {}
        wl_ = self.wload(l, "rwl", 0, "wrwl", 1)
        carA = k.sb("rcarA", [128, 2], F32, es=aes)
        xa = k.sb("rxa", [128, BW], F32, es=aes)
        rawA = k.sb("rrawA", [128, BW + 1], F32, es=aes)
        tmpA = k.sb("rtmpA", [128, BW], F32, es=aes)
        shiftA = mkshift(rawA, tmpA)
        for t0 in range(0, S, BW):
            ts_ = slice(t0, t0 + BW)
            pa, pg = self.ps(), self.ps()
            for kc in range(KC):
                k.mm(pa[:, 0:BW], wl_[:, kc, 0:128], self.uT[:, kc, ts_], start=(kc == 0), stop=(kc == KC - 1))
            for kc in range(KC):
                k.mm(pg[:, 0:BW], wl_[:, kc, 128:256], self.uT[:, kc, ts_], start=(kc == 0), stop=(kc == KC - 1))
            shiftA(pa[:, 0:BW], xa[:, :], mu[:, 12:13], carA[:, 0:1], t0 == 0)
            k.act(LA[0:64, ts_], xa[0:64, :], AF.Tanh)
            k.cp(LA[64:128, ts_], xa[64:128, :])
            shiftA(pg[:, 0:BW], xa[:, :], mu[:, 13:14], carA[:, 1:2], t0 == 0)
            k.act(xa[:, :], xa[:, :], AF.Exp, scale=-1.0)
            k.act(xa[:, :], xa[:, :], AF.Ln, bias=self.cst[:, 128:129])
            k.act(LG[:, ts_], xa[:, :], AF.Exp, scale=-1.0)
        k.scope_end()
        aes.close()
        self.pes, self.wbufs = save

        def stream(hp, sid):
            sfx = "_%d" % sid
            T2 = lambda nm, dt=F32, w=BW: k.sb(nm + sfx, [128, w], dt, es=self.pes)
            raw, tmp, ee = T2("rraw", F32, BW + 1), T2("rtmp"), T2("ree")
            car = T2("rcar", F32, 4)
            shift = mkshift(raw, tmp)
            r_, k_, v_, a_, kk, ka, cw, lw, rk, gg = (T2("rr"), T2("rk"), T2("rv"), T2("ra"), T2("rkk"), T2("rka"),
                                                       T2("rcw"), T2("rlw"), T2("rrk"), T2("rgg"))
            Tt, X, XT, U0, WT = T2("rTt"), T2("rX"), T2("rXT"), T2("rU0"), T2("rWT")
            vb, Bb, Kb, Ab = T2("rvb", BF16), T2("rBb", BF16), T2("rKb", BF16), T2("rAb", BF16)
            Vtok, Btok, Ktok, Atok = T2("rVtok", BF16), T2("rBtok", BF16), T2("rKtok", BF16), T2("rAtok", BF16)
            AakT, ArbT, ArkT, Ttb = T2("rAak", BF16), T2("rArb", BF16), T2("rArk", BF16), T2("rTtb", BF16)
            Ub = T2("rUb", BF16, 64)
            H = T2("rH", F32, 64)
            PC = T2("rPC", F32, NCH)
            pst = [3 * sid, 0]
            psY = self.psb[6 + sid]

            def PS():
                self.ps_stream = pst
                p = self.ps()
                self.ps_stream = None
                return p

            def PSBF():
                p = PS()
                return V(p, p.h[:, :].bitcast(BF16))

            w = self.wload(l, "rw", hp, "wrw" + sfx, 1)
            wAB = self.wload(l, "rwAB", hp, "wrAB" + sfx, 1)
            wgB = self.wload(l, "rwgB", hp, "wrgB" + sfx, 1)
            k.memset(H[:, :], 0.0)
            P4 = lambda nm: self.P(l, nm)[:, hp:hp + 1]
            units = [(slice(h * 64, h * 64 + 64), slice(c * 64, c * 64 + 64)) for h in range(2) for c in range(NCH)]
            for t0 in range(0, S, BW):
                ts_ = slice(t0, t0 + BW)
                for i, (dst, cc) in enumerate(((r_, 0), (k_, 1), (v_, 2))):
                    pp = PS()
                    for kc in range(KC):
                        k.mm(pp[:, 0:BW], w[:, kc, i * 128:(i + 1) * 128], self.uT[:, kc, ts_], start=(kc == 0), stop=(kc == KC - 1))
                    yield
                    shift(pp[:, 0:BW], dst[:, :], mu[:, i * 4 + hp:i * 4 + hp + 1], car[:, cc:cc + 1], t0 == 0)
                    yield
                pw, pa, pg = PS(), PS(), PS()
                k.mm(pw[:, 0:BW], wAB[0:64, 0, :], LA[0:64, ts_])
                k.mm(pa[:, 0:BW], wAB[64:128, 0, :], LA[64:128, ts_])
                k.mm(pg[:, 0:BW], wgB[:, 0, :], LG[:, ts_])
                yield
                k.act(lw[:, :], pw[:, 0:BW], AF.Exp, bias=negw[:, hp:hp + 1], scale=-1.0)
                k.act(a_[:, :], pa[:, 0:BW], AF.Exp, bias=negw[:, 4 + hp:5 + hp], scale=-1.0)
                k.cp(gg[:, :], pg[:, 0:BW])
                k.act(lw[:, :], lw[:, :], AF.Ln, bias=self.cst[:, 128:129])
                k.act(a_[:, :], a_[:, :], AF.Ln, bias=self.cst[:, 128:129])
                k.act(lw[:, :], lw[:, :], AF.Exp, scale=-1.0)
                k.act(a_[:, :], a_[:, :], AF.Exp, scale=-1.0)
                k.ts(lw[:, :], lw[:, :], -0.6065306597126334, ALU.mult)
                yield
                k.ts(kk[:, :], k_[:, :], P4("rw_kk"), ALU.mult)
                k.tt(tmp[:, :], kk[:, :], kk[:, :], ALU.mult, eng="pool")
                pn = PS()
                k.mm(pn[:, 0:BW], blkf, tmp[:, :])
                yield
                k.ts(tmp[:, :], pn[:, 0:BW], 1e-24, ALU.max)
                k.act(tmp[:, :], tmp[:, :], AF.Ln)
                k.act(tmp[:, :], tmp[:, :], AF.Exp, scale=-0.5)
                k.tt(kk[:, :], kk[:, :], tmp[:, :], ALU.mult)
                k.ts(tmp[:, :], a_[:, :], -1.0, ALU.add, P4("rw_ka"), ALU.mult)
                k.stt(k_[:, :], tmp[:, :], 1.0, k_[:, :], ALU.add, ALU.mult)
                k.stt(rk[:, :], r_[:, :], P4("rw_rk"), k_[:, :], ALU.mult, ALU.mult)
                k.tt(ka[:, :], kk[:, :], a_[:, :], ALU.mult, eng="pool")
                k.cp(vb[:, :], v_[:, :], eng="pool")
                yield
                k.scan(cw[:, :], self.cmask[:, ts_], lw[:, :], 0.0, ALU.mult, ALU.add)
                cw3 = cw[:, :].rr("p (c t) -> p c t", t=64)
                cwend = V(cw, cw3.ap[:, :, 63:64].broadcast_to([128, NCH, 64]))
                k.act(PC[:, :], cw3[:, :, 63], AF.Exp)
                k.tt(lw[:, :], cw[:, :], lw[:, :], ALU.subtract)
                k.act(ee[:, :], lw[:, :], AF.Exp)
                k.stt(kk[:, :], kk[:, :], -1.0, ee[:, :], ALU.mult, ALU.mult)
                k.cp(Ab[:, :], kk[:, :], eng="pool")
                yield
                k.tt(tmp[:, :].rr("p (c t) -> p c t", t=64), cwend, cw3, ALU.subtract)
                k.act(ee[:, :], tmp[:, :], AF.Exp)
                k.tt(Bb[:, :], ka[:, :], ee[:, :], ALU.mult, eng="pool")
                k.tt(Kb[:, :], k_[:, :], ee[:, :], ALU.mult)
                yield
                k.act(ee[:, :], cw[:, :], AF.Exp, scale=-1.0)
                k.tt(ka[:, :], ka[:, :], ee[:, :], ALU.mult, eng="pool")
                k.tt(k_[:, :], k_[:, :], ee[:, :], ALU.mult)
                k.act(ee[:, :], cw[:, :], AF.Exp)
                k.tt(r_[:, :], r_[:, :], ee[:, :], ALU.mult)
                yield
                for src, dstt in ((vb, Vtok), (Bb, Btok), (Kb, Ktok), (Ab, Atok)):
                    pt = PSBF()
                    for hs, cs in units:
                        k.tr(pt[hs, cs], src[hs, cs], self.ident_bf[hs, hs])
                    yield
                    k.cp(dstt[:, :], pt[:, 0:BW], eng=("act" if src in (Bb, Ab) else "dve"))
                pN, pNT, pAk = PS(), PS(), PS()
                for hs, cs in units:
                    k.mm(pN[hs, cs], ka[hs, cs], kk[hs, cs])
                    k.mm(pNT[hs, cs], kk[hs, cs], ka[hs, cs])
                    k.mm(pAk[hs, cs], k_[hs, cs], kk[hs, cs])
                yield
                k.tt(X[:, :], pN[:, 0:BW], msu, ALU.mult)
                k.tt(XT[:, :], pNT[:, 0:BW], msl, ALU.mult)
                k.tt(AakT[:, :], pAk[:, 0:BW], msu, ALU.mult)
                k.tt(Tt[:, :], X[:, :], idb, ALU.add, eng="pool")
                yield
                pRb, pRk = PS(), PS()
                for hs, cs in units:
                    k.mm(pRb[hs, cs], ka[hs, cs], r_[hs, cs])
                    k.mm(pRk[hs, cs], k_[hs, cs], r_[hs, cs])
                yield
                k.tt(ArbT[:, :], pRb[:, 0:BW], cmk, ALU.mult)
                k.tt(ArkT[:, :], pRk[:, 0:BW], cmk, ALU.mult)
                for j in range(1, 6):
                    pX, pXT = PS(), PS()
                    for hs, cs in units:
                        if j < 5:
                            k.mm(pX[hs, cs], XT[hs, cs], X[hs, cs])
                        k.mm(pXT[hs, cs], X[hs, cs], XT[hs, cs])
                    yield
                    if j < 5:
                        k.cp(X[:, :], pX[:, 0:BW])
                    k.cp(XT[:, :], pXT[:, 0:BW], eng="act")
                    pT = PS()
                    for hs, cs in units:
                        k.mm(pT[hs, cs], XT[hs, cs], Tt[hs, cs])
                    yield
                    k.tt(Tt[:, :], Tt[:, :], pT[:, 0:BW], ALU.add)
                k.cp(Ttb[:, :], Tt[:, :], eng="act")
                pX1 = PS()
                for hs, cs in units:
                    k.mm(pX1[hs, cs], AakT[hs, cs], Vtok[hs, cs])
                yield
                k.cp(tmp[:, :], pX1[:, 0:BW])
                pU0, pWT = PS(), PS()
                for hs, cs in units:
                    k.mm(pU0[hs, cs], Tt[hs, cs], tmp[hs, cs])
                    k.mm(pWT[hs, cs], Atok[hs, cs], Ttb[hs, cs])
                yield
                k.cp(U0[:, :], pU0[:, 0:BW])
                k.cp(WT[:, :], pWT[:, 0:BW], eng="act")
                for c in range(NCH):
                    cs = slice(c * 64, c * 64 + 64)
                    psU = PS()
                    for h in range(2):
                        hs = slice(h * 64, h * 64 + 64)
                        k.mm(psU[hs, 0:64], WT[hs, cs], H[hs, :])
                    yield
                    k.tt(Ub[:, :], psU[:, 0:64], U0[:, cs], ALU.add)
                    psH = PS()
                    for h in range(2):
                        hs = slice(h * 64, h * 64 + 64)
                        k.mm(psY[hs, cs], H[hs, :], r_[hs, cs], start=True, stop=False)
                        k.mm(psY[hs, cs], Ub[hs, :], ArbT[hs, cs], start=False, stop=False)
                        k.mm(psY[hs, cs], Vtok[hs, cs], ArkT[hs, cs], start=False, stop=True)
                        k.mm(psH[hs, 0:64], Btok[hs, cs], Ub[hs, :], start=True, stop=False)
                        k.mm(psH[hs, 0:64], Ktok[hs, cs], Vtok[hs, cs], start=False, stop=True)
                    yield
                    k.stt(H[:, :], H[:, :], PC[:, c:c + 1], psH[:, 0:64], ALU.mult, ALU.add)
                y = X
                k.cp(y[:, :], psY[:, 0:BW])
                pm = PS()
                k.mm(pm[:, 0:BW], blkf, y[:, :])
                yield
                k.stt(y[:, :], pm[:, 0:BW], -1.0 / 64, y[:, :], ALU.mult, ALU.add)
                k.tt(tmp[:, :], y[:, :], y[:, :], ALU.mult, eng="pool")
                pv_ = PS()
                k.mm(pv_[:, 0:BW], blkf, tmp[:, :])
                pb_ = PS()
                k.mm(pb_[:, 0:BW], blkf, rk[:, :])
                yield
                k.act(tmp[:, :], pv_[:, 0:BW], AF.Ln, bias=self.cst[:, 131:132], scale=1.0 / 64)
                k.act(tmp[:, :], tmp[:, :], AF.Exp, scale=-0.5)
                k.tt(y[:, :], y[:, :], tmp[:, :], ALU.mult)
                k.ts(y[:, :], y[:, :], P4("rw_ln_w"), ALU.mult, P4("rw_ln_b"), ALU.add)
                k.tt(tmp[:, :], pb_[:, 0:BW], v_[:, :], ALU.mult)
                k.tt(y[:, :], y[:, :], tmp[:, :], ALU.add)
                k.tt(oT[:, hp, ts_], y[:, :], gg[:, :], ALU.mult)
                yield

        for pair in ((0, 1), (2, 3)):
            pes_ = ExitStack()
            k.scope_begin()
            save = self.pes, self.wbufs
            self.pes, self.wbufs = pes_, {}
            gens = [stream(pair[0], 0), stream(pair[1], 1)]
            if self.flags.get("rw1"):
                for g_ in gens:
                    for _ in g_:
                        pass
            else:
                alive = list(gens)
                while alive:
                    for g_ in list(alive):
                        try:
                            next(g_)
                        except StopIteration:
                            alive.remove(g_)
            k.scope_end()
            pes_.close()
            self.pes, self.wbufs = save

    def build(self):
        k = self.k
        S, NL = self.S, self.NL
        self.pes = k.es
        k.dma("sp", self.par[:, :, :], V(self.par_d, self.par_d.h.rearrange("l p n -> p l n")))
        k.dma("sp", self.cst[:, :], V(self.cst_d, self.cst_d.h[:, 0:896]))
        k.dma("pool", self.cmask[:, :], V(self.cst_d, self.cst_d.h[:, 896:896 + S]))
        k.dma("sp", self.cst2[:, :], V(self.cst_d, self.cst_d.h[:, 896 + S:896 + S + 768]))
        k.dma("pool", self.blk_bf[:, :], V(self.cst_d, self.cst_d.h[:, 256:384]))
        k.dma("pool", self.ident_bf[:, :], V(self.cst_d, self.cst_d.h[:, 0:128]))
        k.memset(self.ones_bf[:, :], 1.0)
        self.prep_lb()
        for l in range(NL):
            self.cast_weights(l)
        TT = min(1024, S)
        for q in range(self.NSEQ):
            src = self.xT.h[q].rearrange("(c p) t -> p c t", p=128)
            for c in range(KC):
                k.dma("sp", self.hT[:, c, :], V(self.xT, src[:, c, :]))
            for l in range(NL):
                self.phase_begin()
                self.ffn_all(l, "ffn1", "ffn1_norm", TT)
                self.phase_end()
                self.mixers(l, q)
                self.phase_begin()
                self.ffn_all(l, "ffn2", "ffn2_norm", TT)
                self.phase_end()
                for t0 in range(0, S, TT):
                    self.phase_begin()
                    self.ple(l, q, t0, TT)
                    self.phase_end()
            self.phase_begin()
            of = k.sb("of", [128, KC, 512], F32, es=self.pes)
            dst = self.outT.h[q].rearrange("(c p) t -> p c t", p=128)
            for t0 in range(0, S, 512):
                self.rmsnorm(self.P(0, "final_norm"), t0, 512, of, 0)
                k.dma("sp", V(self.outT, dst[:, :, t0:t0 + 512]), of[:, :, :])
            self.phase_end()
        k._wait("sp", self.outT.w)


def make_consts(S):
    c = np.zeros((128, 896 + S + 768), np.float32)
    p_ = np.arange(128)[:, None] % 64
    t_ = np.arange(256)[None, :] % 64
    c[:, 896 + S:896 + S + 256] = (p_ < t_)
    c[:, 896 + S + 256:896 + S + 512] = (t_ < p_)
    c[:, 896 + S + 512:896 + S + 768] = (t_ == p_)
    c[:, 0:128] = np.eye(128, dtype=np.float32)
    c[:, 128] = 1.0
    c[:, 130] = 1e-6
    c[:, 131] = 64e-5
    p = np.arange(128)
    c[:, 256:384] = (p[:, None] // 64 == p[None, :] // 64)
    t = np.arange(512)
    c[:, 384:896] = ((p[:, None] % 64) <= (t[None, :] % 64))
    tt = np.arange(S)
    c[:, 896:896 + S] = (tt % 64 != 0)[None, :]
    return c


def t5_bucket_np(n):
    n = np.maximum(n, 0)
    nf = np.maximum(n, 1).astype(np.float32)
    large = 16 + (np.log(nf / np.float32(16)) / np.float32(np.log(8.0)) * np.float32(16)).astype(np.int32)
    large = np.minimum(large, 31)
    return np.where(n < 16, n, large)


NEG = -30000.0


def nsa_consts(rel_bias, S):
    NT = S // 128
    rb = np.asarray(rel_bias, np.float32)
    kk_ = np.arange(128)[:, None]
    qq = np.arange(128)[None, :]
    dtl = np.zeros((2, 4, 128, 4, 128), np.float32)
    bcb = np.full((2, NT, 128, 4, 128), NEG, np.float32)
    n = np.arange(128)[:, None]
    for g in range(2):
        for hh in range(4):
            hd = g * 4 + hh
            d0 = qq - kk_
            dtl[g, 0, :, hh, :] = np.where(d0 >= 0, rb[t5_bucket_np(d0), hd], NEG)
            d1 = 128 + qq - kk_
            dtl[g, 1, :, hh, :] = rb[t5_bucket_np(d1), hd]
            d2 = 256 + qq - kk_
            dtl[g, 2, :, hh, :] = np.where(d2 < 256, rb[t5_bucket_np(d2), hd], NEG)
            dtl[g, 3, :, hh, :] = rb[31, hd]
            for qt in range(NT):
                dc = (128 * qt + qq) - (16 * n + 31)
                bcb[g, qt, :, hh, :] = np.where(dc >= 0, rb[t5_bucket_np(dc), hd], NEG)
    selc = np.zeros((NT, 128, 64), np.float32)
    m = np.arange(32)[None, :]
    for qt in range(NT):
        cur = ((128 * qt + np.arange(128)) // 64)[:, None]
        allowed = m <= cur
        forced = (m == 0) | (m == cur) | (m == cur - 1)
        selc[qt, :, 0:32] = allowed
        selc[qt, :, 32:64] = np.where(forced, 1e9, np.where(allowed, 0.0, -1e9))
    exc = (np.arange(S)[None, :] // 64 == np.arange(32)[:, None]).astype(np.float32)
    nn = np.arange(128)[:, None]
    mm_ = np.arange(32)[None, :]
    cov = ((16 * nn <= 64 * mm_ + 63) & (16 * nn + 31 >= 64 * mm_)).astype(np.float32)
    return (dtl.reshape(2, 4, 128, 512), bcb.reshape(2, NT, 128, 512), selc, exc, cov)


_CACHE = {}


def run(inputs, S, NL, B, flags, ncores=8):
    NSEQ = B // ncores
    packs = [pack_layer(inputs, l) for l in range(NL)]
    pars = [pack_params(inputs, l) for l in range(NL)]
    wsize = packs[0][0].size
    npar = pars[0][0].shape[1]
    prog = Prog(S, NL, NSEQ, [p[1] for p in packs], pars[0][1], wsize, npar, flags)
    x = np.asarray(inputs["x"], np.float32)
    p = np.asarray(inputs["p"], np.float32)
    xT = np.ascontiguousarray(x.transpose(0, 2, 1))
    pT = np.ascontiguousarray(p.transpose(0, 1, 3, 2))
    par = np.stack([pp[0] for pp in pars])
    cst = make_consts(S)
    dtl, bcb, selc, exc, cov = nsa_consts(inputs["rel_bias"], S)
    in_maps = []
    for c in range(ncores):
        m = {"xT": xT[c * NSEQ:(c + 1) * NSEQ], "pT": np.ascontiguousarray(pT[:NL, c * NSEQ:(c + 1) * NSEQ]),
             "par": par, "cst": cst, "dtl": dtl, "bcb": bcb, "selc": selc, "exc": exc, "cov": cov}
        for l in range(NL):
            m["wl%d" % l] = packs[l][0].reshape(-1, 1024)
        in_maps.append(m)
    res = run_bass_kernel_spmd(prog.k.nc, in_maps, core_ids=list(range(ncores)))
    outT = np.concatenate([r["outT"] for r in res.results], axis=0)
    return np.ascontiguousarray(outT.transpose(0, 2, 1)), prog


def kernel(**inputs):
    inputs = {k_: np.asarray(v) for k_, v in inputs.items()}
    B, S, _ = inputs["x"].shape
    out, _ = run(inputs, S, 4, B, {"mix0": True, "mix1": True, "mix2": True})
    return out.astype(np.float32)
```

```python
import numpy as np
from contextlib import ExitStack
import concourse.bass as bass
import concourse.mybir as mybir
from concourse.bass_utils import run_bass_kernel_spmd

F32 = mybir.dt.float32
BF16 = mybir.dt.bfloat16
I32 = mybir.dt.int32
U8 = mybir.dt.uint8
AF = mybir.ActivationFunctionType
ALU = mybir.AluOpType
AX = mybir.AxisListType

D = 1024
KC = 8
DFF = 2816
NJ = 22
PLE = 256
INCOLS = 8216
SEM_LIMIT = 30000


class Buf:
    def __init__(self, name, h, kind):
        self.name = name
        self.h = h
        self.kind = kind
        self.w = None
        self.w_eng = None
        self.r = {}
        self.dsem = None
        self.dcnt = 0

    def __getitem__(self, idx):
        return V(self, self.h[idx])


class V:
    def __init__(self, buf, ap):
        self.buf = buf
        self.ap = ap

    def __getitem__(self, idx):
        return V(self.buf, self.ap[idx])

    def rr(self, pat, **kw):
        return V(self.buf, self.ap.rearrange(pat, **kw))


class KB:
    def __init__(self):
        self.nc = bass.Bass("TRN2", target_bir_lowering=False)
        nc = self.nc
        self.es = ExitStack()
        self.E = {"pe": nc.tensor, "dve": nc.vector, "act": nc.scalar, "pool": nc.gpsimd, "sp": nc.sync}
        self.sems = []
        self.cur = {}
        self.cnt = {}
        self.seen = {e: {} for e in self.E}
        for e in self.E:
            self._newsem(e)
        self.nins = 0
        self.free_dsems = []
        self.semcnt = {}
        self.scopes = []

    def get_dsem(self, name):
        if self.free_dsems:
            return self.free_dsems.pop()
        s_ = self._alloc_sem(name)
        self.semcnt[s_] = 0
        return s_

    def _alloc_sem(self, name):
        s = self.es.enter_context(self.nc.semaphore(name))
        self.sems.append(s)
        return len(self.sems) - 1

    def _newsem(self, e):
        self.cur[e] = self._alloc_sem("s_%s_%d" % (e, len(self.sems)))
        self.cnt[e] = 0

    def sb(self, name, shape, dt, es=None):
        self.uid = getattr(self, "uid", 0) + 1
        name = "%s_%d" % (name, self.uid)
        h = (es or self.es).enter_context(self.nc.sbuf_tensor(name, list(shape), dt))
        b = Buf(name, h, "sb")
        if self.scopes:
            self.scopes[-1].append(b)
        return b

    def scope_begin(self):
        self.scopes.append([])

    def scope_end(self):
        bufs = self.scopes.pop()
        self.barrier(bufs)
        for b in bufs:
            if b.dsem is not None:
                self.free_dsems.append(b.dsem)
                b.dsem = None

    def ps(self, name, shape, dt):
        h = self.es.enter_context(self.nc.psum_tensor(name, list(shape), dt))
        return Buf(name, h, "ps")

    def dram(self, name, shape, dt, kind):
        h = self.nc.dram_tensor(name, list(shape), dt, kind=kind)
        b = Buf(name, h.ap(), "dram")
        return b

    def _wait(self, e, ev):
        if ev is None:
            return
        s, v = ev
        if self.seen[e].get(s, 0) >= v:
            return
        self.E[e].wait_ge(self.sems[s], v)
        self.seen[e][s] = v

    def _deps(self, e, reads, writes):
        for v in reads:
            b = v.buf
            if b.w is not None:
                self._wait(e, b.w)
        for v in writes:
            b = v.buf
            if b.w is not None and not (e == "pe" and b.w_eng == "pe"):
                if not (b.w_eng == e and e != "dma"):
                    self._wait(e, b.w)
            for s, val in b.r.items():
                if s == self.cur.get(e, -1):
                    continue
                self._wait(e, (s, val))

    def emit(self, e, fn, reads, writes):
        reads = [v for v in reads if isinstance(v, V)]
        self._deps(e, reads, writes)
        ins = fn()
        if self.cnt[e] >= SEM_LIMIT:
            self._newsem(e)
        self.cnt[e] += 1
        ins.then_inc(self.sems[self.cur[e]], 1)
        ev = (self.cur[e], self.cnt[e])
        for v in reads:
            b = v.buf
            if b.r.get(ev[0], 0) < ev[1]:
                b.r[ev[0]] = ev[1]
        for v in writes:
            b = v.buf
            b.w = ev
            b.w_eng = e
            b.r = {}
        self.nins += 1
        return ins

    def dma(self, q, out, in_, track=None, **kw):
        tb = track or (out.buf if out.buf.kind != "dramin" else in_.buf)
        if tb.dsem is None:
            tb.dsem = self.get_dsem("d_" + tb.name)
        reads = [in_] if in_.buf.kind != "dramin" else []
        writes = [out]
        for v in reads:
            if v.buf.w is not None:
                self._wait(q, v.buf.w)
        for v in writes:
            if v.buf.w is not None:
                self._wait(q, v.buf.w)
            for s, val in v.buf.r.items():
                self._wait(q, (s, val))
        ins = self.E[q].dma_start(out=out.ap, in_=in_.ap, **kw)
        ins.then_inc(self.sems[tb.dsem], 16)
        self.semcnt[tb.dsem] += 16
        ev = (tb.dsem, self.semcnt[tb.dsem])
        for v in reads:
            v.buf.r[ev[0]] = ev[1]
        out.buf.w = ev
        out.buf.w_eng = "dma"
        out.buf.r = {}
        self.nins += 1

    def barrier(self, bufs=()):
        evs = [(self.cur[e], self.cnt[e]) for e in self.E if self.cnt[e] > 0]
        for b in bufs:
            if b.w is not None:
                evs.append(b.w)
            evs.extend(b.r.items())
        for e in self.E:
            for ev in evs:
                if ev[0] == self.cur[e]:
                    continue
                self._wait(e, ev)

    def mm(self, out, lhsT, rhs, start=True, stop=True, sgc=False):
        return self.emit("pe", lambda: self.nc.tensor.matmul(out.ap, lhsT.ap, rhs.ap, start=start, stop=stop,
                                                             skip_group_check=sgc), [lhsT, rhs], [out])

    def tr(self, out, in_, ident):
        return self.emit("pe", lambda: self.nc.tensor.transpose(out.ap, in_.ap, ident.ap), [in_, ident], [out])

    def act(self, out, in_, func, bias=None, scale=None, accum=None):
        kw = {}
        rd = [in_]
        if bias is not None:
            kw["bias"] = bias.ap if isinstance(bias, V) else bias
            rd.append(bias)
        if scale is not None:
            kw["scale"] = scale.ap if isinstance(scale, V) else scale
            rd.append(scale)
        wr = [out]
        if accum is not None:
            kw["accum_out"] = accum.ap
            wr.append(accum)
        return self.emit("act", lambda: self.nc.scalar.activation(out=out.ap, in_=in_.ap, func=func, **kw), rd, wr)

    def ts(self, out, in0, s1, op0, s2=None, op1=None, eng="dve"):
        a1 = s1.ap if isinstance(s1, V) else s1
        a2 = s2.ap if isinstance(s2, V) else s2
        kw = {}
        if op1 is not None:
            kw["op1"] = op1
        E = self.E[eng]
        return self.emit(eng, lambda: E.tensor_scalar(out=out.ap, in0=in0.ap, scalar1=a1, scalar2=a2, op0=op0, **kw),
                         [in0, s1, s2], [out])

    def tt(self, out, in0, in1, op, eng="dve"):
        E = self.E[eng]
        return self.emit(eng, lambda: E.tensor_tensor(out=out.ap, in0=in0.ap, in1=in1.ap, op=op), [in0, in1], [out])

    def stt(self, out, in0, scalar, in1, op0, op1):
        a = scalar.ap if isinstance(scalar, V) else scalar
        return self.emit("dve", lambda: self.nc.vector.scalar_tensor_tensor(
            out=out.ap, in0=in0.ap, scalar=a, in1=in1.ap, op0=op0, op1=op1), [in0, scalar, in1], [out])

    def cp(self, out, in_, eng="dve"):
        if eng == "act":
            return self.emit("act", lambda: self.nc.scalar.copy(out=out.ap, in_=in_.ap), [in_], [out])
        E = self.E[eng]
        return self.emit(eng, lambda: E.tensor_copy(out=out.ap, in_=in_.ap), [in_], [out])

    def scan(self, out, d0, d1, init, op0, op1):
        a = init.ap if isinstance(init, V) else init
        return self.emit("dve", lambda: self.nc.vector.tensor_tensor_scan(
            out=out.ap, data0=d0.ap, data1=d1.ap, initial=a, op0=op0, op1=op1), [d0, d1, init], [out])

    def memset(self, out, val, eng="dve"):
        E = self.E[eng]
        return self.emit(eng, lambda: E.memset(out.ap, val), [], [out])

    def recip(self, out, in_):
        return self.emit("dve", lambda: self.nc.vector.reciprocal(out=out.ap, in_=in_.ap), [in_], [out])

    def max8(self, out, in_):
        return self.emit("dve", lambda: self.nc.vector.max(out=out.ap, in_=in_.ap), [in_], [out])

    def red(self, out, in_, op, axis=AX.X):
        return self.emit("dve", lambda: self.nc.vector.tensor_reduce(out=out.ap, in_=in_.ap, axis=axis, op=op),
                         [in_], [out])

    def cpred(self, out, mask, data):
        return self.emit("dve", lambda: self.nc.vector.copy_predicated(out=out.ap, mask=mask.ap, data=data.ap),
                         [mask, data, out], [out])


def blockify(W, colsets):
    K = W.shape[0]
    kc = K // 128
    out = []
    for cols in colsets:
        blk = W[:, cols]
        blk = blk.reshape(kc, 128, len(cols)).transpose(1, 0, 2)
        out.append(np.ascontiguousarray(blk).reshape(-1))
    return out


class Packer:
    def __init__(self):
        self.parts = []
        self.off = 0
        self.idx = {}

    def add(self, name, W, colsets):
        K = W.shape[0]
        blks = blockify(W, colsets)
        lst = []
        for b, cols in zip(blks, colsets):
            lst.append((self.off, K // 128, len(cols)))
            self.parts.append(b)
            self.off += b.size
        self.idx[name] = lst

    def finish(self, mult=16384):
        pad = (-self.off) % mult
        if pad:
            self.parts.append(np.zeros(pad, np.float32))
            self.off += pad
        return np.concatenate(self.parts)


def ar(a, n):
    return np.arange(a, a + n)


HG_OFF = 0
NSA_Q_OFF = 2048
NSA_KV_OFF = 2560
NSA_GATE_OFF = 3328
RW_OFF = 3352
MG_OFF = 5144


def pack_layer(inp, l):
    pk = Packer()
    for nm in ("ffn1", "ffn2"):
        wgu = inp[nm + "_wgu"][l]
        pk.add(nm + "_gu", wgu, [np.concatenate([ar(j * 128, 128), ar(DFF + j * 128, 128)]) for j in range(NJ)])
        wd = inp[nm + "_wd"][l]
        pk.add(nm + "_d", wd, [ar(m * 128, 128) for m in range(8)])
    w_in = inp["w_in"][l]
    pk.add("mg", w_in, [ar(MG_OFF + b * D + m * 128, 128) for b in range(3) for m in range(8)])
    wb = inp["w_branch"][l]
    for b in range(3):
        pk.add("br%d" % b, wb[b], [ar(m * 128, 128) for m in range(8)])
    pk.add("wout", inp["w_out"][l], [ar(m * 128, 128) for m in range(8)])
    pk.add("pgw", inp["ple_gate_w"][l], [ar(m * 128, 128) for m in range(8)])
    pk.add("plw", inp["ple_w"][l], [ar(m * 128, 128) for m in range(8)])
    pk.add("hg", w_in, [np.concatenate([ar(HG_OFF + t * 512 + hp * 128, 128) for t in range(4)]) for hp in range(4)])
    pk.add("nq", w_in, [ar(NSA_Q_OFF + g * 256, 256) for g in range(2)])
    pk.add("nkv", w_in, [np.concatenate([ar(NSA_KV_OFF + t * 128 + g * 64, 64) for t in range(6)]) for g in range(2)])
    pk.add("ngate", w_in, [ar(NSA_GATE_OFF, 24)])
    for kv in range(2):
        w1 = inp["cmp_w1"][l][kv]
        w1r = np.zeros((128, 4096), np.float32)
        w1r[0:64] = w1.reshape(32, 64, 128).transpose(1, 0, 2).reshape(64, 4096)
        pk.add("cw1_%d" % kv, w1r, [ar(0, 4096)])
        pk.add("cw1c_%d" % kv, w1, [ar(0, 128)])
        pk.add("cw2_%d" % kv, inp["cmp_w2"][l][kv], [ar(0, 64)])
    RW = RW_OFF
    pk.add("rwl", w_in, [ar(RW + 1536, 256)])
    pk.add("rw", w_in, [np.concatenate([ar(RW + t * 512 + hp * 128, 128) for t in range(3)]) for hp in range(4)])
    AB = np.concatenate([inp["rw_wB"][l], inp["rw_aB"][l]], axis=0)
    pk.add("rwAB", AB, [ar(hp * 128, 128) for hp in range(4)])
    pk.add("rwgB", inp["rw_gB"][l], [ar(hp * 128, 128) for hp in range(4)])
    flat = pk.finish()
    return flat, pk.idx


def pack_params(inp, l):
    cols = []
    idx = {}

    def add(name, vec):
        v = np.asarray(vec, np.float32).reshape(-1, 128).T
        idx[name] = (sum(c.shape[1] for c in cols), v.shape[1])
        cols.append(v)

    add("ffn1_norm", inp["ffn1_norm"][l])
    add("mix_norm", inp["mix_norm"][l])
    add("ffn2_norm", inp["ffn2_norm"][l])
    add("ple_norm", inp["ple_norm"][l])
    add("final_norm", inp["final_norm"])
    add("hg_norm", inp["hg_norm"][l])
    for j in range(4):
        add("hg_lb%d" % j, inp["hg_lb"][j])
    add("cmp_pe0", inp["cmp_pe"][l][0].reshape(-1))
    add("cmp_pe1", inp["cmp_pe"][l][1].reshape(-1))
    add("rw_mu", inp["rw_mu"][l])
    for nm in ("rw_w0", "rw_a0", "rw_kk", "rw_ka", "rw_ln_w", "rw_ln_b"):
        add(nm, inp[nm][l])
    add("rw_rk", inp["rw_rk"][l].reshape(-1))
    return np.ascontiguousarray(np.concatenate(cols, axis=1)), idx


class Prog:
    def __init__(self, S, NL, NSEQ, widx, pidx, wsize, npar, flags):
        self.S, self.NL, self.NSEQ = S, NL, NSEQ
        self.widx, self.pidx = widx, pidx
        self.flags = flags
        k = self.k = KB()
        nc = k.nc
        self.NT = S // 512
        self.xT = k.dram("xT", [NSEQ, D, S], F32, "ExternalInput")
        self.xT.kind = "dramin"
        self.pT = k.dram("pT", [NL, NSEQ, PLE, S], F32, "ExternalInput")
        self.pT.kind = "dramin"
        self.wl = []
        self.ws = []
        for l in range(NL):
            b = k.dram("wl%d" % l, [wsize // 1024, 1024], F32, "ExternalInput")
            b.kind = "dramin"
            self.wl.append(b)
            self.ws.append(k.dram("ws%d" % l, [wsize // 1024, 1024], BF16, "Internal"))
        self.par_d = k.dram("par", [NL, 128, npar], F32, "ExternalInput")
        self.par_d.kind = "dramin"
        self.NCST = 896 + S + 768
        self.cst_d = k.dram("cst", [128, self.NCST], F32, "ExternalInput")
        self.cst_d.kind = "dramin"
        self.outT = k.dram("outT", [NSEQ, D, S], F32, "ExternalOutput")
        NTq = S // 128
        self.dtl_d = k.dram("dtl", [2, 4, 128, 512], F32, "ExternalInput")
        self.bcb_d = k.dram("bcb", [2, NTq, 128, 512], F32, "ExternalInput")
        self.selc_d = k.dram("selc", [NTq, 128, 64], F32, "ExternalInput")
        self.exc_d = k.dram("exc", [32, S], F32, "ExternalInput")
        self.cov_d = k.dram("cov", [128, 32], F32, "ExternalInput")
        for b_ in (self.dtl_d, self.bcb_d, self.selc_d, self.exc_d, self.cov_d):
            b_.kind = "dramin"
        self.hT = k.sb("hT", [128, KC, S], F32)
        self.uT = k.sb("uT", [128, KC, S], BF16)
        self.par = k.sb("par_sb", [128, NL, npar], F32)
        self.cst = k.sb("cst_sb", [128, 896], F32)
        self.cst2 = k.sb("cst2_sb", [128, 768], F32)
        self.ones_bf = k.sb("ones_bf", [128, 128], BF16)
        self.blk_bf = k.sb("blk_bf", [128, 128], BF16)
        self.ident_bf = k.sb("ident_bf", [128, 128], BF16)
        self.cmask = k.sb("cmask", [128, S], BF16)
        self.oml = k.sb("oml", [128, 4, NL], F32)
        self.psb = [k.ps("ps%d" % i, [128, 512], F32) for i in range(8)]
        self.psi = 0
        self.wbufs = {}
        self.build()

    def ps(self):
        if getattr(self, "ps_fixed", None) is not None:
            return self.psb[self.ps_fixed]
        if getattr(self, "ps_stream", None) is not None:
            st = self.ps_stream
            st[1] += 1
            return self.psb[st[0] + st[1] % 3]
        p = self.psb[self.psi % getattr(self, "psn", 6)]
        self.psi += 1
        return p

    def P(self, l, name):
        o, n = self.pidx[name]
        return self.par[:, l, o:o + n]

    def wbuf(self, tag, shape, nbuf=3):
        if tag not in self.wbufs:
            self.wbufs[tag] = [[self.k.sb("w_%s_%d" % (tag, i), shape, BF16, es=self.pes) for i in range(nbuf)], 0]
        lst = self.wbufs[tag]
        b = lst[0][lst[1] % len(lst[0])]
        lst[1] += 1
        return b

    def wload(self, l, name, j, tag, nbuf=3):
        off, kc, nb = self.widx[l][name][j]
        b = self.wbuf(tag, [128, kc, nb], nbuf)
        flat = self.ws[l].h.rearrange("a b -> (a b)")
        src = flat[off:off + 128 * kc * nb].rearrange("(p k n) -> p k n", p=128, k=kc)
        if l == 0:
            n = 128 * kc * nb
            ra, rb = off // 1024, (off + n - 1) // 1024
            cbs = [cb for r0, cb in self.wchunks.items() if r0 <= rb and r0 + 2048 > ra]
            for cb in cbs[:-1]:
                self.k._wait("sp", cb.w)
            self.k.dma("sp", b[:, :, :], V(cbs[-1], src))
        else:
            self.k.dma("sp", b[:, :, :], V(self.ws[l], src))
        return b

    def cast_weights(self, l):
        k = self.k
        R = self.wl[l].h.shape[0]
        step = 2048
        self.wchunks = getattr(self, "wchunks", {})
        for r0 in range(0, R, step):
            r1 = min(R, r0 + step)
            if l == 0:
                cb = Buf("ws0_c%d" % r0, self.ws[l].h, "dram")
                self.wchunks[r0] = cb
                k.dma("pool", V(cb, self.ws[l].h[r0:r1, :]), V(self.wl[l], self.wl[l].h[r0:r1, :]))
            else:
                k.dma("pool", V(self.ws[l], self.ws[l].h[r0:r1, :]), V(self.wl[l], self.wl[l].h[r0:r1, :]))

    def rmsnorm(self, g, t0, n, out, out_t0, eps=1e-6):
        k = self.k
        for s0 in range(0, n, 512):
            ts_ = slice(t0 + s0, t0 + s0 + 512)
            pp = self.ps()
            for kc in range(KC):
                sq = self.wbuf_f("sq", [128, 512], BF16)
                k.act(sq[:, :], self.hT[:, kc, ts_], AF.Square)
                k.mm(pp[:, :], self.ones_bf[:, :], sq[:, :], start=(kc == 0), stop=(kc == KC - 1))
            rs = self.wbuf_f("rstd", [128, 512], F32, 2)
            k.act(rs[:, :], pp[:, :], AF.Sqrt, bias=self.cst[:, 130:131], scale=1.0 / D)
            k.recip(rs[:, :], rs[:, :])
            os_ = slice(out_t0 + s0, out_t0 + s0 + 512)
            for kc in range(KC):
                k.stt(out[:, kc, os_], self.hT[:, kc, ts_], g[:, kc:kc + 1], rs[:, :], ALU.mult, ALU.mult)

    def wbuf_f(self, tag, shape, dt, nbuf=3):
        key = "f_" + tag
        if key not in self.wbufs:
            self.wbufs[key] = [[self.k.sb("t_%s_%d" % (tag, i), shape, dt, es=self.pes) for i in range(nbuf)], 0]
        lst = self.wbufs[key]
        b = lst[0][lst[1] % len(lst[0])]
        lst[1] += 1
        return b

    def rmsnorm_gen(self, g, t0, n, out, out_t0):
        k = self.k
        for s0 in range(0, n, 512):
            ts_ = slice(t0 + s0, t0 + s0 + 512)
            pp = self.psb[7]
            for kc in range(KC):
                sq = self.wbuf_f("sq", [128, 512], BF16)
                k.act(sq[:, :], self.hT[:, kc, ts_], AF.Square)
                k.mm(pp[:, :], self.ones_bf[:, :], sq[:, :], start=(kc == 0), stop=(kc == KC - 1))
                yield
            rs = self.wbuf_f("rstd", [128, 512], F32, 2)
            k.act(rs[:, :], pp[:, :], AF.Sqrt, bias=self.cst[:, 130:131], scale=1.0 / D)
            k.recip(rs[:, :], rs[:, :])
            os_ = slice(out_t0 + s0, out_t0 + s0 + 512)
            for kc in range(KC):
                k.stt(out[:, kc, os_], self.hT[:, kc, ts_], g[:, kc:kc + 1], rs[:, :], ALU.mult, ALU.mult)
                yield

    def ffn_all(self, l, nm, gname, TT):
        k = self.k
        S = self.S
        g = self.P(l, gname)
        tiles = list(range(0, S, TT))
        nh = len(tiles)
        full = self.uT.h
        uh = [Buf("uT_h%d" % i, full[:, :, t0:t0 + TT], "sb") for i, t0 in enumerate(tiles)]
        actT = k.sb("actT", [128, NJ, TT], BF16, es=self.pes)
        for _ in self.rmsnorm_gen(g, tiles[0], TT, uh[0], 0):
            pass
        for ti, t0 in enumerate(tiles):
            u = uh[ti]
            gen = self.rmsnorm_gen(g, tiles[ti + 1], TT, uh[ti + 1], 0) if ti + 1 < nh else None
            for j in range(NJ):
                w = self.wload(l, nm + "_gu", j, "gu")
                for s0 in range(0, TT, 512):
                    pg, pu = self.ps(), self.ps()
                    for kc in range(KC):
                        k.mm(pg[:, :], w[:, kc, 0:128], u[:, kc, s0:s0 + 512], start=(kc == 0), stop=(kc == KC - 1))
                    for kc in range(KC):
                        k.mm(pu[:, :], w[:, kc, 128:256], u[:, kc, s0:s0 + 512], start=(kc == 0), stop=(kc == KC - 1))
                    sg = self.wbuf_f("sg", [128, 512], F32)
                    k.act(sg[:, :], pg[:, :], AF.Silu)
                    k.tt(actT[:, j, s0:s0 + 512], sg[:, :], pu[:, :], ALU.mult)
                    if gen is not None:
                        for _ in range(2):
                            try:
                                next(gen)
                            except StopIteration:
                                gen = None
                                break
            if gen is not None:
                for _ in gen:
                    pass
            for m in range(8):
                w = self.wload(l, nm + "_d", m, "wd", 2)
                for s0 in range(0, TT, 512):
                    ts_ = slice(t0 + s0, t0 + s0 + 512)
                    po = self.ps()
                    for j in range(NJ):
                        k.mm(po[:, :], w[:, j, :], actT[:, j, s0:s0 + 512], start=(j == 0), stop=(j == NJ - 1))
                    k.stt(self.hT[:, m, ts_], po[:, :], 0.5, self.hT[:, m, ts_], ALU.mult, ALU.add)

    def ple(self, l, q, t0, n):
        k = self.k
        u = self.uT
        self.rmsnorm(self.P(l, "ple_norm"), t0, n, u, t0)
        pf = k.sb("pf", [128, 2, n], F32, es=self.pes)
        pb = k.sb("pb", [128, 2, n], BF16, es=self.pes)
        src = self.pT.h[l, q].rearrange("(c p) t -> p c t", p=128)[:, :, t0:t0 + n]
        k.dma("sp", pf[:, :, :], V(self.pT, src))
        k.cp(pb[:, :, :], pf[:, :, :], eng="act")
        for m in range(8):
            wg = self.wload(l, "pgw", m, "w8", 3)
            wp = self.wload(l, "plw", m, "w2", 2)
            for s0 in range(0, n, 512):
                ts_ = slice(t0 + s0, t0 + s0 + 512)
                pg, pp = self.ps(), self.ps()
                for kc in range(KC):
                    k.mm(pg[:, :], wg[:, kc, :], u[:, kc, ts_], start=(kc == 0), stop=(kc == KC - 1))
                for c in range(2):
                    k.mm(pp[:, :], wp[:, c, :], pb[:, c, s0:s0 + 512], start=(c == 0), stop=(c == 1))
                sg = self.wbuf_f("sg", [128, 512], F32)
                k.act(sg[:, :], pg[:, :], AF.Sigmoid)
                k.tt(sg[:, :], sg[:, :], pp[:, :], ALU.mult)
                k.tt(self.hT[:, m, ts_], self.hT[:, m, ts_], sg[:, :], ALU.add)

    def merge_branch(self, l, b, oT, merged, first):
        k = self.k
        S = self.S
        for m in range(8):
            wb = self.wload(l, "br%d" % b, m, "w4", 2)
            wg = self.wload(l, "mg", b * 8 + m, "w8", 3)
            for t0 in range(0, S, 512):
                ts_ = slice(t0, t0 + 512)
                pa, pg = self.ps(), self.ps()
                for c in range(4):
                    k.mm(pa[:, :], wb[:, c, :], oT[:, c, ts_], start=(c == 0), stop=(c == 3))
                for kc in range(KC):
                    k.mm(pg[:, :], wg[:, kc, :], self.uT[:, kc, ts_], start=(kc == 0), stop=(kc == KC - 1))
                sg = self.wbuf_f("sg", [128, 512], F32)
                k.act(sg[:, :], pg[:, :], AF.Sigmoid)
                if first:
                    k.tt(merged[:, m, ts_], sg[:, :], pa[:, :], ALU.mult)
                else:
                    k.tt(sg[:, :], sg[:, :], pa[:, :], ALU.mult)
                    k.tt(merged[:, m, ts_], merged[:, m, ts_], sg[:, :], ALU.add)

    def out_proj(self, l, merged):
        k = self.k
        for m in range(8):
            w = self.wload(l, "wout", m, "w8", 3)
            for t0 in range(0, self.S, 512):
                ts_ = slice(t0, t0 + 512)
                po = self.ps()
                for kc in range(KC):
                    k.mm(po[:, :], w[:, kc, :], merged[:, kc, ts_], start=(kc == 0), stop=(kc == KC - 1))
                k.tt(self.hT[:, m, ts_], self.hT[:, m, ts_], po[:, :], ALU.add)

    def phase_begin(self):
        self.pes = ExitStack()
        self.wbufs = {}
        self.k.scope_begin()

    def phase_end(self):
        self.k.scope_end()
        self.pes.close()
        self.wbufs = {}

    def mixers(self, l, q):
        k = self.k
        S = self.S
        self.phase_begin()
        self.rmsnorm(self.P(l, "mix_norm"), 0, S, self.uT, 0)
        self.phase_end()
        self.phase_begin()
        oT = k.sb("oT", [128, 4, S], BF16, es=self.pes)
        merged = None
        first = True
        for b in (2, 0, 1):
            if not self.flags.get("mix%d" % b, False):
                continue
            mes = ExitStack()
            save = self.pes, self.wbufs
            self.pes, self.wbufs = mes, {}
            k.scope_begin()
            [self.hgrn2, self.nsa, self.rwkv][b](l, q, oT)
            k.scope_end()
            mes.close()
            self.pes, self.wbufs = save
            if merged is None:
                merged = k.sb("merged", [128, KC, S], BF16, es=self.pes)
            mes = ExitStack()
            self.pes, self.wbufs = mes, {}
            k.scope_begin()
            self.merge_branch(l, b, oT, merged, first)
            k.scope_end()
            mes.close()
            self.pes, self.wbufs = save
            first = False
        if not first:
            self.out_proj(l, merged)
        self.phase_end()


    def prep_lb(self):
        k = self.k
        NL = self.NL
        e = k.sb("lb_e", [128, 4, 4], F32)
        ssum = k.sb("lb_s", [128, 4], F32)
        for j in range(4):
            k.act(e[:, :, j], self.P(0, "hg_lb%d" % j), AF.Exp)
        k.tt(ssum[:, :], e[:, :, 0], e[:, :, 1], ALU.add)
        k.tt(ssum[:, :], ssum[:, :], e[:, :, 2], ALU.add)
        k.tt(ssum[:, :], ssum[:, :], e[:, :, 3], ALU.add)
        k.recip(ssum[:, :], ssum[:, :])
        acc = k.sb("lb_acc", [128, 4], F32)
        k.memset(acc[:, :], 1.0)
        for l in range(NL):
            if l > 0:
                tmp = k.sb("lb_t%d" % l, [128, 4], F32)
                k.tt(tmp[:, :], e[:, :, l], ssum[:, :], ALU.mult)
                k.tt(acc[:, :], acc[:, :], tmp[:, :], ALU.subtract)
            k.cp(self.oml[:, :, l], acc[:, :])

    def psbf(self):
        p = self.ps()
        return V(p, p.h[:, :].bitcast(BF16))

    def hgrn2(self, l, q, oT):
        k = self.k
        S = self.S
        BW = 256
        NCH = BW // 64
        cmk = self.cst[:, 384:384 + BW]
        one = self.cst[:, 128:129]
        gn = self.P(l, "hg_norm")

        def stream(hp, sid):
            sfx = "_%d" % sid
            T2 = lambda nm, dt=F32, w=BW: k.sb(nm + sfx, [128, w], dt, es=self.pes)
            qT, kT, bT, sgT = T2("hq"), T2("hk"), T2("hb"), T2("hsg")
            dT, eT, q1, k1, q2 = T2("hd"), T2("he"), T2("hq1"), T2("hk1"), T2("hq2")
            vb, k2b, vtok, k2tok, scb, o2 = (T2("hvb", BF16), T2("hk2b", BF16), T2("hvtok", BF16), T2("hk2tok", BF16),
                                             T2("hscb", BF16), T2("ho2", BF16))
            dec = T2("hdec", F32, NCH)
            rs, tmp = T2("hrs"), T2("htmp")
            state = T2("hstate", F32, 64)
            pst = [3 * sid, 0]
            psO = self.psb[6 + sid]

            def PS():
                self.ps_stream = pst
                p = self.ps()
                self.ps_stream = None
                return p

            def PSBF():
                p = PS()
                return V(p, p.h[:, :].bitcast(BF16))

            def sigm(out, in_, sign):
                k.act(out, in_, AF.Exp, scale=-1.0 * sign)
                k.act(out, out, AF.Ln, bias=one)
                k.act(out, out, AF.Exp, scale=-1.0)

            w = self.wload(l, "hg", hp, "whg" + sfx, 1)
            k.memset(state[:, :], 0.0)
            units = [(slice(h * 64, h * 64 + 64), slice(c * 64, c * 64 + 64)) for h in range(2) for c in range(NCH)]

            def proj(i, ts_):
                pp = PS()
                for kc in range(KC):
                    k.mm(pp[:, 0:BW], w[:, kc, i * 128:(i + 1) * 128], self.uT[:, kc, ts_], start=(kc == 0), stop=(kc == KC - 1))
                return pp

            for t0 in range(0, S, BW):
                ts_ = slice(t0, t0 + BW)
                pq = proj(0, ts_)
                yield
                sigm(qT[:, :], pq[:, 0:BW], 1.0)
                k.tt(qT[:, :], qT[:, :], pq[:, 0:BW], ALU.mult)
                pg = proj(3, ts_)
                yield
                sigm(sgT[:, :], pg[:, 0:BW], 1.0)
                k.tt(sgT[:, :], sgT[:, :], pg[:, 0:BW], ALU.mult)
                pf = proj(1, ts_)
                yield
                sigm(kT[:, :], pf[:, 0:BW], -1.0)
                k.ts(kT[:, :], kT[:, :], self.oml[:, hp, l:l + 1], ALU.mult)
                k.act(dT[:, :], kT[:, :], AF.Ln, bias=one, scale=-1.0)
                pi = proj(2, ts_)
                yield
                k.cp(vb[:, :], pi[:, 0:BW], eng="act")
                k.scan(bT[:, :], self.cmask[:, ts_], dT[:, :], 0.0, ALU.mult, ALU.add)
                b3 = bT[:, :].rr("p (c t) -> p c t", t=64)
                bmid = V(bT, b3.ap[:, :, 31:32].broadcast_to([128, NCH, 64]))
                bend = V(bT, b3.ap[:, :, 63:64].broadcast_to([128, NCH, 64]))
                d3 = dT[:, :].rr("p (c t) -> p c t", t=64)
                k.tt(d3, b3, bmid, ALU.subtract)
                k.act(eT[:, :], dT[:, :], AF.Exp)
                k.tt(q1[:, :], qT[:, :], eT[:, :], ALU.mult)
                yield
                k.act(eT[:, :], dT[:, :], AF.Exp, scale=-1.0)
                k.tt(k1[:, :], kT[:, :], eT[:, :], ALU.mult)
                k.act(eT[:, :], bT[:, :], AF.Exp)
                k.tt(q2[:, :], qT[:, :], eT[:, :], ALU.mult)
                yield
                k.tt(d3, bend, b3, ALU.subtract)
                k.act(eT[:, :], dT[:, :], AF.Exp)
                k.tt(k2b[:, :], kT[:, :], eT[:, :], ALU.mult)
                k.act(dec[:, :], b3[:, :, 63], AF.Exp)
                yield
                pv = PSBF()
                for hs, cs in units:
                    k.tr(pv[hs, cs], vb[hs, cs], self.ident_bf[hs, hs])
                pk2 = PSBF()
                for hs, cs in units:
                    k.tr(pk2[hs, cs], k2b[hs, cs], self.ident_bf[hs, hs])
                yield
                k.cp(vtok[:, :], pv[:, 0:BW], eng="act")
                k.cp(k2tok[:, :], pk2[:, 0:BW])
                psS = PS()
                for hs, cs in units:
                    k.mm(psS[hs, cs], k1[hs, cs], q1[hs, cs])
                yield
                k.tt(scb[:, :], psS[:, 0:BW], cmk, ALU.mult)
                for c in range(NCH):
                    cs = slice(c * 64, c * 64 + 64)
                    psH = PS()
                    for h in range(2):
                        hs = slice(h * 64, h * 64 + 64)
                        k.mm(psO[hs, cs], vtok[hs, cs], scb[hs, cs], start=True, stop=False)
                        k.mm(psO[hs, cs], state[hs, :], q2[hs, cs], start=False, stop=True)
                        k.mm(psH[hs, 0:64], k2tok[hs, cs], vtok[hs, cs])
                    yield
                    k.stt(state[:, :], state[:, :], dec[:, c:c + 1], psH[:, 0:64], ALU.mult, ALU.add)
                k.cp(tmp[:, :], psO[:, 0:BW], eng="act")
                k.tt(o2[:, :], tmp[:, :], tmp[:, :], ALU.mult)
                psN = PS()
                k.mm(psN[:, 0:BW], self.blk_bf[:, :], o2[:, :])
                yield
                k.act(rs[:, :], psN[:, 0:BW], AF.Ln, bias=self.cst[:, 130:131], scale=1.0 / 64)
                k.act(rs[:, :], rs[:, :], AF.Exp, scale=-0.5)
                k.stt(tmp[:, :], tmp[:, :], gn[:, hp:hp + 1], rs[:, :], ALU.mult, ALU.mult)
                k.tt(oT[:, hp, ts_], tmp[:, :], sgT[:, :], ALU.mult)
                yield

        for pair in ((0, 1), (2, 3)):
            pes_ = ExitStack()
            k.scope_begin()
            save = self.pes, self.wbufs
            self.pes, self.wbufs = pes_, {}
            alive = [stream(pair[0], 0), stream(pair[1], 1)]
            if self.flags.get("hg1"):
                for g_ in alive:
                    for _ in g_:
                        pass
                alive = []
            while alive:
                for g_ in list(alive):
                    try:
                        next(g_)
                    except StopIteration:
                        alive.remove(g_)
            k.scope_end()
            pes_.close()
            self.pes, self.wbufs = save

    def nsa(self, l, q, oT):
        k = self.k
        S = self.S
        NT = S // 128
        NC = S // 16 - 1
        self.psn = 4
        es = self.pes
        sbt = lambda nm, shape, dt=F32: k.sb(nm, shape, dt, es=es)
        Ex = sbt("nEx", [128, S], BF16)
        k.memset(Ex[:, :], 0.0)
        k.dma("pool", Ex[0:32, :], V(self.exc_d, self.exc_d.h[:, :]))
        KsT, KwT = sbt("nKsT", [128, S], BF16), sbt("nKwT", [128, S], BF16)
        for t_ in (KsT, KwT):
            k.memset(t_[64:128, :], 0.0)
            k.memset(t_[64:65, :], 1.0)
        Vs, Vw = sbt("nVs", [128, NT, 65], BF16), sbt("nVw", [128, NT, 65], BF16)
        kcT = sbt("nkcT", [128, 128], BF16)
        k.memset(kcT[:, :], 0.0)
        Vc = sbt("nVc", [128, 97], BF16)
        pe_bf = sbt("npe", [128, 2, 16], BF16)
        cb = sbt("ncb", [128, 2])
        k.cp(pe_bf[:, 0, :], self.P(l, "cmp_pe0"))
        k.cp(pe_bf[:, 1, :], self.P(l, "cmp_pe1"))
        for g in range(2):
            ces = ExitStack()
            k.scope_begin()
            save = self.pes, self.wbufs
            self.pes, self.wbufs = ces, {}
            wkv = self.wload(l, "nkv", g, "wnkv", 1)
            xc = [k.sb("nxc%d" % i, [128, S], BF16, es=ces) for i in range(2)]
            for t_ in xc:
                k.memset(t_[64:128, :], 0.0)
            hid = k.sb("nhid", [128, 128], BF16, es=ces)
            k.memset(Vs[:, :, 64:65], 1.0)
            k.memset(Vw[:, :, 64:65], 1.0)
            for t0 in range(0, S, 512):
                ts_ = slice(t0, t0 + 512)
                for i, dst in ((0, xc[0]), (1, xc[1]), (2, KsT), (4, KwT)):
                    pp = self.ps()
                    for kc in range(KC):
                        k.mm(pp[0:64, :], wkv[:, kc, i * 64:(i + 1) * 64], self.uT[:, kc, ts_], start=(kc == 0), stop=(kc == KC - 1))
                    k.cp(dst[0:64, ts_], pp[0:64, :], eng=("act" if i % 4 == 0 else "dve"))
            for qt in range(NT):
                qs = slice(qt * 128, qt * 128 + 128)
                for i, dst in ((3, Vs), (5, Vw)):
                    pp = self.ps()
                    for kc in range(KC):
                        k.mm(pp[:, 0:64], self.uT[:, kc, qs], wkv[:, kc, i * 64:(i + 1) * 64], start=(kc == 0), stop=(kc == KC - 1))
                    k.cp(dst[:, qt, 0:64], pp[:, 0:64], eng=("act" if i == 3 else "dve"))
            for kv in range(2):
                w1 = self.wload(l, "cw1_%d" % kv, 0, "wcw1", 1)
                w1c = self.wload(l, "cw1c_%d" % kv, 0, "wcw1c", 1)
                w2 = self.wload(l, "cw2_%d" % kv, 0, "wcw2", 1)
                pc_ = self.ps()
                for kc in range(16):
                    k.mm(pc_[:, 0:1], w1c[:, kc, :], pe_bf[:, kv, kc:kc + 1], start=(kc == 0), stop=(kc == 15))
                k.cp(cb[:, kv:kv + 1], pc_[:, 0:1])
                ph = self.ps()
                for l_ in range(32):
                    k.mm(ph[:, 0:NC], w1[:, 0, l_ * 128:(l_ + 1) * 128], xc[kv][:, l_:l_ + 16 * (NC - 1) + 1:16],
                         start=(l_ == 0), stop=(l_ == 31))
                k.act(hid[:, 0:NC], ph[:, 0:NC], AF.Silu, bias=cb[:, kv:kv + 1])
                po = self.ps()
                if kv == 0:
                    k.mm(po[0:64, 0:NC], w2[:, 0, :], hid[:, 0:NC])
                    k.cp(kcT[0:64, 0:NC], po[0:64, 0:NC])
                else:
                    k.mm(po[0:NC, 0:64], hid[:, 0:NC], w2[:, 0, :])
                    k.cp(Vc[0:NC, 0:64], po[0:NC, 0:64])
                    k.memset(Vc[:, 64:65], 1.0)
                    k.dma("pool", Vc[:, 65:97], V(self.cov_d, self.cov_d.h[:, :]))
            k.scope_end()
            ces.close()
            self.pes, self.wbufs = save
            aes = ExitStack()
            k.scope_begin()
            save = self.pes, self.wbufs
            self.pes, self.wbufs = aes, {}
            sba = lambda nm, shape, dt=F32: k.sb(nm, shape, dt, es=aes)
            Dt = sba("nDt", [128, 4, 512])
            BCb = [sba("nBC%d" % i, [128, 512]) for i in range(2)]
            PTb = [sba("nPT%d" % i, [128, 512], BF16) for i in range(3)]
            QTb = [sba("nQT%d" % i, [128, 512], BF16) for i in range(2)]
            Ocb = [sba("nOc%d" % i, [128, 4, 97]) for i in range(2)]
            Osw = [sba("nOsw%d" % i, [128, 4, 2, 65]) for i in range(2)]
            seltb = [sba("nselc%d" % i, [128, 64]) for i in range(2)]
            Gb = [sba("nG%d" % i, [128, 12]) for i in range(2)]
            rzb = [sba("nrz%d" % i, [128, 12]) for i in range(2)]
            selT4b = [sba("nselT%d" % i, [128, 512], BF16) for i in range(2)]
            for t_ in selT4b:
                k.memset(t_[:, :], 0.0)
            imp, sc, m8 = sba("nimp", [128, 32]), sba("nsc", [128, 32]), sba("nm8", [128, 8])
            selb = sba("nselb", [128, 32], BF16)
            otok = sba("notok", [128, 256], BF16)
            coef = sba("ncoef", [128, 12])
            tmpo = sba("ntmpo", [128, 64])
            wgate = self.wload(l, "ngate", 0, "wng", 1)
            wq = self.wload(l, "nq", g, "wnq", 1)
            k.dma("sp", Dt[:, :, :], V(self.dtl_d, self.dtl_d.h[g].rearrange("j p n -> p j n")))
            for par in range(2):
                k.memset(QTb[par][64:128, :], 0.0)
                k.dma("pool", QTb[par][64:65, :], V(self.dtl_d, self.dtl_d.h[g, 3, 64:65, :]))
            for j in range(3):
                k.tt(Dt[:, j, :], Dt[:, j, :], Dt[:, 3, :], ALU.subtract)
            Awb = [self.psb[4], self.psb[6]]
            Asb = [self.psb[5], self.psb[7]]

            def prologue(qt):
                par = qt % 2
                qs = slice(qt * 128, qt * 128 + 128)
                BC, selt, QT, Oc, G, rz, selT4 = BCb[par], seltb[par], QTb[par], Ocb[par], Gb[par], rzb[par], selT4b[par]
                k.dma("sp", BC[:, :], V(self.bcb_d, self.bcb_d.h[g, qt]))
                k.dma("sp", selt[:, :], V(self.selc_d, self.selc_d.h[qt]))
                yield
                pq = self.ps()
                for hh in range(4):
                    for kc in range(KC):
                        k.mm(pq[0:64, hh * 128:(hh + 1) * 128], wq[:, kc, hh * 64:(hh + 1) * 64], self.uT[:, kc, qs],
                             start=(kc == 0), stop=(kc == KC - 1))
                    yield
                k.act(QT[0:64, :], pq[0:64, :], AF.Copy, scale=0.125)
                pg_ = self.ps()
                for kc in range(KC):
                    k.mm(pg_[:, 0:12], self.uT[:, kc, qs], wgate[:, kc, g * 12:(g + 1) * 12], start=(kc == 0), stop=(kc == KC - 1))
                k.act(G[:, :], pg_[:, 0:12], AF.Exp, scale=-1.0)
                k.ts(G[:, :], G[:, :], 1.0, ALU.add)
                k.recip(G[:, :], G[:, :])
                yield
                pl = self.ps()
                k.mm(pl[0:NC, :], kcT[:, 0:NC], QT[:, :])
                PT = PTb[2]
                k.tt(BC[0:NC, :], pl[0:NC, :], BC[0:NC, :], ALU.add)
                yield
                k.act(PT[0:NC, :], BC[0:NC, :], AF.Exp)
                yield
                po = self.ps()
                for hh in range(4):
                    k.mm(po[:, hh * 97:(hh + 1) * 97], PT[0:NC, hh * 128:(hh + 1) * 128], Vc[0:NC, :])
                yield
                k.cp(Oc[:, :, :], po[:, 0:388].rr("p (h n) -> p h n", n=97))
                k.ts(rz[:, 0:4], Oc[:, :, 64], 1e-30, ALU.max)
                k.recip(rz[:, 0:4], rz[:, 0:4])
                yield
                k.ts(imp[:, :], Oc[:, 0, 65:97], rz[:, 0:1], ALU.mult)
                for hh in range(1, 4):
                    k.stt(imp[:, :], Oc[:, hh, 65:97], rz[:, hh:hh + 1], imp[:, :], ALU.mult, ALU.add)
                yield
                k.tt(sc[:, :], imp[:, :], selt[:, 0:32], ALU.mult)
                k.tt(sc[:, :], sc[:, :], selt[:, 32:64], ALU.add)
                k.max8(m8[:, :], sc[:, :])
                yield
                k.ts(sc[:, :], sc[:, :], m8[:, 7:8], ALU.is_ge)
                k.ts(selb[:, :], sc[:, :], 30000.0, ALU.mult, -30000.0, ALU.add)
                yield
                pst = self.psbf()
                for hh in range(4):
                    k.tr(pst[0:32, hh * 128:(hh + 1) * 128], selb[:, :], self.ident_bf[:, :])
                k.cp(selT4[0:32, :], pst[0:32, 0:512])
                yield

            def epilogue(qt):
                par = qt % 2
                qs = slice(qt * 128, qt * 128 + 128)
                Oc, G, rz, OO = Ocb[par], Gb[par], rzb[par], Osw[par]
                k.cp(OO[:, :, 0, :], Awb[par][:, 0:260].rr("p (h n) -> p h n", n=65), eng="act")
                k.cp(OO[:, :, 1, :], Asb[par][:, 0:260].rr("p (h n) -> p h n", n=65), eng="dve")
                yield
                k.ts(rz[:, 4:8], OO[:, :, 1, 64], 1e-30, ALU.max)
                k.ts(rz[:, 8:12], OO[:, :, 0, 64], 1e-30, ALU.max)
                k.recip(rz[:, 4:12], rz[:, 4:12])
                G3 = G[:, :].rr("p (h j) -> p h j", j=3)
                for j in range(3):
                    k.tt(coef[:, j * 4:(j + 1) * 4], G3[:, :, j], rz[:, j * 4:(j + 1) * 4], ALU.mult)
                yield
                for hh in range(4):
                    k.ts(tmpo[:, 0:64], Oc[:, hh, 0:64], coef[:, hh:hh + 1], ALU.mult)
                    k.stt(tmpo[:, 0:64], OO[:, hh, 1, 0:64], coef[:, 4 + hh:5 + hh], tmpo[:, 0:64], ALU.mult, ALU.add)
                    k.stt(otok[:, hh * 64:(hh + 1) * 64], OO[:, hh, 0, 0:64], coef[:, 8 + hh:9 + hh], tmpo[:, 0:64], ALU.mult, ALU.add)
                    yield
                for j in range(2):
                    ptr = V(self.psb[2], self.psb[2].h[:, :].bitcast(BF16))
                    k.tr(ptr[:, 0:128], otok[:, j * 128:(j + 1) * 128], self.ident_bf[:, :])
                    k.cp(oT[:, g * 2 + j, qs], ptr[:, 0:128], eng=("act" if j else "dve"))
                yield

            def drain(gen):
                if gen is not None:
                    for _ in gen:
                        pass

            def step(gen):
                if gen is not None:
                    try:
                        next(gen)
                    except StopIteration:
                        return None
                return gen

            self.ps_fixed = 3
            drain(prologue(0))
            epi = None
            for qt in range(NT):
                par = qt % 2
                QT, selT4 = QTb[par], selT4b[par]
                pro = prologue(qt + 1) if qt + 1 < NT else None
                if self.flags.get("nopipe") or self.flags.get("noepi"):
                    drain(epi)
                    epi = None
                if self.flags.get("nopipe") or self.flags.get("nopro"):
                    drain(pro)
                    pro = None
                its = [("w", kt) for kt in (qt - 2, qt - 1, qt) if kt >= 0] + [("s", kt) for kt in range(qt + 1)]
                nw = len([1 for x in its if x[0] == "w"])
                pls = {}

                def logits(i):
                    kind, kt = its[i]
                    ks = slice(kt * 128, kt * 128 + 128)
                    pl = self.psb[i % 2]
                    if kind == "w":
                        k.mm(pl[:, :], KwT[:, ks], QT[:, :])
                    else:
                        k.mm(pl[:, :], KsT[:, ks], QT[:, :], start=True, stop=False)
                        k.mm(pl[:, :], Ex[:, ks], selT4[:, :], start=False, stop=True)
                    pls[i] = pl

                logits(0)
                for i, (kind, kt) in enumerate(its):
                    if i + 1 < len(its):
                        logits(i + 1)
                    pl = pls.pop(i)
                    dlt = qt - kt
                    PT = PTb[i % 2]
                    if dlt <= 1 or (kind == "w" and dlt == 2):
                        k.tt(pl[:, :], pl[:, :], Dt[:, dlt, :], ALU.add)
                        k.act(PT[:, :], pl[:, :], AF.Exp)
                    else:
                        k.act(PT[:, :], pl[:, :], AF.Exp)
                    for hh in range(4):
                        if kind == "w":
                            k.mm(Awb[par][:, hh * 65:(hh + 1) * 65], PT[:, hh * 128:(hh + 1) * 128], Vw[:, kt, :],
                                 start=(i == 0 and hh == 0), stop=(i == nw - 1), sgc=True)
                        else:
                            k.mm(Asb[par][:, hh * 65:(hh + 1) * 65], PT[:, hh * 128:(hh + 1) * 128], Vs[:, kt, :],
                                 start=(kt == 0 and hh == 0), stop=(kt == qt), sgc=True)
                    epi = step(epi)
                    pro = step(pro)
                drain(epi)
                drain(pro)
                epi = epilogue(qt)
            drain(epi)
            self.ps_fixed = None
            k.scope_end()
            aes.close()
            self.pes, self.wbufs = save
        self.psn = 6


    def rwkv(self, l, q, oT):
        k = self.k
        S = self.S
        BW = 256
        NCH = BW // 64
        T = lambda nm, dt=F32, w=BW: k.sb(nm, [128, w], dt, es=self.pes)
        mu = self.P(l, "rw_mu")
        LA = k.sb("rLA", [128, S], BF16, es=self.pes)
        LG = k.sb("rLG", [128, S], BF16, es=self.pes)
        msu, msl, idb = self.cst2[:, 0:256], self.cst2[:, 256:512], self.cst2[:, 512:768]
        cmk = self.cst[:, 384:384 + BW]
        blkf = self.cst[:, 256:384]
        negw = k.sb("rnegw", [128, 8], F32, es=self.pes)
        k.ts(negw[:, 0:4], self.P(l, "rw_w0"), -1.0, ALU.mult)
        k.ts(negw[:, 4:8], self.P(l, "rw_a0"), -1.0, ALU.mult)

        def mkshift(raw, tmp, BW=BW):
            def shift(ps_, out, mucol, carry, first):
                if first:
                    k.memset(raw[:, 0:1], 0.0)
                else:
                    k.cp(raw[:, 0:1], carry[:, 0:1])
                k.cp(raw[:, 1:BW + 1], ps_, eng="act")
                k.tt(tmp[:, :], raw[:, 0:BW], raw[:, 1:BW + 1], ALU.subtract)
                k.stt(out, tmp[:, :], mucol, raw[:, 1:BW + 1], ALU.mult, ALU.add)
                k.cp(carry[:, 0:1], raw[:, BW:BW + 1])
            return shift

        aes = ExitStack()
        k.scope_begin()
        save = self.pes, self.wbufs
        self.pes, self.wbufs = aes, {}
        wl_ = self.wload(l, "rwl", 0, "wrwl", 1)
        carA = k.sb("rcarA", [128, 2], F32, es=aes)
        BA = 512
        xa = k.sb("rxa", [128, BA], F32, es=aes)
        xg = k.sb("rxg", [128, BA], F32, es=aes)
        rawA = k.sb("rrawA", [128, BA + 1], F32, es=aes)
        rawG = k.sb("rrawG", [128, BA + 1], F32, es=aes)
        tmpG = k.sb("rtmpG", [128, BA], F32, es=aes)
        tmpA = k.sb("rtmpA", [128, BA], F32, es=aes)
        shiftA = mkshift(rawA, tmpA, BA)
        shiftG = mkshift(rawG, tmpG, BA)
        for t0 in range(0, S, BA):
            ts_ = slice(t0, t0 + BA)
            pa, pg = self.ps(), self.ps()
            for kc in range(KC):
                k.mm(pa[:, 0:BA], wl_[:, kc, 0:128], self.uT[:, kc, ts_], start=(kc == 0), stop=(kc == KC - 1))
            for kc in range(KC):
                k.mm(pg[:, 0:BA], wl_[:, kc, 128:256], self.uT[:, kc, ts_], start=(kc == 0), stop=(kc == KC - 1))
            shiftA(pa[:, 0:BA], xa[:, :], mu[:, 12:13], carA[:, 0:1], t0 == 0)
            k.act(LA[0:64, ts_], xa[0:64, :], AF.Tanh)
            k.cp(LA[64:128, ts_], xa[64:128, :])
            shiftG(pg[:, 0:BA], xg[:, :], mu[:, 13:14], carA[:, 1:2], t0 == 0)
            k.act(xg[:, :], xg[:, :], AF.Exp, scale=-1.0)
            k.act(xg[:, :], xg[:, :], AF.Ln, bias=self.cst[:, 128:129])
            k.act(LG[:, ts_], xg[:, :], AF.Exp, scale=-1.0)
        k.scope_end()
        aes.close()
        self.pes, self.wbufs = save

        def stream(hp, sid):
            sfx = "_%d" % sid
            T2 = lambda nm, dt=F32, w=BW: k.sb(nm + sfx, [128, w], dt, es=self.pes)
            raw, tmp, ee = T2("rraw", F32, BW + 1), T2("rtmp"), T2("ree")
            car = T2("rcar", F32, 4)
            shift = mkshift(raw, tmp)
            r_, k_, v_, a_, kk, ka, cw, lw, rk, gg = (T2("rr"), T2("rk"), T2("rv"), T2("ra"), T2("rkk"), T2("rka"),
                                                       T2("rcw"), T2("rlw"), T2("rrk"), T2("rgg"))
            Tt, X, XT, U0, WT = T2("rTt"), T2("rX"), T2("rXT"), T2("rU0"), T2("rWT")
            vb, Bb, Kb, Ab = T2("rvb", BF16), T2("rBb", BF16), T2("rKb", BF16), T2("rAb", BF16)
            Vtok, Btok, Ktok, Atok = T2("rVtok", BF16), T2("rBtok", BF16), T2("rKtok", BF16), T2("rAtok", BF16)
            AakT, ArbT, ArkT, Ttb = T2("rAak", BF16), T2("rArb", BF16), T2("rArk", BF16), T2("rTtb", BF16)
            Ub = T2("rUb", BF16, 64)
            H = T2("rH", F32, 64)
            PC = T2("rPC", F32, NCH)
            pst = [3 * sid, 0]
            psY = self.psb[6 + sid]

            def PS():
                self.ps_stream = pst
                p = self.ps()
                self.ps_stream = None
                return p

            def PSBF():
                p = PS()
                return V(p, p.h[:, :].bitcast(BF16))

            w = self.wload(l, "rw", hp, "wrw" + sfx, 1)
            wAB = self.wload(l, "rwAB", hp, "wrAB" + sfx, 1)
            wgB = self.wload(l, "rwgB", hp, "wrgB" + sfx, 1)
            k.memset(H[:, :], 0.0)
            P4 = lambda nm: self.P(l, nm)[:, hp:hp + 1]
            units = [(slice(h * 64, h * 64 + 64), slice(c * 64, c * 64 + 64)) for h in range(2) for c in range(NCH)]
            for t0 in range(0, S, BW):
                ts_ = slice(t0, t0 + BW)
                for i, (dst, cc) in enumerate(((r_, 0), (k_, 1), (v_, 2))):
                    pp = PS()
                    for kc in range(KC):
                        k.mm(pp[:, 0:BW], w[:, kc, i * 128:(i + 1) * 128], self.uT[:, kc, ts_], start=(kc == 0), stop=(kc == KC - 1))
                    yield
                    shift(pp[:, 0:BW], dst[:, :], mu[:, i * 4 + hp:i * 4 + hp + 1], car[:, cc:cc + 1], t0 == 0)
                    yield
                pw, pa, pg = PS(), PS(), PS()
                k.mm(pw[:, 0:BW], wAB[0:64, 0, :], LA[0:64, ts_])
                k.mm(pa[:, 0:BW], wAB[64:128, 0, :], LA[64:128, ts_])
                k.mm(pg[:, 0:BW], wgB[:, 0, :], LG[:, ts_])
                yield
                k.act(lw[:, :], pw[:, 0:BW], AF.Exp, bias=negw[:, hp:hp + 1], scale=-1.0)
                k.act(a_[:, :], pa[:, 0:BW], AF.Exp, bias=negw[:, 4 + hp:5 + hp], scale=-1.0)
                k.cp(gg[:, :], pg[:, 0:BW])
                k.act(lw[:, :], lw[:, :], AF.Ln, bias=self.cst[:, 128:129])
                k.act(a_[:, :], a_[:, :], AF.Ln, bias=self.cst[:, 128:129])
                k.act(lw[:, :], lw[:, :], AF.Exp, scale=-1.0)
                k.act(a_[:, :], a_[:, :], AF.Exp, scale=-1.0)
                k.ts(lw[:, :], lw[:, :], -0.6065306597126334, ALU.mult)
                yield
                k.ts(kk[:, :], k_[:, :], P4("rw_kk"), ALU.mult)
                k.tt(tmp[:, :], kk[:, :], kk[:, :], ALU.mult)
                pn = PS()
                k.mm(pn[:, 0:BW], blkf, tmp[:, :])
                yield
                k.ts(tmp[:, :], pn[:, 0:BW], 1e-24, ALU.max)
                k.act(tmp[:, :], tmp[:, :], AF.Ln)
                k.act(tmp[:, :], tmp[:, :], AF.Exp, scale=-0.5)
                k.tt(kk[:, :], kk[:, :], tmp[:, :], ALU.mult)
                k.ts(tmp[:, :], a_[:, :], -1.0, ALU.add, P4("rw_ka"), ALU.mult)
                k.stt(k_[:, :], tmp[:, :], 1.0, k_[:, :], ALU.add, ALU.mult)
                k.stt(rk[:, :], r_[:, :], P4("rw_rk"), k_[:, :], ALU.mult, ALU.mult)
                k.tt(ka[:, :], kk[:, :], a_[:, :], ALU.mult)
                k.cp(vb[:, :], v_[:, :])
                yield
                k.scan(cw[:, :], self.cmask[:, ts_], lw[:, :], 0.0, ALU.mult, ALU.add)
                cw3 = cw[:, :].rr("p (c t) -> p c t", t=64)
                cwend = V(cw, cw3.ap[:, :, 63:64].broadcast_to([128, NCH, 64]))
                k.act(PC[:, :], cw3[:, :, 63], AF.Exp)
                k.tt(lw[:, :], cw[:, :], lw[:, :], ALU.subtract)
                k.act(ee[:, :], lw[:, :], AF.Exp)
                k.stt(kk[:, :], kk[:, :], -1.0, ee[:, :], ALU.mult, ALU.mult)
                k.cp(Ab[:, :], kk[:, :])
                yield
                k.tt(tmp[:, :].rr("p (c t) -> p c t", t=64), cwend, cw3, ALU.subtract)
                k.act(ee[:, :], tmp[:, :], AF.Exp)
                k.tt(Bb[:, :], ka[:, :], ee[:, :], ALU.mult)
                k.tt(Kb[:, :], k_[:, :], ee[:, :], ALU.mult)
                yield
                k.act(ee[:, :], cw[:, :], AF.Exp, scale=-1.0)
                k.tt(ka[:, :], ka[:, :], ee[:, :], ALU.mult)
                k.tt(k_[:, :], k_[:, :], ee[:, :], ALU.mult)
                k.act(ee[:, :], cw[:, :], AF.Exp)
                k.tt(r_[:, :], r_[:, :], ee[:, :], ALU.mult)
                yield
                for src, dstt in ((vb, Vtok), (Bb, Btok), (Kb, Ktok), (Ab, Atok)):
                    pt = PSBF()
                    for hs, cs in units:
                        k.tr(pt[hs, cs], src[hs, cs], self.ident_bf[hs, hs])
                    yield
                    k.cp(dstt[:, :], pt[:, 0:BW], eng=("act" if src in (Bb, Ab) else "dve"))
                pN, pNT, pAk = PS(), PS(), PS()
                for hs, cs in units:
                    k.mm(pN[hs, cs], ka[hs, cs], kk[hs, cs])
                    k.mm(pNT[hs, cs], kk[hs, cs], ka[hs, cs])
                    k.mm(pAk[hs, cs], k_[hs, cs], kk[hs, cs])
                yield
                k.tt(X[:, :], pN[:, 0:BW], msu, ALU.mult)
                k.tt(XT[:, :], pNT[:, 0:BW], msl, ALU.mult)
                k.tt(AakT[:, :], pAk[:, 0:BW], msu, ALU.mult)
                k.tt(Tt[:, :], X[:, :], idb, ALU.add)
                yield
                pRb, pRk = PS(), PS()
                for hs, cs in units:
                    k.mm(pRb[hs, cs], ka[hs, cs], r_[hs, cs])
                    k.mm(pRk[hs, cs], k_[hs, cs], r_[hs, cs])
                yield
                k.tt(ArbT[:, :], pRb[:, 0:BW], cmk, ALU.mult)
                k.tt(ArkT[:, :], pRk[:, 0:BW], cmk, ALU.mult)
                for j in range(1, 6):
                    pX, pXT = PS(), PS()
                    for hs, cs in units:
                        if j < 5:
                            k.mm(pX[hs, cs], XT[hs, cs], X[hs, cs])
                        k.mm(pXT[hs, cs], X[hs, cs], XT[hs, cs])
                    yield
                    if j < 5:
                        k.cp(X[:, :], pX[:, 0:BW])
                    k.cp(XT[:, :], pXT[:, 0:BW], eng="act")
                    pT = PS()
                    for hs, cs in units:
                        k.mm(pT[hs, cs], XT[hs, cs], Tt[hs, cs])
                    yield
                    k.tt(Tt[:, :], Tt[:, :], pT[:, 0:BW], ALU.add)
                k.cp(Ttb[:, :], Tt[:, :], eng="act")
                pX1 = PS()
                for hs, cs in units:
                    k.mm(pX1[hs, cs], AakT[hs, cs], Vtok[hs, cs])
                yield
                k.cp(tmp[:, :], pX1[:, 0:BW])
                pU0, pWT = PS(), PS()
                for hs, cs in units:
                    k.mm(pU0[hs, cs], Tt[hs, cs], tmp[hs, cs])
                    k.mm(pWT[hs, cs], Atok[hs, cs], Ttb[hs, cs])
                yield
                k.cp(U0[:, :], pU0[:, 0:BW])
                k.cp(WT[:, :], pWT[:, 0:BW], eng="act")
                for c in range(NCH):
                    cs = slice(c * 64, c * 64 + 64)
                    psU = PS()
                    for h in range(2):
                        hs = slice(h * 64, h * 64 + 64)
                        k.mm(psU[hs, 0:64], WT[hs, cs], H[hs, :])
                    yield
                    k.tt(Ub[:, :], psU[:, 0:64], U0[:, cs], ALU.add)
                    psH = PS()
                    for h in range(2):
                        hs = slice(h * 64, h * 64 + 64)
                        k.mm(psY[hs, cs], H[hs, :], r_[hs, cs], start=True, stop=False)
                        k.mm(psY[hs, cs], Ub[hs, :], ArbT[hs, cs], start=False, stop=False)
                        k.mm(psY[hs, cs], Vtok[hs, cs], ArkT[hs, cs], start=False, stop=True)
                        k.mm(psH[hs, 0:64], Btok[hs, cs], Ub[hs, :], start=True, stop=False)
                        k.mm(psH[hs, 0:64], Ktok[hs, cs], Vtok[hs, cs], start=False, stop=True)
                    yield
                    k.stt(H[:, :], H[:, :], PC[:, c:c + 1], psH[:, 0:64], ALU.mult, ALU.add)
                y = X
                k.cp(y[:, :], psY[:, 0:BW])
                pm = PS()
                k.mm(pm[:, 0:BW], blkf, y[:, :])
                yield
                k.stt(y[:, :], pm[:, 0:BW], -1.0 / 64, y[:, :], ALU.mult, ALU.add)
                k.tt(tmp[:, :], y[:, :], y[:, :], ALU.mult)
                pv_ = PS()
                k.mm(pv_[:, 0:BW], blkf, tmp[:, :])
                pb_ = PS()
                k.mm(pb_[:, 0:BW], blkf, rk[:, :])
                yield
                k.act(tmp[:, :], pv_[:, 0:BW], AF.Ln, bias=self.cst[:, 131:132], scale=1.0 / 64)
                k.act(tmp[:, :], tmp[:, :], AF.Exp, scale=-0.5)
                k.tt(y[:, :], y[:, :], tmp[:, :], ALU.mult)
                k.ts(y[:, :], y[:, :], P4("rw_ln_w"), ALU.mult, P4("rw_ln_b"), ALU.add)
                k.tt(tmp[:, :], pb_[:, 0:BW], v_[:, :], ALU.mult)
                k.tt(y[:, :], y[:, :], tmp[:, :], ALU.add)
                k.tt(oT[:, hp, ts_], y[:, :], gg[:, :], ALU.mult)
                yield

        for pair in ((0, 1), (2, 3)):
            pes_ = ExitStack()
            k.scope_begin()
            save = self.pes, self.wbufs
            self.pes, self.wbufs = pes_, {}
            gens = [stream(pair[0], 0), stream(pair[1], 1)]
            if self.flags.get("rw1"):
                for g_ in gens:
                    for _ in g_:
                        pass
            else:
                alive = list(gens)
                while alive:
                    for g_ in list(alive):
                        try:
                            next(g_)
                        except StopIteration:
                            alive.remove(g_)
            k.scope_end()
            pes_.close()
            self.pes, self.wbufs = save

    def build(self):
        k = self.k
        S, NL = self.S, self.NL
        self.pes = k.es
        k.dma("sp", self.par[:, :, :], V(self.par_d, self.par_d.h.rearrange("l p n -> p l n")))
        k.dma("sp", self.cst[:, :], V(self.cst_d, self.cst_d.h[:, 0:896]))
        k.dma("pool", self.cmask[:, :], V(self.cst_d, self.cst_d.h[:, 896:896 + S]))
        k.dma("sp", self.cst2[:, :], V(self.cst_d, self.cst_d.h[:, 896 + S:896 + S + 768]))
        k.dma("pool", self.blk_bf[:, :], V(self.cst_d, self.cst_d.h[:, 256:384]))
        k.dma("pool", self.ident_bf[:, :], V(self.cst_d, self.cst_d.h[:, 0:128]))
        k.memset(self.ones_bf[:, :], 1.0)
        self.prep_lb()
        for l in range(NL):
            self.cast_weights(l)
        TT = min(1024, S)
        for q in range(self.NSEQ):
            src = self.xT.h[q].rearrange("(c p) t -> p c t", p=128)
            for c in range(KC):
                k.dma("sp", self.hT[:, c, :], V(self.xT, src[:, c, :]))
            for l in range(NL):
                self.phase_begin()
                self.ffn_all(l, "ffn1", "ffn1_norm", TT)
                self.phase_end()
                self.mixers(l, q)
                self.phase_begin()
                self.ffn_all(l, "ffn2", "ffn2_norm", TT)
                self.phase_end()
                for t0 in range(0, S, TT):
                    self.phase_begin()
                    self.ple(l, q, t0, TT)
                    self.phase_end()
            self.phase_begin()
            of = k.sb("of", [128, KC, 512], F32, es=self.pes)
            dst = self.outT.h[q].rearrange("(c p) t -> p c t", p=128)
            for t0 in range(0, S, 512):
                self.rmsnorm(self.P(0, "final_norm"), t0, 512, of, 0)
                k.dma("sp", V(self.outT, dst[:, :, t0:t0 + 512]), of[:, :, :])
            self.phase_end()
        k._wait("sp", self.outT.w)


def make_consts(S):
    c = np.zeros((128, 896 + S + 768), np.float32)
    p_ = np.arange(128)[:, None] % 64
    t_ = np.arange(256)[None, :] % 64
    c[:, 896 + S:896 + S + 256] = (p_ < t_)
    c[:, 896 + S + 256:896 + S + 512] = (t_ < p_)
    c[:, 896 + S + 512:896 + S + 768] = (t_ == p_)
    c[:, 0:128] = np.eye(128, dtype=np.float32)
    c[:, 128] = 1.0
    c[:, 130] = 1e-6
    c[:, 131] = 64e-5
    p = np.arange(128)
    c[:, 256:384] = (p[:, None] // 64 == p[None, :] // 64)
    t = np.arange(512)
    c[:, 384:896] = ((p[:, None] % 64) <= (t[None, :] % 64))
    tt = np.arange(S)
    c[:, 896:896 + S] = (tt % 64 != 0)[None, :]
    return c


def t5_bucket_np(n):
    n = np.maximum(n, 0)
    nf = np.maximum(n, 1).astype(np.float32)
    large = 16 + (np.log(nf / np.float32(16)) / np.float32(np.log(8.0)) * np.float32(16)).astype(np.int32)
    large = np.minimum(large, 31)
    return np.where(n < 16, n, large)


NEG = -30000.0


def nsa_consts(rel_bias, S):
    NT = S // 128
    rb = np.asarray(rel_bias, np.float32)
    kk_ = np.arange(128)[:, None]
    qq = np.arange(128)[None, :]
    dtl = np.zeros((2, 4, 128, 4, 128), np.float32)
    bcb = np.full((2, NT, 128, 4, 128), NEG, np.float32)
    n = np.arange(128)[:, None]
    for g in range(2):
        for hh in range(4):
            hd = g * 4 + hh
            d0 = qq - kk_
            dtl[g, 0, :, hh, :] = np.where(d0 >= 0, rb[t5_bucket_np(d0), hd], NEG)
            d1 = 128 + qq - kk_
            dtl[g, 1, :, hh, :] = rb[t5_bucket_np(d1), hd]
            d2 = 256 + qq - kk_
            dtl[g, 2, :, hh, :] = np.where(d2 < 256, rb[t5_bucket_np(d2), hd], NEG)
            dtl[g, 3, :, hh, :] = rb[31, hd]
            for qt in range(NT):
                dc = (128 * qt + qq) - (16 * n + 31)
                bcb[g, qt, :, hh, :] = np.where(dc >= 0, rb[t5_bucket_np(dc), hd], NEG)
    selc = np.zeros((NT, 128, 64), np.float32)
    m = np.arange(32)[None, :]
    for qt in range(NT):
        cur = ((128 * qt + np.arange(128)) // 64)[:, None]
        allowed = m <= cur
        forced = (m == 0) | (m == cur) | (m == cur - 1)
        selc[qt, :, 0:32] = allowed
        selc[qt, :, 32:64] = np.where(forced, 1e9, np.where(allowed, 0.0, -1e9))
    exc = (np.arange(S)[None, :] // 64 == np.arange(32)[:, None]).astype(np.float32)
    nn = np.arange(128)[:, None]
    mm_ = np.arange(32)[None, :]
    cov = ((16 * nn <= 64 * mm_ + 63) & (16 * nn + 31 >= 64 * mm_)).astype(np.float32)
    return (dtl.reshape(2, 4, 128, 512), bcb.reshape(2, NT, 128, 512), selc, exc, cov)


_CACHE = {}


def run(inputs, S, NL, B, flags, ncores=8):
    NSEQ = B // ncores
    packs = [pack_layer(inputs, l) for l in range(NL)]
    pars = [pack_params(inputs, l) for l in range(NL)]
    wsize = packs[0][0].size
    npar = pars[0][0].shape[1]
    prog = Prog(S, NL, NSEQ, [p[1] for p in packs], pars[0][1], wsize, npar, flags)
    x = np.asarray(inputs["x"], np.float32)
    p = np.asarray(inputs["p"], np.float32)
    xT = np.ascontiguousarray(x.transpose(0, 2, 1))
    pT = np.ascontiguousarray(p.transpose(0, 1, 3, 2))
    par = np.stack([pp[0] for pp in pars])
    cst = make_consts(S)
    dtl, bcb, selc, exc, cov = nsa_consts(inputs["rel_bias"], S)
    in_maps = []
    for c in range(ncores):
        m = {"xT": xT[c * NSEQ:(c + 1) * NSEQ], "pT": np.ascontiguousarray(pT[:NL, c * NSEQ:(c + 1) * NSEQ]),
             "par": par, "cst": cst, "dtl": dtl, "bcb": bcb, "selc": selc, "exc": exc, "cov": cov}
        for l in range(NL):
            m["wl%d" % l] = packs[l][0].reshape(-1, 1024)
        in_maps.append(m)
    res = run_bass_kernel_spmd(prog.k.nc, in_maps, core_ids=list(range(ncores)))
    outT = np.concatenate([r["outT"] for r in res.results], axis=0)
    return np.ascontiguousarray(outT.transpose(0, 2, 1)), prog


def kernel(**inputs):
    inputs = {k_: np.asarray(v) for k_, v in inputs.items()}
    B, S, _ = inputs["x"].shape
    out, _ = run(inputs, S, 4, B, {"mix0": True, "mix1": True, "mix2": True})
    return out.astype(np.float32)
```

```python
import numpy as np
from contextlib import ExitStack
import concourse.bass as bass
import concourse.mybir as mybir
from concourse.bass_utils import run_bass_kernel_spmd

F32 = mybir.dt.float32
BF16 = mybir.dt.bfloat16
I32 = mybir.dt.int32
U8 = mybir.dt.uint8
AF = mybir.ActivationFunctionType
ALU = mybir.AluOpType
AX = mybir.AxisListType

D = 1024
KC = 8
DFF = 2816
NJ = 22
PLE = 256
INCOLS = 8216
SEM_LIMIT = 30000


class Buf:
    def __init__(self, name, h, kind):
        self.name = name
        self.h = h
        self.kind = kind
        self.w = None
        self.w_eng = None
        self.r = {}
        self.dsem = None
        self.dcnt = 0

    def __getitem__(self, idx):
        return V(self, self.h[idx])


class V:
    def __init__(self, buf, ap):
        self.buf = buf
        self.ap = ap

    def __getitem__(self, idx):
        return V(self.buf, self.ap[idx])

    def rr(self, pat, **kw):
        return V(self.buf, self.ap.rearrange(pat, **kw))


class KB:
    def __init__(self):
        self.nc = bass.Bass("TRN2", target_bir_lowering=False)
        nc = self.nc
        self.es = ExitStack()
        self.E = {"pe": nc.tensor, "dve": nc.vector, "act": nc.scalar, "pool": nc.gpsimd, "sp": nc.sync}
        self.sems = []
        self.cur = {}
        self.cnt = {}
        self.seen = {e: {} for e in self.E}
        for e in self.E:
            self._newsem(e)
        self.nins = 0
        self.free_dsems = []
        self.semcnt = {}
        self.scopes = []

    def get_dsem(self, name):
        if self.free_dsems:
            return self.free_dsems.pop()
        s_ = self._alloc_sem(name)
        self.semcnt[s_] = 0
        return s_

    def _alloc_sem(self, name):
        s = self.es.enter_context(self.nc.semaphore(name))
        self.sems.append(s)
        return len(self.sems) - 1

    def _newsem(self, e):
        self.cur[e] = self._alloc_sem("s_%s_%d" % (e, len(self.sems)))
        self.cnt[e] = 0

    def sb(self, name, shape, dt, es=None):
        self.uid = getattr(self, "uid", 0) + 1
        name = "%s_%d" % (name, self.uid)
        h = (es or self.es).enter_context(self.nc.sbuf_tensor(name, list(shape), dt))
        b = Buf(name, h, "sb")
        if self.scopes:
            self.scopes[-1].append(b)
        return b

    def scope_begin(self):
        self.scopes.append([])

    def scope_end(self):
        bufs = self.scopes.pop()
        self.barrier(bufs)
        for b in bufs:
            if b.dsem is not None:
                self.free_dsems.append(b.dsem)
                b.dsem = None

    def ps(self, name, shape, dt):
        h = self.es.enter_context(self.nc.psum_tensor(name, list(shape), dt))
        return Buf(name, h, "ps")

    def dram(self, name, shape, dt, kind):
        h = self.nc.dram_tensor(name, list(shape), dt, kind=kind)
        b = Buf(name, h.ap(), "dram")
        return b

    def _wait(self, e, ev):
        if ev is None:
            return
        s, v = ev
        if self.seen[e].get(s, 0) >= v:
            return
        self.E[e].wait_ge(self.sems[s], v)
        self.seen[e][s] = v

    def _deps(self, e, reads, writes):
        for v in reads:
            b = v.buf
            if b.w is not None:
                self._wait(e, b.w)
        for v in writes:
            b = v.buf
            if b.w is not None and not (e == "pe" and b.w_eng == "pe"):
                if not (b.w_eng == e and e != "dma"):
                    self._wait(e, b.w)
            for s, val in b.r.items():
                if s == self.cur.get(e, -1):
                    continue
                self._wait(e, (s, val))

    def emit(self, e, fn, reads, writes):
        reads = [v for v in reads if isinstance(v, V)]
        self._deps(e, reads, writes)
        ins = fn()
        if self.cnt[e] >= SEM_LIMIT:
            self._newsem(e)
        self.cnt[e] += 1
        ins.then_inc(self.sems[self.cur[e]], 1)
        ev = (self.cur[e], self.cnt[e])
        for v in reads:
            b = v.buf
            if b.r.get(ev[0], 0) < ev[1]:
                b.r[ev[0]] = ev[1]
        for v in writes:
            b = v.buf
            b.w = ev
            b.w_eng = e
            b.r = {}
        self.nins += 1
        return ins

    def dma(self, q, out, in_, track=None, **kw):
        tb = track or (out.buf if out.buf.kind != "dramin" else in_.buf)
        if tb.dsem is None:
            tb.dsem = self.get_dsem("d_" + tb.name)
        reads = [in_] if in_.buf.kind != "dramin" else []
        writes = [out]
        for v in reads:
            if v.buf.w is not None:
                self._wait(q, v.buf.w)
        for v in writes:
            if v.buf.w is not None:
                self._wait(q, v.buf.w)
            for s, val in v.buf.r.items():
                self._wait(q, (s, val))
        ins = self.E[q].dma_start(out=out.ap, in_=in_.ap, **kw)
        ins.then_inc(self.sems[tb.dsem], 16)
        self.semcnt[tb.dsem] += 16
        ev = (tb.dsem, self.semcnt[tb.dsem])
        for v in reads:
            v.buf.r[ev[0]] = ev[1]
        out.buf.w = ev
        out.buf.w_eng = "dma"
        out.buf.r = {}
        self.nins += 1

    def barrier(self, bufs=()):
        evs = [(self.cur[e], self.cnt[e]) for e in self.E if self.cnt[e] > 0]
        for b in bufs:
            if b.w is not None:
                evs.append(b.w)
            evs.extend(b.r.items())
        for e in self.E:
            for ev in evs:
                if ev[0] == self.cur[e]:
                    continue
                self._wait(e, ev)

    def mm(self, out, lhsT, rhs, start=True, stop=True, sgc=False):
        return self.emit("pe", lambda: self.nc.tensor.matmul(out.ap, lhsT.ap, rhs.ap, start=start, stop=stop,
                                                             skip_group_check=sgc), [lhsT, rhs], [out])

    def tr(self, out, in_, ident):
        return self.emit("pe", lambda: self.nc.tensor.transpose(out.ap, in_.ap, ident.ap), [in_, ident], [out])

    def act(self, out, in_, func, bias=None, scale=None, accum=None):
        kw = {}
        rd = [in_]
        if bias is not None:
            kw["bias"] = bias.ap if isinstance(bias, V) else bias
            rd.append(bias)
        if scale is not None:
            kw["scale"] = scale.ap if isinstance(scale, V) else scale
            rd.append(scale)
        wr = [out]
        if accum is not None:
            kw["accum_out"] = accum.ap
            wr.append(accum)
        return self.emit("act", lambda: self.nc.scalar.activation(out=out.ap, in_=in_.ap, func=func, **kw), rd, wr)

    def ts(self, out, in0, s1, op0, s2=None, op1=None, eng="dve"):
        a1 = s1.ap if isinstance(s1, V) else s1
        a2 = s2.ap if isinstance(s2, V) else s2
        kw = {}
        if op1 is not None:
            kw["op1"] = op1
        E = self.E[eng]
        return self.emit(eng, lambda: E.tensor_scalar(out=out.ap, in0=in0.ap, scalar1=a1, scalar2=a2, op0=op0, **kw),
                         [in0, s1, s2], [out])

    def tt(self, out, in0, in1, op, eng="dve"):
        E = self.E[eng]
        return self.emit(eng, lambda: E.tensor_tensor(out=out.ap, in0=in0.ap, in1=in1.ap, op=op), [in0, in1], [out])

    def stt(self, out, in0, scalar, in1, op0, op1):
        a = scalar.ap if isinstance(scalar, V) else scalar
        return self.emit("dve", lambda: self.nc.vector.scalar_tensor_tensor(
            out=out.ap, in0=in0.ap, scalar=a, in1=in1.ap, op0=op0, op1=op1), [in0, scalar, in1], [out])

    def cp(self, out, in_, eng="dve"):
        if eng == "act":
            return self.emit("act", lambda: self.nc.scalar.copy(out=out.ap, in_=in_.ap), [in_], [out])
        E = self.E[eng]
        return self.emit(eng, lambda: E.tensor_copy(out=out.ap, in_=in_.ap), [in_], [out])

    def scan(self, out, d0, d1, init, op0, op1):
        a = init.ap if isinstance(init, V) else init
        return self.emit("dve", lambda: self.nc.vector.tensor_tensor_scan(
            out=out.ap, data0=d0.ap, data1=d1.ap, initial=a, op0=op0, op1=op1), [d0, d1, init], [out])

    def memset(self, out, val, eng="dve"):
        E = self.E[eng]
        return self.emit(eng, lambda: E.memset(out.ap, val), [], [out])

    def recip(self, out, in_):
        return self.emit("dve", lambda: self.nc.vector.reciprocal(out=out.ap, in_=in_.ap), [in_], [out])

    def max8(self, out, in_):
        return self.emit("dve", lambda: self.nc.vector.max(out=out.ap, in_=in_.ap), [in_], [out])

    def red(self, out, in_, op, axis=AX.X):
        return self.emit("dve", lambda: self.nc.vector.tensor_reduce(out=out.ap, in_=in_.ap, axis=axis, op=op),
                         [in_], [out])

    def cpred(self, out, mask, data):
        return self.emit("dve", lambda: self.nc.vector.copy_predicated(out=out.ap, mask=mask.ap, data=data.ap),
                         [mask, data, out], [out])


def blockify(W, colsets):
    K = W.shape[0]
    kc = K // 128
    out = []
    for cols in colsets:
        blk = W[:, cols]
        blk = blk.reshape(kc, 128, len(cols)).transpose(1, 0, 2)
        out.append(np.ascontiguousarray(blk).reshape(-1))
    return out


class Packer:
    def __init__(self):
        self.parts = []
        self.off = 0
        self.idx = {}

    def add(self, name, W, colsets):
        K = W.shape[0]
        blks = blockify(W, colsets)
        lst = []
        for b, cols in zip(blks, colsets):
            lst.append((self.off, K // 128, len(cols)))
            self.parts.append(b)
            self.off += b.size
        self.idx[name] = lst

    def finish(self, mult=16384):
        pad = (-self.off) % mult
        if pad:
            self.parts.append(np.zeros(pad, np.float32))
            self.off += pad
        return np.concatenate(self.parts)


def ar(a, n):
    return np.arange(a, a + n)


HG_OFF = 0
NSA_Q_OFF = 2048
NSA_KV_OFF = 2560
NSA_GATE_OFF = 3328
RW_OFF = 3352
MG_OFF = 5144


def pack_layer(inp, l):
    pk = Packer()
    for nm in ("ffn1", "ffn2"):
        wgu = inp[nm + "_wgu"][l]
        pk.add(nm + "_gu", wgu, [np.concatenate([ar(j * 128, 128), ar(DFF + j * 128, 128)]) for j in range(NJ)])
        wd = inp[nm + "_wd"][l]
        pk.add(nm + "_d", wd, [ar(m * 128, 128) for m in range(8)])
    w_in = inp["w_in"][l]
    pk.add("mg", w_in, [ar(MG_OFF + b * D + m * 128, 128) for b in range(3) for m in range(8)])
    wb = inp["w_branch"][l]
    for b in range(3):
        pk.add("br%d" % b, wb[b], [ar(m * 128, 128) for m in range(8)])
    pk.add("wout", inp["w_out"][l], [ar(m * 128, 128) for m in range(8)])
    pk.add("pgw", inp["ple_gate_w"][l], [ar(m * 128, 128) for m in range(8)])
    pk.add("plw", inp["ple_w"][l], [ar(m * 128, 128) for m in range(8)])
    pk.add("hg", w_in, [np.concatenate([ar(HG_OFF + t * 512 + hp * 128, 128) for t in range(4)]) for hp in range(4)])
    pk.add("nq", w_in, [ar(NSA_Q_OFF + g * 256, 256) for g in range(2)])
    pk.add("nkv", w_in, [np.concatenate([ar(NSA_KV_OFF + t * 128 + g * 64, 64) for t in range(6)]) for g in range(2)])
    pk.add("ngate", w_in, [ar(NSA_GATE_OFF, 24)])
    for kv in range(2):
        w1 = inp["cmp_w1"][l][kv]
        w1r = np.zeros((128, 4096), np.float32)
        w1r[0:64] = w1.reshape(32, 64, 128).transpose(1, 0, 2).reshape(64, 4096)
        pk.add("cw1_%d" % kv, w1r, [ar(0, 4096)])
        pk.add("cw1c_%d" % kv, w1, [ar(0, 128)])
        pk.add("cw2_%d" % kv, inp["cmp_w2"][l][kv], [ar(0, 64)])
    RW = RW_OFF
    pk.add("rwl", w_in, [ar(RW + 1536, 256)])
    pk.add("rw", w_in, [np.concatenate([ar(RW + t * 512 + hp * 128, 128) for t in range(3)]) for hp in range(4)])
    AB = np.concatenate([inp["rw_wB"][l], inp["rw_aB"][l]], axis=0)
    pk.add("rwAB", AB, [ar(hp * 128, 128) for hp in range(4)])
    pk.add("rwgB", inp["rw_gB"][l], [ar(hp * 128, 128) for hp in range(4)])
    flat = pk.finish()
    return flat, pk.idx


def pack_params(inp, l):
    cols = []
    idx = {}

    def add(name, vec):
        v = np.asarray(vec, np.float32).reshape(-1, 128).T
        idx[name] = (sum(c.shape[1] for c in cols), v.shape[1])
        cols.append(v)

    add("ffn1_norm", inp["ffn1_norm"][l])
    add("mix_norm", inp["mix_norm"][l])
    add("ffn2_norm", inp["ffn2_norm"][l])
    add("ple_norm", inp["ple_norm"][l])
    add("final_norm", inp["final_norm"])
    add("hg_norm", inp["hg_norm"][l])
    for j in range(4):
        add("hg_lb%d" % j, inp["hg_lb"][j])
    add("cmp_pe0", inp["cmp_pe"][l][0].reshape(-1))
    add("cmp_pe1", inp["cmp_pe"][l][1].reshape(-1))
    add("rw_mu", inp["rw_mu"][l])
    for nm in ("rw_w0", "rw_a0", "rw_kk", "rw_ka", "rw_ln_w", "rw_ln_b"):
        add(nm, inp[nm][l])
    add("rw_rk", inp["rw_rk"][l].reshape(-1))
    return np.ascontiguousarray(np.concatenate(cols, axis=1)), idx


class Prog:
    def __init__(self, S, NL, NSEQ, widx, pidx, wsize, npar, flags):
        self.S, self.NL, self.NSEQ = S, NL, NSEQ
        self.widx, self.pidx = widx, pidx
        self.flags = flags
        k = self.k = KB()
        nc = k.nc
        self.NT = S // 512
        self.xT = k.dram("xT", [NSEQ, D, S], F32, "ExternalInput")
        self.xT.kind = "dramin"
        self.pT = k.dram("pT", [NL, NSEQ, PLE, S], F32, "ExternalInput")
        self.pT.kind = "dramin"
        self.wl = []
        self.ws = []
        for l in range(NL):
            b = k.dram("wl%d" % l, [wsize // 1024, 1024], F32, "ExternalInput")
            b.kind = "dramin"
            self.wl.append(b)
            self.ws.append(k.dram("ws%d" % l, [wsize // 1024, 1024], BF16, "Internal"))
        self.par_d = k.dram("par", [NL, 128, npar], F32, "ExternalInput")
        self.par_d.kind = "dramin"
        self.NCST = 896 + S + 768
        self.cst_d = k.dram("cst", [128, self.NCST], F32, "ExternalInput")
        self.cst_d.kind = "dramin"
        self.outT = k.dram("outT", [NSEQ, D, S], F32, "ExternalOutput")
        NTq = S // 128
        self.dtl_d = k.dram("dtl", [2, 4, 128, 512], F32, "ExternalInput")
        self.bcb_d = k.dram("bcb", [2, NTq, 128, 512], F32, "ExternalInput")
        self.selc_d = k.dram("selc", [NTq, 128, 64], F32, "ExternalInput")
        self.exc_d = k.dram("exc", [32, S], F32, "ExternalInput")
        self.cov_d = k.dram("cov", [128, 32], F32, "ExternalInput")
        for b_ in (self.dtl_d, self.bcb_d, self.selc_d, self.exc_d, self.cov_d):
            b_.kind = "dramin"
        self.hT = k.sb("hT", [128, KC, S], F32)
        self.uT = k.sb("uT", [128, KC, S], BF16)
        self.par = k.sb("par_sb", [128, NL, npar], F32)
        self.cst = k.sb("cst_sb", [128, 896], F32)
        self.cst2 = k.sb("cst2_sb", [128, 768], F32)
        self.ones_bf = k.sb("ones_bf", [128, 128], BF16)
        self.blk_bf = k.sb("blk_bf", [128, 128], BF16)
        self.ident_bf = k.sb("ident_bf", [128, 128], BF16)
        self.cmask = k.sb("cmask", [128, S], BF16)
        self.oml = k.sb("oml", [128, 4, NL], F32)
        self.psb = [k.ps("ps%d" % i, [128, 512], F32) for i in range(8)]
        self.psi = 0
        self.wbufs = {}
        self.build()

    def ps(self):
        if getattr(self, "ps_fixed", None) is not None:
            return self.psb[self.ps_fixed]
        if getattr(self, "ps_stream", None) is not None:
            st = self.ps_stream
            st[1] += 1
            return self.psb[st[0] + st[1] % 3]
        p = self.psb[self.psi % getattr(self, "psn", 6)]
        self.psi += 1
        return p

    def P(self, l, name):
        o, n = self.pidx[name]
        return self.par[:, l, o:o + n]

    def wbuf(self, tag, shape, nbuf=3):
        if tag not in self.wbufs:
            self.wbufs[tag] = [[self.k.sb("w_%s_%d" % (tag, i), shape, BF16, es=self.pes) for i in range(nbuf)], 0]
        lst = self.wbufs[tag]
        b = lst[0][lst[1] % len(lst[0])]
        lst[1] += 1
        return b

    def wload(self, l, name, j, tag, nbuf=3):
        off, kc, nb = self.widx[l][name][j]
        b = self.wbuf(tag, [128, kc, nb], nbuf)
        flat = self.ws[l].h.rearrange("a b -> (a b)")
        src = flat[off:off + 128 * kc * nb].rearrange("(p k n) -> p k n", p=128, k=kc)
        if l == 0:
            n = 128 * kc * nb
            ra, rb = off // 1024, (off + n - 1) // 1024
            cbs = [cb for r0, cb in self.wchunks.items() if r0 <= rb and r0 + 2048 > ra]
            for cb in cbs[:-1]:
                self.k._wait("sp", cb.w)
            self.k.dma("sp", b[:, :, :], V(cbs[-1], src))
        else:
            self.k.dma("sp", b[:, :, :], V(self.ws[l], src))
        return b

    def cast_weights(self, l):
        k = self.k
        R = self.wl[l].h.shape[0]
        step = 2048
        self.wchunks = getattr(self, "wchunks", {})
        for r0 in range(0, R, step):
            r1 = min(R, r0 + step)
            if l == 0:
                cb = Buf("ws0_c%d" % r0, self.ws[l].h, "dram")
                self.wchunks[r0] = cb
                k.dma("pool", V(cb, self.ws[l].h[r0:r1, :]), V(self.wl[l], self.wl[l].h[r0:r1, :]))
            else:
                k.dma("pool", V(self.ws[l], self.ws[l].h[r0:r1, :]), V(self.wl[l], self.wl[l].h[r0:r1, :]))

    def rmsnorm(self, g, t0, n, out, out_t0, eps=1e-6):
        k = self.k
        for s0 in range(0, n, 512):
            ts_ = slice(t0 + s0, t0 + s0 + 512)
            pp = self.ps()
            for kc in range(KC):
                sq = self.wbuf_f("sq", [128, 512], BF16)
                k.act(sq[:, :], self.hT[:, kc, ts_], AF.Square)
                k.mm(pp[:, :], self.ones_bf[:, :], sq[:, :], start=(kc == 0), stop=(kc == KC - 1))
            rs = self.wbuf_f("rstd", [128, 512], F32, 2)
            k.act(rs[:, :], pp[:, :], AF.Sqrt, bias=self.cst[:, 130:131], scale=1.0 / D)
            k.recip(rs[:, :], rs[:, :])
            os_ = slice(out_t0 + s0, out_t0 + s0 + 512)
            for kc in range(KC):
                k.stt(out[:, kc, os_], self.hT[:, kc, ts_], g[:, kc:kc + 1], rs[:, :], ALU.mult, ALU.mult)

    def wbuf_f(self, tag, shape, dt, nbuf=3):
        key = "f_" + tag
        if key not in self.wbufs:
            self.wbufs[key] = [[self.k.sb("t_%s_%d" % (tag, i), shape, dt, es=self.pes) for i in range(nbuf)], 0]
        lst = self.wbufs[key]
        b = lst[0][lst[1] % len(lst[0])]
        lst[1] += 1
        return b

    def rmsnorm_gen(self, g, t0, n, out, out_t0):
        k = self.k
        for s0 in range(0, n, 512):
            ts_ = slice(t0 + s0, t0 + s0 + 512)
            pp = self.psb[7]
            for kc in range(KC):
                sq = self.wbuf_f("sq", [128, 512], BF16)
                k.act(sq[:, :], self.hT[:, kc, ts_], AF.Square)
                k.mm(pp[:, :], self.ones_bf[:, :], sq[:, :], start=(kc == 0), stop=(kc == KC - 1))
                yield
            rs = self.wbuf_f("rstd", [128, 512], F32, 2)
            k.act(rs[:, :], pp[:, :], AF.Sqrt, bias=self.cst[:, 130:131], scale=1.0 / D)
            k.recip(rs[:, :], rs[:, :])
            os_ = slice(out_t0 + s0, out_t0 + s0 + 512)
            for kc in range(KC):
                k.stt(out[:, kc, os_], self.hT[:, kc, ts_], g[:, kc:kc + 1], rs[:, :], ALU.mult, ALU.mult)
                yield

    def ffn_all(self, l, nm, gname, TT):
        k = self.k
        S = self.S
        g = self.P(l, gname)
        tiles = list(range(0, S, TT))
        nh = len(tiles)
        full = self.uT.h
        uh = [Buf("uT_h%d" % i, full[:, :, t0:t0 + TT], "sb") for i, t0 in enumerate(tiles)]
        actT = k.sb("actT", [128, NJ, TT], BF16, es=self.pes)
        for _ in self.rmsnorm_gen(g, tiles[0], TT, uh[0], 0):
            pass
        for ti, t0 in enumerate(tiles):
            u = uh[ti]
            gen = self.rmsnorm_gen(g, tiles[ti + 1], TT, uh[ti + 1], 0) if ti + 1 < nh else None
            for j in range(NJ):
                w = self.wload(l, nm + "_gu", j, "gu")
                for s0 in range(0, TT, 512):
                    pg, pu = self.ps(), self.ps()
                    for kc in range(KC):
                        k.mm(pg[:, :], w[:, kc, 0:128], u[:, kc, s0:s0 + 512], start=(kc == 0), stop=(kc == KC - 1))
                    for kc in range(KC):
                        k.mm(pu[:, :], w[:, kc, 128:256], u[:, kc, s0:s0 + 512], start=(kc == 0), stop=(kc == KC - 1))
                    sg = self.wbuf_f("sg", [128, 512], F32)
                    k.act(sg[:, :], pg[:, :], AF.Silu)
                    k.tt(actT[:, j, s0:s0 + 512], sg[:, :], pu[:, :], ALU.mult)
                    if gen is not None:
                        for _ in range(2):
                            try:
                                next(gen)
                            except StopIteration:
                                gen = None
                                break
            if gen is not None:
                for _ in gen:
                    pass
            for m in range(8):
                w = self.wload(l, nm + "_d", m, "wd", 2)
                for s0 in range(0, TT, 512):
                    ts_ = slice(t0 + s0, t0 + s0 + 512)
                    po = self.ps()
                    for j in range(NJ):
                        k.mm(po[:, :], w[:, j, :], actT[:, j, s0:s0 + 512], start=(j == 0), stop=(j == NJ - 1))
                    k.stt(self.hT[:, m, ts_], po[:, :], 0.5, self.hT[:, m, ts_], ALU.mult, ALU.add)

    def ple(self, l, q, t0, n):
        k = self.k
        u = self.uT
        self.rmsnorm(self.P(l, "ple_norm"), t0, n, u, t0)
        pf = k.sb("pf", [128, 2, n], F32, es=self.pes)
        pb = k.sb("pb", [128, 2, n], BF16, es=self.pes)
        src = self.pT.h[l, q].rearrange("(c p) t -> p c t", p=128)[:, :, t0:t0 + n]
        k.dma("sp", pf[:, :, :], V(self.pT, src))
        k.cp(pb[:, :, :], pf[:, :, :], eng="act")
        for m in range(8):
            wg = self.wload(l, "pgw", m, "w8", 3)
            wp = self.wload(l, "plw", m, "w2", 2)
            for s0 in range(0, n, 512):
                ts_ = slice(t0 + s0, t0 + s0 + 512)
                pg, pp = self.ps(), self.ps()
                for kc in range(KC):
                    k.mm(pg[:, :], wg[:, kc, :], u[:, kc, ts_], start=(kc == 0), stop=(kc == KC - 1))
                for c in range(2):
                    k.mm(pp[:, :], wp[:, c, :], pb[:, c, s0:s0 + 512], start=(c == 0), stop=(c == 1))
                sg = self.wbuf_f("sg", [128, 512], F32)
                k.act(sg[:, :], pg[:, :], AF.Sigmoid)
                k.tt(sg[:, :], sg[:, :], pp[:, :], ALU.mult)
                k.tt(self.hT[:, m, ts_], self.hT[:, m, ts_], sg[:, :], ALU.add)

    def merge_branch(self, l, b, oT, merged, first):
        k = self.k
        S = self.S
        for m in range(8):
            wb = self.wload(l, "br%d" % b, m, "w4", 2)
            wg = self.wload(l, "mg", b * 8 + m, "w8", 3)
            for t0 in range(0, S, 512):
                ts_ = slice(t0, t0 + 512)
                pa, pg = self.ps(), self.ps()
                for c in range(4):
                    k.mm(pa[:, :], wb[:, c, :], oT[:, c, ts_], start=(c == 0), stop=(c == 3))
                for kc in range(KC):
                    k.mm(pg[:, :], wg[:, kc, :], self.uT[:, kc, ts_], start=(kc == 0), stop=(kc == KC - 1))
                sg = self.wbuf_f("sg", [128, 512], F32)
                k.act(sg[:, :], pg[:, :], AF.Sigmoid)
                if first:
                    k.tt(merged[:, m, ts_], sg[:, :], pa[:, :], ALU.mult)
                else:
                    k.tt(sg[:, :], sg[:, :], pa[:, :], ALU.mult)
                    k.tt(merged[:, m, ts_], merged[:, m, ts_], sg[:, :], ALU.add)

    def out_proj(self, l, merged):
        k = self.k
        for m in range(8):
            w = self.wload(l, "wout", m, "w8", 3)
            for t0 in range(0, self.S, 512):
                ts_ = slice(t0, t0 + 512)
                po = self.ps()
                for kc in range(KC):
                    k.mm(po[:, :], w[:, kc, :], merged[:, kc, ts_], start=(kc == 0), stop=(kc == KC - 1))
                k.tt(self.hT[:, m, ts_], self.hT[:, m, ts_], po[:, :], ALU.add)

    def phase_begin(self):
        self.pes = ExitStack()
        self.wbufs = {}
        self.k.scope_begin()

    def phase_end(self):
        self.k.scope_end()
        self.pes.close()
        self.wbufs = {}

    def mixers(self, l, q):
        k = self.k
        S = self.S
        self.phase_begin()
        self.rmsnorm(self.P(l, "mix_norm"), 0, S, self.uT, 0)
        self.phase_end()
        self.phase_begin()
        oT = k.sb("oT", [128, 4, S], BF16, es=self.pes)
        merged = None
        first = True
        for b in (2, 0, 1):
            if not self.flags.get("mix%d" % b, False):
                continue
            mes = ExitStack()
            save = self.pes, self.wbufs
            self.pes, self.wbufs = mes, {}
            k.scope_begin()
            [self.hgrn2, self.nsa, self.rwkv][b](l, q, oT)
            k.scope_end()
            mes.close()
            self.pes, self.wbufs = save
            if merged is None:
                merged = k.sb("merged", [128, KC, S], BF16, es=self.pes)
            mes = ExitStack()
            self.pes, self.wbufs = mes, {}
            k.scope_begin()
            self.merge_branch(l, b, oT, merged, first)
            k.scope_end()
            mes.close()
            self.pes, self.wbufs = save
            first = False
        if not first:
            self.out_proj(l, merged)
        self.phase_end()


    def prep_lb(self):
        k = self.k
        NL = self.NL
        e = k.sb("lb_e", [128, 4, 4], F32)
        ssum = k.sb("lb_s", [128, 4], F32)
        for j in range(4):
            k.act(e[:, :, j], self.P(0, "hg_lb%d" % j), AF.Exp)
        k.tt(ssum[:, :], e[:, :, 0], e[:, :, 1], ALU.add)
        k.tt(ssum[:, :], ssum[:, :], e[:, :, 2], ALU.add)
        k.tt(ssum[:, :], ssum[:, :], e[:, :, 3], ALU.add)
        k.recip(ssum[:, :], ssum[:, :])
        acc = k.sb("lb_acc", [128, 4], F32)
        k.memset(acc[:, :], 1.0)
        for l in range(NL):
            if l > 0:
                tmp = k.sb("lb_t%d" % l, [128, 4], F32)
                k.tt(tmp[:, :], e[:, :, l], ssum[:, :], ALU.mult)
                k.tt(acc[:, :], acc[:, :], tmp[:, :], ALU.subtract)
            k.cp(self.oml[:, :, l], acc[:, :])

    def psbf(self):
        p = self.ps()
        return V(p, p.h[:, :].bitcast(BF16))

    def hgrn2(self, l, q, oT):
        k = self.k
        S = self.S
        BW = 256
        NCH = BW // 64
        cmk = self.cst[:, 384:384 + BW]
        one = self.cst[:, 128:129]
        gn = self.P(l, "hg_norm")

        def stream(hp, sid):
            sfx = "_%d" % sid
            T2 = lambda nm, dt=F32, w=BW: k.sb(nm + sfx, [128, w], dt, es=self.pes)
            qT, kT, bT, sgT = T2("hq"), T2("hk"), T2("hb"), T2("hsg")
            dT, eT, q1, k1, q2 = T2("hd"), T2("he"), T2("hq1"), T2("hk1"), T2("hq2")
            vb, k2b, vtok, k2tok, scb, o2 = (T2("hvb", BF16), T2("hk2b", BF16), T2("hvtok", BF16), T2("hk2tok", BF16),
                                             T2("hscb", BF16), T2("ho2", BF16))
            dec = T2("hdec", F32, NCH)
            rs, tmp = T2("hrs"), T2("htmp")
            state = T2("hstate", F32, 64)
            pst = [3 * sid, 0]
            psO = self.psb[6 + sid]

            def PS():
                self.ps_stream = pst
                p = self.ps()
                self.ps_stream = None
                return p

            def PSBF():
                p = PS()
                return V(p, p.h[:, :].bitcast(BF16))

            def sigm(out, in_, sign):
                k.act(out, in_, AF.Exp, scale=-1.0 * sign)
                k.act(out, out, AF.Ln, bias=one)
                k.act(out, out, AF.Exp, scale=-1.0)

            w = self.wload(l, "hg", hp, "whg" + sfx, 1)
            k.memset(state[:, :], 0.0)
            units = [(slice(h * 64, h * 64 + 64), slice(c * 64, c * 64 + 64)) for c in range(NCH) for h in range(2)]

            def proj(i, ts_):
                pp = PS()
                for kc in range(KC):
                    k.mm(pp[:, 0:BW], w[:, kc, i * 128:(i + 1) * 128], self.uT[:, kc, ts_], start=(kc == 0), stop=(kc == KC - 1))
                return pp

            for t0 in range(0, S, BW):
                ts_ = slice(t0, t0 + BW)
                pq = proj(0, ts_)
                yield
                sigm(qT[:, :], pq[:, 0:BW], 1.0)
                k.tt(qT[:, :], qT[:, :], pq[:, 0:BW], ALU.mult)
                pg = proj(3, ts_)
                yield
                sigm(sgT[:, :], pg[:, 0:BW], 1.0)
                k.tt(sgT[:, :], sgT[:, :], pg[:, 0:BW], ALU.mult)
                pf = proj(1, ts_)
                yield
                sigm(kT[:, :], pf[:, 0:BW], -1.0)
                k.ts(kT[:, :], kT[:, :], self.oml[:, hp, l:l + 1], ALU.mult)
                k.act(dT[:, :], kT[:, :], AF.Ln, bias=one, scale=-1.0)
                pi = proj(2, ts_)
                yield
                k.cp(vb[:, :], pi[:, 0:BW], eng="act")
                k.scan(bT[:, :], self.cmask[:, ts_], dT[:, :], 0.0, ALU.mult, ALU.add)
                b3 = bT[:, :].rr("p (c t) -> p c t", t=64)
                bmid = V(bT, b3.ap[:, :, 31:32].broadcast_to([128, NCH, 64]))
                bend = V(bT, b3.ap[:, :, 63:64].broadcast_to([128, NCH, 64]))
                d3 = dT[:, :].rr("p (c t) -> p c t", t=64)
                k.tt(d3, b3, bmid, ALU.subtract)
                k.act(eT[:, :], dT[:, :], AF.Exp)
                k.tt(q1[:, :], qT[:, :], eT[:, :], ALU.mult)
                yield
                k.act(eT[:, :], dT[:, :], AF.Exp, scale=-1.0)
                k.tt(k1[:, :], kT[:, :], eT[:, :], ALU.mult)
                k.act(eT[:, :], bT[:, :], AF.Exp)
                k.tt(q2[:, :], qT[:, :], eT[:, :], ALU.mult)
                yield
                k.tt(d3, bend, b3, ALU.subtract)
                k.act(eT[:, :], dT[:, :], AF.Exp)
                k.tt(k2b[:, :], kT[:, :], eT[:, :], ALU.mult)
                k.act(dec[:, :], b3[:, :, 63], AF.Exp)
                yield
                pv = PSBF()
                for hs, cs in units:
                    k.tr(pv[hs, cs], vb[hs, cs], self.ident_bf[hs, hs])
                pk2 = PSBF()
                for hs, cs in units:
                    k.tr(pk2[hs, cs], k2b[hs, cs], self.ident_bf[hs, hs])
                yield
                k.cp(vtok[:, :], pv[:, 0:BW], eng="act")
                k.cp(k2tok[:, :], pk2[:, 0:BW])
                psS = PS()
                for hs, cs in units:
                    k.mm(psS[hs, cs], k1[hs, cs], q1[hs, cs])
                yield
                k.tt(scb[:, :], psS[:, 0:BW], cmk, ALU.mult)
                for c in range(NCH):
                    cs = slice(c * 64, c * 64 + 64)
                    psH = PS()
                    for h in range(2):
                        hs = slice(h * 64, h * 64 + 64)
                        k.mm(psO[hs, cs], vtok[hs, cs], scb[hs, cs], start=True, stop=False)
                        k.mm(psO[hs, cs], state[hs, :], q2[hs, cs], start=False, stop=True)
                        k.mm(psH[hs, 0:64], k2tok[hs, cs], vtok[hs, cs])
                    yield
                    k.stt(state[:, :], state[:, :], dec[:, c:c + 1], psH[:, 0:64], ALU.mult, ALU.add)
                k.cp(tmp[:, :], psO[:, 0:BW], eng="act")
                k.tt(o2[:, :], tmp[:, :], tmp[:, :], ALU.mult)
                psN = PS()
                k.mm(psN[:, 0:BW], self.blk_bf[:, :], o2[:, :])
                yield
                k.act(rs[:, :], psN[:, 0:BW], AF.Ln, bias=self.cst[:, 130:131], scale=1.0 / 64)
                k.act(rs[:, :], rs[:, :], AF.Exp, scale=-0.5)
                k.stt(tmp[:, :], tmp[:, :], gn[:, hp:hp + 1], rs[:, :], ALU.mult, ALU.mult)
                k.tt(oT[:, hp, ts_], tmp[:, :], sgT[:, :], ALU.mult)
                yield

        for pair in ((0, 1), (2, 3)):
            pes_ = ExitStack()
            k.scope_begin()
            save = self.pes, self.wbufs
            self.pes, self.wbufs = pes_, {}
            alive = [stream(pair[0], 0), stream(pair[1], 1)]
            if self.flags.get("hg1"):
                for g_ in alive:
                    for _ in g_:
                        pass
                alive = []
            while alive:
                for g_ in list(alive):
                    try:
                        next(g_)
                    except StopIteration:
                        alive.remove(g_)
            k.scope_end()
            pes_.close()
            self.pes, self.wbufs = save

    def nsa(self, l, q, oT):
        k = self.k
        S = self.S
        NT = S // 128
        NC = S // 16 - 1
        self.psn = 4
        es = self.pes
        sbt = lambda nm, shape, dt=F32: k.sb(nm, shape, dt, es=es)
        Ex = sbt("nEx", [128, S], BF16)
        k.memset(Ex[:, :], 0.0)
        k.dma("pool", Ex[0:32, :], V(self.exc_d, self.exc_d.h[:, :]))
        KsT, KwT = sbt("nKsT", [128, S], BF16), sbt("nKwT", [128, S], BF16)
        for t_ in (KsT, KwT):
            k.memset(t_[64:128, :], 0.0)
            k.memset(t_[64:65, :], 1.0)
        Vs, Vw = sbt("nVs", [128, NT, 65], BF16), sbt("nVw", [128, NT, 65], BF16)
        kcT = sbt("nkcT", [128, 128], BF16)
        k.memset(kcT[:, :], 0.0)
        Vc = sbt("nVc", [128, 97], BF16)
        pe_bf = sbt("npe", [128, 2, 16], BF16)
        cb = sbt("ncb", [128, 2])
        k.cp(pe_bf[:, 0, :], self.P(l, "cmp_pe0"))
        k.cp(pe_bf[:, 1, :], self.P(l, "cmp_pe1"))
        for g in range(2):
            ces = ExitStack()
            k.scope_begin()
            save = self.pes, self.wbufs
            self.pes, self.wbufs = ces, {}
            wkv = self.wload(l, "nkv", g, "wnkv", 1)
            xc = [k.sb("nxc%d" % i, [128, S], BF16, es=ces) for i in range(2)]
            for t_ in xc:
                k.memset(t_[64:128, :], 0.0)
            hid = k.sb("nhid", [128, 128], BF16, es=ces)
            k.memset(Vs[:, :, 64:65], 1.0)
            k.memset(Vw[:, :, 64:65], 1.0)
            for t0 in range(0, S, 512):
                ts_ = slice(t0, t0 + 512)
                for i, dst in ((0, xc[0]), (1, xc[1]), (2, KsT), (4, KwT)):
                    pp = self.ps()
                    for kc in range(KC):
                        k.mm(pp[0:64, :], wkv[:, kc, i * 64:(i + 1) * 64], self.uT[:, kc, ts_], start=(kc == 0), stop=(kc == KC - 1))
                    k.cp(dst[0:64, ts_], pp[0:64, :], eng=("act" if i % 4 == 0 else "dve"))
            for qt in range(NT):
                qs = slice(qt * 128, qt * 128 + 128)
                for i, dst in ((3, Vs), (5, Vw)):
                    pp = self.ps()
                    for kc in range(KC):
                        k.mm(pp[:, 0:64], self.uT[:, kc, qs], wkv[:, kc, i * 64:(i + 1) * 64], start=(kc == 0), stop=(kc == KC - 1))
                    k.cp(dst[:, qt, 0:64], pp[:, 0:64], eng=("act" if i == 3 else "dve"))
            for kv in range(2):
                w1 = self.wload(l, "cw1_%d" % kv, 0, "wcw1", 1)
                w1c = self.wload(l, "cw1c_%d" % kv, 0, "wcw1c", 1)
                w2 = self.wload(l, "cw2_%d" % kv, 0, "wcw2", 1)
                pc_ = self.ps()
                for kc in range(16):
                    k.mm(pc_[:, 0:1], w1c[:, kc, :], pe_bf[:, kv, kc:kc + 1], start=(kc == 0), stop=(kc == 15))
                k.cp(cb[:, kv:kv + 1], pc_[:, 0:1])
                ph = self.ps()
                for l_ in range(32):
                    k.mm(ph[:, 0:NC], w1[:, 0, l_ * 128:(l_ + 1) * 128], xc[kv][:, l_:l_ + 16 * (NC - 1) + 1:16],
                         start=(l_ == 0), stop=(l_ == 31))
                k.act(hid[:, 0:NC], ph[:, 0:NC], AF.Silu, bias=cb[:, kv:kv + 1])
                po = self.ps()
                if kv == 0:
                    k.mm(po[0:64, 0:NC], w2[:, 0, :], hid[:, 0:NC])
                    k.cp(kcT[0:64, 0:NC], po[0:64, 0:NC])
                else:
                    k.mm(po[0:NC, 0:64], hid[:, 0:NC], w2[:, 0, :])
                    k.cp(Vc[0:NC, 0:64], po[0:NC, 0:64])
                    k.memset(Vc[:, 64:65], 1.0)
                    k.dma("pool", Vc[:, 65:97], V(self.cov_d, self.cov_d.h[:, :]))
            k.scope_end()
            ces.close()
            self.pes, self.wbufs = save
            aes = ExitStack()
            k.scope_begin()
            save = self.pes, self.wbufs
            self.pes, self.wbufs = aes, {}
            sba = lambda nm, shape, dt=F32: k.sb(nm, shape, dt, es=aes)
            Dt = sba("nDt", [128, 4, 512])
            BCb = [sba("nBC%d" % i, [128, 512]) for i in range(2)]
            PTb = [sba("nPT%d" % i, [128, 512], BF16) for i in range(3)]
            QTb = [sba("nQT%d" % i, [128, 512], BF16) for i in range(2)]
            Ocb = [sba("nOc%d" % i, [128, 4, 97]) for i in range(2)]
            Osw = [sba("nOsw%d" % i, [128, 4, 2, 65]) for i in range(2)]
            seltb = [sba("nselc%d" % i, [128, 64]) for i in range(2)]
            Gb = [sba("nG%d" % i, [128, 12]) for i in range(2)]
            rzb = [sba("nrz%d" % i, [128, 12]) for i in range(2)]
            selT4b = [sba("nselT%d" % i, [128, 512], BF16) for i in range(2)]
            for t_ in selT4b:
                k.memset(t_[:, :], 0.0)
            imp, sc, m8 = sba("nimp", [128, 32]), sba("nsc", [128, 32]), sba("nm8", [128, 8])
            selb = sba("nselb", [128, 32], BF16)
            otok = sba("notok", [128, 256], BF16)
            coef = sba("ncoef", [128, 12])
            tmpo = sba("ntmpo", [128, 64])
            wgate = self.wload(l, "ngate", 0, "wng", 1)
            wq = self.wload(l, "nq", g, "wnq", 1)
            k.dma("sp", Dt[:, :, :], V(self.dtl_d, self.dtl_d.h[g].rearrange("j p n -> p j n")))
            for par in range(2):
                k.memset(QTb[par][64:128, :], 0.0)
                k.dma("pool", QTb[par][64:65, :], V(self.dtl_d, self.dtl_d.h[g, 3, 64:65, :]))
            for j in range(3):
                k.tt(Dt[:, j, :], Dt[:, j, :], Dt[:, 3, :], ALU.subtract)
            Awb = [self.psb[4], self.psb[6]]
            Asb = [self.psb[5], self.psb[7]]

            def prologue(qt):
                par = qt % 2
                qs = slice(qt * 128, qt * 128 + 128)
                BC, selt, QT, Oc, G, rz, selT4 = BCb[par], seltb[par], QTb[par], Ocb[par], Gb[par], rzb[par], selT4b[par]
                k.dma("sp", BC[:, :], V(self.bcb_d, self.bcb_d.h[g, qt]))
                k.dma("sp", selt[:, :], V(self.selc_d, self.selc_d.h[qt]))
                yield
                pq = self.ps()
                for hh in range(4):
                    for kc in range(KC):
                        k.mm(pq[0:64, hh * 128:(hh + 1) * 128], wq[:, kc, hh * 64:(hh + 1) * 64], self.uT[:, kc, qs],
                             start=(kc == 0), stop=(kc == KC - 1))
                    yield
                k.act(QT[0:64, :], pq[0:64, :], AF.Copy, scale=0.125)
                pg_ = self.ps()
                for kc in range(KC):
                    k.mm(pg_[:, 0:12], self.uT[:, kc, qs], wgate[:, kc, g * 12:(g + 1) * 12], start=(kc == 0), stop=(kc == KC - 1))
                k.act(G[:, :], pg_[:, 0:12], AF.Exp, scale=-1.0)
                k.ts(G[:, :], G[:, :], 1.0, ALU.add)
                k.recip(G[:, :], G[:, :])
                yield
                pl = self.ps()
                k.mm(pl[0:NC, :], kcT[:, 0:NC], QT[:, :])
                PT = PTb[2]
                k.tt(BC[0:NC, :], pl[0:NC, :], BC[0:NC, :], ALU.add)
                yield
                k.act(PT[0:NC, :], BC[0:NC, :], AF.Exp)
                yield
                po = self.ps()
                for hh in range(4):
                    k.mm(po[:, hh * 97:(hh + 1) * 97], PT[0:NC, hh * 128:(hh + 1) * 128], Vc[0:NC, :])
                yield
                k.cp(Oc[:, :, :], po[:, 0:388].rr("p (h n) -> p h n", n=97))
                k.ts(rz[:, 0:4], Oc[:, :, 64], 1e-30, ALU.max)
                k.recip(rz[:, 0:4], rz[:, 0:4])
                yield
                k.ts(imp[:, :], Oc[:, 0, 65:97], rz[:, 0:1], ALU.mult)
                for hh in range(1, 4):
                    k.stt(imp[:, :], Oc[:, hh, 65:97], rz[:, hh:hh + 1], imp[:, :], ALU.mult, ALU.add)
                yield
                k.tt(sc[:, :], imp[:, :], selt[:, 0:32], ALU.mult)
                k.tt(sc[:, :], sc[:, :], selt[:, 32:64], ALU.add)
                k.max8(m8[:, :], sc[:, :])
                yield
                k.ts(sc[:, :], sc[:, :], m8[:, 7:8], ALU.is_ge)
                k.ts(selb[:, :], sc[:, :], 30000.0, ALU.mult, -30000.0, ALU.add)
                yield
                pst = self.psbf()
                for hh in range(4):
                    k.tr(pst[0:32, hh * 128:(hh + 1) * 128], selb[:, :], self.ident_bf[:, :])
                k.cp(selT4[0:32, :], pst[0:32, 0:512])
                yield

            def epilogue(qt):
                par = qt % 2
                qs = slice(qt * 128, qt * 128 + 128)
                Oc, G, rz, OO = Ocb[par], Gb[par], rzb[par], Osw[par]
                k.cp(OO[:, :, 0, :], Awb[par][:, 0:260].rr("p (h n) -> p h n", n=65), eng="act")
                k.cp(OO[:, :, 1, :], Asb[par][:, 0:260].rr("p (h n) -> p h n", n=65), eng="dve")
                yield
                k.ts(rz[:, 4:8], OO[:, :, 1, 64], 1e-30, ALU.max)
                k.ts(rz[:, 8:12], OO[:, :, 0, 64], 1e-30, ALU.max)
                k.recip(rz[:, 4:12], rz[:, 4:12])
                G3 = G[:, :].rr("p (h j) -> p h j", j=3)
                for j in range(3):
                    k.tt(coef[:, j * 4:(j + 1) * 4], G3[:, :, j], rz[:, j * 4:(j + 1) * 4], ALU.mult)
                yield
                for hh in range(4):
                    k.ts(tmpo[:, 0:64], Oc[:, hh, 0:64], coef[:, hh:hh + 1], ALU.mult)
                    k.stt(tmpo[:, 0:64], OO[:, hh, 1, 0:64], coef[:, 4 + hh:5 + hh], tmpo[:, 0:64], ALU.mult, ALU.add)
                    k.stt(otok[:, hh * 64:(hh + 1) * 64], OO[:, hh, 0, 0:64], coef[:, 8 + hh:9 + hh], tmpo[:, 0:64], ALU.mult, ALU.add)
                    yield
                for j in range(2):
                    ptr = V(self.psb[2], self.psb[2].h[:, :].bitcast(BF16))
                    k.tr(ptr[:, 0:128], otok[:, j * 128:(j + 1) * 128], self.ident_bf[:, :])
                    k.cp(oT[:, g * 2 + j, qs], ptr[:, 0:128], eng=("act" if j else "dve"))
                yield

            def drain(gen):
                if gen is not None:
                    for _ in gen:
                        pass

            def step(gen):
                if gen is not None:
                    try:
                        next(gen)
                    except StopIteration:
                        return None
                return gen

            self.ps_fixed = 3
            drain(prologue(0))
            epi = None
            for qt in range(NT):
                par = qt % 2
                QT, selT4 = QTb[par], selT4b[par]
                pro = prologue(qt + 1) if qt + 1 < NT else None
                if self.flags.get("nopipe") or self.flags.get("noepi"):
                    drain(epi)
                    epi = None
                if self.flags.get("nopipe") or self.flags.get("nopro"):
                    drain(pro)
                    pro = None
                its = [("w", kt) for kt in (qt - 2, qt - 1, qt) if kt >= 0] + [("s", kt) for kt in range(qt + 1)]
                nw = len([1 for x in its if x[0] == "w"])
                pls = {}

                def logits(i):
                    kind, kt = its[i]
                    ks = slice(kt * 128, kt * 128 + 128)
                    pl = self.psb[i % 2]
                    if kind == "w":
                        k.mm(pl[:, :], KwT[:, ks], QT[:, :])
                    else:
                        k.mm(pl[:, :], KsT[:, ks], QT[:, :], start=True, stop=False)
                        k.mm(pl[:, :], Ex[:, ks], selT4[:, :], start=False, stop=True)
                    pls[i] = pl

                logits(0)
                for i, (kind, kt) in enumerate(its):
                    if i + 1 < len(its):
                        logits(i + 1)
                    pl = pls.pop(i)
                    dlt = qt - kt
                    PT = PTb[i % 2]
                    if dlt <= 1 or (kind == "w" and dlt == 2):
                        k.tt(pl[:, :], pl[:, :], Dt[:, dlt, :], ALU.add)
                        k.act(PT[:, :], pl[:, :], AF.Exp)
                    else:
                        k.act(PT[:, :], pl[:, :], AF.Exp)
                    for hh in range(4):
                        if kind == "w":
                            k.mm(Awb[par][:, hh * 65:(hh + 1) * 65], PT[:, hh * 128:(hh + 1) * 128], Vw[:, kt, :],
                                 start=(i == 0 and hh == 0), stop=(i == nw - 1), sgc=True)
                        else:
                            k.mm(Asb[par][:, hh * 65:(hh + 1) * 65], PT[:, hh * 128:(hh + 1) * 128], Vs[:, kt, :],
                                 start=(kt == 0 and hh == 0), stop=(kt == qt), sgc=True)
                    epi = step(epi)
                    pro = step(pro)
                drain(epi)
                drain(pro)
                epi = epilogue(qt)
            drain(epi)
            self.ps_fixed = None
            k.scope_end()
            aes.close()
            self.pes, self.wbufs = save
        self.psn = 6


    def rwkv(self, l, q, oT):
        k = self.k
        S = self.S
        BW = 256
        NCH = BW // 64
        T = lambda nm, dt=F32, w=BW: k.sb(nm, [128, w], dt, es=self.pes)
        mu = self.P(l, "rw_mu")
        LA = k.sb("rLA", [128, S], BF16, es=self.pes)
        LG = k.sb("rLG", [128, S], BF16, es=self.pes)
        msu, msl, idb = self.cst2[:, 0:256], self.cst2[:, 256:512], self.cst2[:, 512:768]
        cmk = self.cst[:, 384:384 + BW]
        blkf = self.cst[:, 256:384]
        negw = k.sb("rnegw", [128, 8], F32, es=self.pes)
        k.ts(negw[:, 0:4], self.P(l, "rw_w0"), -1.0, ALU.mult)
        k.ts(negw[:, 4:8], self.P(l, "rw_a0"), -1.0, ALU.mult)

        def mkshift(raw, tmp, BW=BW):
            def shift(ps_, out, mucol, carry, first):
                if first:
                    k.memset(raw[:, 0:1], 0.0)
                else:
                    k.cp(raw[:, 0:1], carry[:, 0:1])
                k.cp(raw[:, 1:BW + 1], ps_, eng="act")
                k.tt(tmp[:, :], raw[:, 0:BW], raw[:, 1:BW + 1], ALU.subtract)
                k.stt(out, tmp[:, :], mucol, raw[:, 1:BW + 1], ALU.mult, ALU.add)
                k.cp(carry[:, 0:1], raw[:, BW:BW + 1])
            return shift

        aes = ExitStack()
        k.scope_begin()
        save = self.pes, self.wbufs
        self.pes, self.wbufs = aes, {}
        wl_ = self.wload(l, "rwl", 0, "wrwl", 1)
        carA = k.sb("rcarA", [128, 2], F32, es=aes)
        BA = 512
        xa = k.sb("rxa", [128, BA], F32, es=aes)
        xg = k.sb("rxg", [128, BA], F32, es=aes)
        rawA = k.sb("rrawA", [128, BA + 1], F32, es=aes)
        rawG = k.sb("rrawG", [128, BA + 1], F32, es=aes)
        tmpG = k.sb("rtmpG", [128, BA], F32, es=aes)
        tmpA = k.sb("rtmpA", [128, BA], F32, es=aes)
        shiftA = mkshift(rawA, tmpA, BA)
        shiftG = mkshift(rawG, tmpG, BA)
        for t0 in range(0, S, BA):
            ts_ = slice(t0, t0 + BA)
            pa, pg = self.ps(), self.ps()
            for kc in range(KC):
                k.mm(pa[:, 0:BA], wl_[:, kc, 0:128], self.uT[:, kc, ts_], start=(kc == 0), stop=(kc == KC - 1))
            for kc in range(KC):
                k.mm(pg[:, 0:BA], wl_[:, kc, 128:256], self.uT[:, kc, ts_], start=(kc == 0), stop=(kc == KC - 1))
            shiftA(pa[:, 0:BA], xa[:, :], mu[:, 12:13], carA[:, 0:1], t0 == 0)
            k.act(LA[0:64, ts_], xa[0:64, :], AF.Tanh)
            k.cp(LA[64:128, ts_], xa[64:128, :])
            shiftG(pg[:, 0:BA], xg[:, :], mu[:, 13:14], carA[:, 1:2], t0 == 0)
            k.act(xg[:, :], xg[:, :], AF.Exp, scale=-1.0)
            k.act(xg[:, :], xg[:, :], AF.Ln, bias=self.cst[:, 128:129])
            k.act(LG[:, ts_], xg[:, :], AF.Exp, scale=-1.0)
        k.scope_end()
        aes.close()
        self.pes, self.wbufs = save

        def stream(hp, sid):
            sfx = "_%d" % sid
            T2 = lambda nm, dt=F32, w=BW: k.sb(nm + sfx, [128, w], dt, es=self.pes)
            raw, tmp, ee = T2("rraw", F32, BW + 1), T2("rtmp"), T2("ree")
            car = T2("rcar", F32, 4)
            shift = mkshift(raw, tmp)
            r_, k_, v_, a_, kk, ka, cw, lw, rk, gg = (T2("rr"), T2("rk"), T2("rv"), T2("ra"), T2("rkk"), T2("rka"),
                                                       T2("rcw"), T2("rlw"), T2("rrk"), T2("rgg"))
            Tt, X, XT, U0, WT = T2("rTt"), T2("rX"), T2("rXT"), T2("rU0"), T2("rWT")
            vb, Bb, Kb, Ab = T2("rvb", BF16), T2("rBb", BF16), T2("rKb", BF16), T2("rAb", BF16)
            Vtok, Btok, Ktok, Atok = T2("rVtok", BF16), T2("rBtok", BF16), T2("rKtok", BF16), T2("rAtok", BF16)
            AakT, ArbT, ArkT, Ttb = T2("rAak", BF16), T2("rArb", BF16), T2("rArk", BF16), T2("rTtb", BF16)
            Ub = T2("rUb", BF16, 64)
            H = T2("rH", F32, 64)
            PC = T2("rPC", F32, NCH)
            pst = [3 * sid, 0]
            psY = self.psb[6 + sid]

            def PS():
                self.ps_stream = pst
                p = self.ps()
                self.ps_stream = None
                return p

            def PSBF():
                p = PS()
                return V(p, p.h[:, :].bitcast(BF16))

            w = self.wload(l, "rw", hp, "wrw" + sfx, 1)
            wAB = self.wload(l, "rwAB", hp, "wrAB" + sfx, 1)
            wgB = self.wload(l, "rwgB", hp, "wrgB" + sfx, 1)
            k.memset(H[:, :], 0.0)
            P4 = lambda nm: self.P(l, nm)[:, hp:hp + 1]
            units = [(slice(h * 64, h * 64 + 64), slice(c * 64, c * 64 + 64)) for c in range(NCH) for h in range(2)]
            for t0 in range(0, S, BW):
                ts_ = slice(t0, t0 + BW)
                for i, (dst, cc) in enumerate(((r_, 0), (k_, 1), (v_, 2))):
                    pp = PS()
                    for kc in range(KC):
                        k.mm(pp[:, 0:BW], w[:, kc, i * 128:(i + 1) * 128], self.uT[:, kc, ts_], start=(kc == 0), stop=(kc == KC - 1))
                    yield
                    shift(pp[:, 0:BW], dst[:, :], mu[:, i * 4 + hp:i * 4 + hp + 1], car[:, cc:cc + 1], t0 == 0)
                    yield
                pw, pa, pg = PS(), PS(), PS()
                k.mm(pw[:, 0:BW], wAB[0:64, 0, :], LA[0:64, ts_])
                k.mm(pa[:, 0:BW], wAB[64:128, 0, :], LA[64:128, ts_])
                k.mm(pg[:, 0:BW], wgB[:, 0, :], LG[:, ts_])
                yield
                k.act(lw[:, :], pw[:, 0:BW], AF.Exp, bias=negw[:, hp:hp + 1], scale=-1.0)
                k.act(a_[:, :], pa[:, 0:BW], AF.Exp, bias=negw[:, 4 + hp:5 + hp], scale=-1.0)
                k.cp(gg[:, :], pg[:, 0:BW])
                k.act(lw[:, :], lw[:, :], AF.Ln, bias=self.cst[:, 128:129])
                k.act(a_[:, :], a_[:, :], AF.Ln, bias=self.cst[:, 128:129])
                k.act(lw[:, :], lw[:, :], AF.Exp, scale=-1.0)
                k.act(a_[:, :], a_[:, :], AF.Exp, scale=-1.0)
                k.ts(lw[:, :], lw[:, :], -0.6065306597126334, ALU.mult)
                yield
                k.ts(kk[:, :], k_[:, :], P4("rw_kk"), ALU.mult)
                k.tt(tmp[:, :], kk[:, :], kk[:, :], ALU.mult)
                pn = PS()
                k.mm(pn[:, 0:BW], blkf, tmp[:, :])
                yield
                k.ts(tmp[:, :], pn[:, 0:BW], 1e-24, ALU.max)
                k.act(tmp[:, :], tmp[:, :], AF.Ln)
                k.act(tmp[:, :], tmp[:, :], AF.Exp, scale=-0.5)
                k.tt(kk[:, :], kk[:, :], tmp[:, :], ALU.mult)
                k.ts(tmp[:, :], a_[:, :], -1.0, ALU.add, P4("rw_ka"), ALU.mult)
                k.stt(k_[:, :], tmp[:, :], 1.0, k_[:, :], ALU.add, ALU.mult)
                k.stt(rk[:, :], r_[:, :], P4("rw_rk"), k_[:, :], ALU.mult, ALU.mult)
                k.tt(ka[:, :], kk[:, :], a_[:, :], ALU.mult)
                k.cp(vb[:, :], v_[:, :])
                yield
                k.scan(cw[:, :], self.cmask[:, ts_], lw[:, :], 0.0, ALU.mult, ALU.add)
                cw3 = cw[:, :].rr("p (c t) -> p c t", t=64)
                cwend = V(cw, cw3.ap[:, :, 63:64].broadcast_to([128, NCH, 64]))
                k.act(PC[:, :], cw3[:, :, 63], AF.Exp)
                k.tt(lw[:, :], cw[:, :], lw[:, :], ALU.subtract)
                k.act(ee[:, :], lw[:, :], AF.Exp)
                k.stt(kk[:, :], kk[:, :], -1.0, ee[:, :], ALU.mult, ALU.mult)
                k.cp(Ab[:, :], kk[:, :])
                yield
                k.tt(tmp[:, :].rr("p (c t) -> p c t", t=64), cwend, cw3, ALU.subtract)
                k.act(ee[:, :], tmp[:, :], AF.Exp)
                k.tt(Bb[:, :], ka[:, :], ee[:, :], ALU.mult)
                k.tt(Kb[:, :], k_[:, :], ee[:, :], ALU.mult)
                yield
                k.act(ee[:, :], cw[:, :], AF.Exp, scale=-1.0)
                k.tt(ka[:, :], ka[:, :], ee[:, :], ALU.mult)
                k.tt(k_[:, :], k_[:, :], ee[:, :], ALU.mult)
                k.act(ee[:, :], cw[:, :], AF.Exp)
                k.tt(r_[:, :], r_[:, :], ee[:, :], ALU.mult)
                yield
                for src, dstt in ((vb, Vtok), (Bb, Btok), (Kb, Ktok), (Ab, Atok)):
                    pt = PSBF()
                    for hs, cs in units:
                        k.tr(pt[hs, cs], src[hs, cs], self.ident_bf[hs, hs])
                    yield
                    k.cp(dstt[:, :], pt[:, 0:BW], eng=("act" if src in (Bb, Ab) else "dve"))
                pN, pNT, pAk = PS(), PS(), PS()
                for hs, cs in units:
                    k.mm(pN[hs, cs], ka[hs, cs], kk[hs, cs])
                    k.mm(pNT[hs, cs], kk[hs, cs], ka[hs, cs])
                    k.mm(pAk[hs, cs], k_[hs, cs], kk[hs, cs])
                yield
                k.tt(X[:, :], pN[:, 0:BW], msu, ALU.mult)
                k.tt(XT[:, :], pNT[:, 0:BW], msl, ALU.mult)
                k.tt(AakT[:, :], pAk[:, 0:BW], msu, ALU.mult)
                k.tt(Tt[:, :], X[:, :], idb, ALU.add)
                yield
                pRb, pRk = PS(), PS()
                for hs, cs in units:
                    k.mm(pRb[hs, cs], ka[hs, cs], r_[hs, cs])
                    k.mm(pRk[hs, cs], k_[hs, cs], r_[hs, cs])
                yield
                k.tt(ArbT[:, :], pRb[:, 0:BW], cmk, ALU.mult)
                k.tt(ArkT[:, :], pRk[:, 0:BW], cmk, ALU.mult)
                for j in range(1, 6):
                    pX, pXT = PS(), PS()
                    for hs, cs in units:
                        if j < 5:
                            k.mm(pX[hs, cs], XT[hs, cs], X[hs, cs])
                        k.mm(pXT[hs, cs], X[hs, cs], XT[hs, cs])
                    yield
                    if j < 5:
                        k.cp(X[:, :], pX[:, 0:BW])
                    k.cp(XT[:, :], pXT[:, 0:BW], eng="act")
                    pT = PS()
                    for hs, cs in units:
                        k.mm(pT[hs, cs], XT[hs, cs], Tt[hs, cs])
                    yield
                    k.tt(Tt[:, :], Tt[:, :], pT[:, 0:BW], ALU.add)
                k.cp(Ttb[:, :], Tt[:, :], eng="act")
                pX1 = PS()
                for hs, cs in units:
                    k.mm(pX1[hs, cs], AakT[hs, cs], Vtok[hs, cs])
                yield
                k.cp(tmp[:, :], pX1[:, 0:BW])
                pU0, pWT = PS(), PS()
                for hs, cs in units:
                    k.mm(pU0[hs, cs], Tt[hs, cs], tmp[hs, cs])
                    k.mm(pWT[hs, cs], Atok[hs, cs], Ttb[hs, cs])
                yield
                k.cp(U0[:, :], pU0[:, 0:BW])
                k.cp(WT[:, :], pWT[:, 0:BW], eng="act")
                for c in range(NCH):
                    cs = slice(c * 64, c * 64 + 64)
                    psU = PS()
                    for h in range(2):
                        hs = slice(h * 64, h * 64 + 64)
                        k.mm(psU[hs, 0:64], WT[hs, cs], H[hs, :])
                    yield
                    k.tt(Ub[:, :], psU[:, 0:64], U0[:, cs], ALU.add)
                    psH = PS()
                    for h in range(2):
                        hs = slice(h * 64, h * 64 + 64)
                        k.mm(psY[hs, cs], H[hs, :], r_[hs, cs], start=True, stop=False)
                        k.mm(psY[hs, cs], Ub[hs, :], ArbT[hs, cs], start=False, stop=False)
                        k.mm(psY[hs, cs], Vtok[hs, cs], ArkT[hs, cs], start=False, stop=True)
                        k.mm(psH[hs, 0:64], Btok[hs, cs], Ub[hs, :], start=True, stop=False)
                        k.mm(psH[hs, 0:64], Ktok[hs, cs], Vtok[hs, cs], start=False, stop=True)
                    yield
                    k.stt(H[:, :], H[:, :], PC[:, c:c + 1], psH[:, 0:64], ALU.mult, ALU.add)
                y = X
                k.cp(y[:, :], psY[:, 0:BW])
                pm = PS()
                k.mm(pm[:, 0:BW], blkf, y[:, :])
                yield
                k.stt(y[:, :], pm[:, 0:BW], -1.0 / 64, y[:, :], ALU.mult, ALU.add)
                k.tt(tmp[:, :], y[:, :], y[:, :], ALU.mult)
                pv_ = PS()
                k.mm(pv_[:, 0:BW], blkf, tmp[:, :])
                pb_ = PS()
                k.mm(pb_[:, 0:BW], blkf, rk[:, :])
                yield
                k.act(tmp[:, :], pv_[:, 0:BW], AF.Ln, bias=self.cst[:, 131:132], scale=1.0 / 64)
                k.act(tmp[:, :], tmp[:, :], AF.Exp, scale=-0.5)
                k.tt(y[:, :], y[:, :], tmp[:, :], ALU.mult)
                k.ts(y[:, :], y[:, :], P4("rw_ln_w"), ALU.mult, P4("rw_ln_b"), ALU.add)
                k.tt(tmp[:, :], pb_[:, 0:BW], v_[:, :], ALU.mult)
                k.tt(y[:, :], y[:, :], tmp[:, :], ALU.add)
                k.tt(oT[:, hp, ts_], y[:, :], gg[:, :], ALU.mult)
                yield

        for pair in ((0, 1), (2, 3)):
            pes_ = ExitStack()
            k.scope_begin()
            save = self.pes, self.wbufs
            self.pes, self.wbufs = pes_, {}
            gens = [stream(pair[0], 0), stream(pair[1], 1)]
            if self.flags.get("rw1"):
                for g_ in gens:
                    for _ in g_:
                        pass
            else:
                alive = list(gens)
                while alive:
                    for g_ in list(alive):
                        try:
                            next(g_)
                        except StopIteration:
                            alive.remove(g_)
            k.scope_end()
            pes_.close()
            self.pes, self.wbufs = save

    def build(self):
        k = self.k
        S, NL = self.S, self.NL
        self.pes = k.es
        k.dma("sp", self.par[:, :, :], V(self.par_d, self.par_d.h.rearrange("l p n -> p l n")))
        k.dma("sp", self.cst[:, :], V(self.cst_d, self.cst_d.h[:, 0:896]))
        k.dma("pool", self.cmask[:, :], V(self.cst_d, self.cst_d.h[:, 896:896 + S]))
        k.dma("sp", self.cst2[:, :], V(self.cst_d, self.cst_d.h[:, 896 + S:896 + S + 768]))
        k.dma("pool", self.blk_bf[:, :], V(self.cst_d, self.cst_d.h[:, 256:384]))
        k.dma("pool", self.ident_bf[:, :], V(self.cst_d, self.cst_d.h[:, 0:128]))
        k.memset(self.ones_bf[:, :], 1.0)
        self.prep_lb()
        for l in range(NL):
            self.cast_weights(l)
        TT = min(1024, S)
        for q in range(self.NSEQ):
            src = self.xT.h[q].rearrange("(c p) t -> p c t", p=128)
            for c in range(KC):
                k.dma("sp", self.hT[:, c, :], V(self.xT, src[:, c, :]))
            for l in range(NL):
                self.phase_begin()
                self.ffn_all(l, "ffn1", "ffn1_norm", TT)
                self.phase_end()
                self.mixers(l, q)
                self.phase_begin()
                self.ffn_all(l, "ffn2", "ffn2_norm", TT)
                self.phase_end()
                for t0 in range(0, S, TT):
                    self.phase_begin()
                    self.ple(l, q, t0, TT)
                    self.phase_end()
            self.phase_begin()
            of = k.sb("of", [128, KC, 512], F32, es=self.pes)
            dst = self.outT.h[q].rearrange("(c p) t -> p c t", p=128)
            for t0 in range(0, S, 512):
                self.rmsnorm(self.P(0, "final_norm"), t0, 512, of, 0)
                k.dma("sp", V(self.outT, dst[:, :, t0:t0 + 512]), of[:, :, :])
            self.phase_end()
        k._wait("sp", self.outT.w)


def make_consts(S):
    c = np.zeros((128, 896 + S + 768), np.float32)
    p_ = np.arange(128)[:, None] % 64
    t_ = np.arange(256)[None, :] % 64
    c[:, 896 + S:896 + S + 256] = (p_ < t_)
    c[:, 896 + S + 256:896 + S + 512] = (t_ < p_)
    c[:, 896 + S + 512:896 + S + 768] = (t_ == p_)
    c[:, 0:128] = np.eye(128, dtype=np.float32)
    c[:, 128] = 1.0
    c[:, 130] = 1e-6
    c[:, 131] = 64e-5
    p = np.arange(128)
    c[:, 256:384] = (p[:, None] // 64 == p[None, :] // 64)
    t = np.arange(512)
    c[:, 384:896] = ((p[:, None] % 64) <= (t[None, :] % 64))
    tt = np.arange(S)
    c[:, 896:896 + S] = (tt % 64 != 0)[None, :]
    return c


def t5_bucket_np(n):
    n = np.maximum(n, 0)
    nf = np.maximum(n, 1).astype(np.float32)
    large = 16 + (np.log(nf / np.float32(16)) / np.float32(np.log(8.0)) * np.float32(16)).astype(np.int32)
    large = np.minimum(large, 31)
    return np.where(n < 16, n, large)


NEG = -30000.0


def nsa_consts(rel_bias, S):
    NT = S // 128
    rb = np.asarray(rel_bias, np.float32)
    kk_ = np.arange(128)[:, None]
    qq = np.arange(128)[None, :]
    dtl = np.zeros((2, 4, 128, 4, 128), np.float32)
    bcb = np.full((2, NT, 128, 4, 128), NEG, np.float32)
    n = np.arange(128)[:, None]
    for g in range(2):
        for hh in range(4):
            hd = g * 4 + hh
            d0 = qq - kk_
            dtl[g, 0, :, hh, :] = np.where(d0 >= 0, rb[t5_bucket_np(d0), hd], NEG)
            d1 = 128 + qq - kk_
            dtl[g, 1, :, hh, :] = rb[t5_bucket_np(d1), hd]
            d2 = 256 + qq - kk_
            dtl[g, 2, :, hh, :] = np.where(d2 < 256, rb[t5_bucket_np(d2), hd], NEG)
            dtl[g, 3, :, hh, :] = rb[31, hd]
            for qt in range(NT):
                dc = (128 * qt + qq) - (16 * n + 31)
                bcb[g, qt, :, hh, :] = np.where(dc >= 0, rb[t5_bucket_np(dc), hd], NEG)
    selc = np.zeros((NT, 128, 64), np.float32)
    m = np.arange(32)[None, :]
    for qt in range(NT):
        cur = ((128 * qt + np.arange(128)) // 64)[:, None]
        allowed = m <= cur
        forced = (m == 0) | (m == cur) | (m == cur - 1)
        selc[qt, :, 0:32] = allowed
        selc[qt, :, 32:64] = np.where(forced, 1e9, np.where(allowed, 0.0, -1e9))
    exc = (np.arange(S)[None, :] // 64 == np.arange(32)[:, None]).astype(np.float32)
    nn = np.arange(128)[:, None]
    mm_ = np.arange(32)[None, :]
    cov = ((16 * nn <= 64 * mm_ + 63) & (16 * nn + 31 >= 64 * mm_)).astype(np.float32)
    return (dtl.reshape(2, 4, 128, 512), bcb.reshape(2, NT, 128, 512), selc, exc, cov)


_CACHE = {}


def run(inputs, S, NL, B, flags, ncores=8):
    NSEQ = B // ncores
    packs = [pack_layer(inputs, l) for l in range(NL)]
    pars = [pack_params(inputs, l) for l in range(NL)]
    wsize = packs[0][0].size
    npar = pars[0][0].shape[1]
    prog = Prog(S, NL, NSEQ, [p[1] for p in packs], pars[0][1], wsize, npar, flags)
    x = np.asarray(inputs["x"], np.float32)
    p = np.asarray(inputs["p"], np.float32)
    xT = np.ascontiguousarray(x.transpose(0, 2, 1))
    pT = np.ascontiguousarray(p.transpose(0, 1, 3, 2))
    par = np.stack([pp[0] for pp in pars])
    cst = make_consts(S)
    dtl, bcb, selc, exc, cov = nsa_consts(inputs["rel_bias"], S)
    in_maps = []
    for c in range(ncores):
        m = {"xT": xT[c * NSEQ:(c + 1) * NSEQ], "pT": np.ascontiguousarray(pT[:NL, c * NSEQ:(c + 1) * NSEQ]),
             "par": par, "cst": cst, "dtl": dtl, "bcb": bcb, "selc": selc, "exc": exc, "cov": cov}
        for l in range(NL):
            m["wl%d" % l] = packs[l][0].reshape(-1, 1024)
        in_maps.append(m)
    res = run_bass_kernel_spmd(prog.k.nc, in_maps, core_ids=list(range(ncores)))
    outT = np.concatenate([r["outT"] for r in res.results], axis=0)
    return np.ascontiguousarray(outT.transpose(0, 2, 1)), prog


def kernel(**inputs):
    inputs = {k_: np.asarray(v) for k_, v in inputs.items()}
    B, S, _ = inputs["x"].shape
    out, _ = run(inputs, S, 4, B, {"mix0": True, "mix1": True, "mix2": True})
    return out.astype(np.float32)
```
